# Optimizing a Trainium2 kernel written in Bass

```python
import math
import jax
import jax.numpy as jnp
from jax import lax
import numpy as np

D_MODEL = 1024
BATCH = 8
SEQ = 2048
DEPTH = 2

HEAD_DIM = 64
BRANCH_WIDTH = D_MODEL // 4
N_HEADS = BRANCH_WIDTH // HEAD_DIM
N_BRANCHES = 5
DIFF_QK_DIM = HEAD_DIM // 2
DIL_PATTERNS = ((128, 1), (512, 4), (2048, 16))
S5_GROUP = 16
S5_GROUPS = BRANCH_WIDTH // S5_GROUP
S5_STATE = 64
S5_DT_MIN = 1e-3
S5_DT_MAX = 1e-1
CMP_BLOCK = 32
CMP_STRIDE = 16
CMP_HIDDEN = 256
SLC_BLOCK = 64
N_SELECT = 16
WINDOW = 512
MEM_LEN = 256
MEM_HEADS = 4
ROPE_THETA = 10000.0
Q_BLOCK = 128
G_BLOCK = 64
RMS_EPS = 1e-6
NEG_INF = -1e30
FORCE_SCORE = 1e9
IN_SIZES = (BRANCH_WIDTH, BRANCH_WIDTH, BRANCH_WIDTH, BRANCH_WIDTH,
            BRANCH_WIDTH, BRANCH_WIDTH, BRANCH_WIDTH, BRANCH_WIDTH,
            BRANCH_WIDTH, BRANCH_WIDTH,
            BRANCH_WIDTH, HEAD_DIM, HEAD_DIM, HEAD_DIM, HEAD_DIM, HEAD_DIM, HEAD_DIM,
            3 * N_HEADS, BRANCH_WIDTH,
            MEM_HEADS * HEAD_DIM, MEM_HEADS * HEAD_DIM)
IN_COLS = sum(IN_SIZES)

kernel_name = 'hybrid_gated_parallel_mixer'


def rmsnorm(x, g):
    x32 = x.astype(jnp.float32)
    y = x32 * lax.rsqrt(jnp.mean(x32 * x32, axis=-1, keepdims=True) + RMS_EPS)
    return (y * g.astype(jnp.float32)).astype(x.dtype)


def rope(x):
    s_len, half = x.shape[1], x.shape[-1] // 2
    inv_freq = ROPE_THETA ** (-jnp.arange(half, dtype=jnp.float32) / half)
    ang = jnp.arange(s_len, dtype=jnp.float32)[:, None] * inv_freq[None, :]
    shape = (1, s_len) + (1,) * (x.ndim - 3) + (half,)
    cos = jnp.cos(ang).reshape(shape)
    sin = jnp.sin(ang).reshape(shape)
    x32 = x.astype(jnp.float32)
    x1, x2 = x32[..., :half], x32[..., half:]
    return jnp.concatenate([x1 * cos - x2 * sin, x2 * cos + x1 * sin], axis=-1).astype(x.dtype)


def masked_softmax(s, mask):
    s = jnp.where(mask, s, NEG_INF)
    m = jnp.max(s, axis=-1, keepdims=True)
    e = jnp.where(mask, jnp.exp(s - m), 0.0)
    return e / jnp.maximum(jnp.sum(e, axis=-1, keepdims=True), 1e-30)


def to_blocks(a, blk):
    b, s_len = a.shape[0], a.shape[1]
    return jnp.moveaxis(a.reshape((b, s_len // blk, blk) + a.shape[2:]), 1, 0)


def from_blocks(o):
    o = jnp.moveaxis(o, 0, 1)
    return o.reshape((o.shape[0], o.shape[1] * o.shape[2]) + o.shape[3:])


def diff_attention(q, k, v, lam, lam_init, subln_g):
    s_len, dqk = q.shape[1], q.shape[-1]
    kpos = jnp.arange(s_len)
    v32 = v.astype(jnp.float32)

    def block(args):
        qi, bi = args
        qpos = bi * Q_BLOCK + jnp.arange(Q_BLOCK)
        mask = kpos[None, :] <= qpos[:, None]
        s = jnp.einsum('bqhcd,bkhcd->bhcqk', qi, k).astype(jnp.float32) * dqk ** -0.5
        p = masked_softmax(s, mask)
        a = p[:, :, 0] - lam * p[:, :, 1]
        return jnp.einsum('bhqk,bkhd->bqhd', a, v32)

    o = from_blocks(lax.map(block, (to_blocks(q, Q_BLOCK), jnp.arange(s_len // Q_BLOCK))))
    return rmsnorm(o, subln_g) * (1.0 - lam_init)


def dilated_attention(q, k, v):
    s_len, hd = q.shape[1], q.shape[-1]

    def block(args):
        qi, bi = args
        qpos = bi * G_BLOCK + jnp.arange(G_BLOCK)
        outs, lses = [], []
        for window, dil in DIL_PATTERNS:
            offs = np.arange(window // dil + 1) * dil
            kidx = qpos[:, None] - offs[None, :]
            valid = (kidx >= 0)[None, None]
            kidx = jnp.maximum(kidx, 0)
            kg = k[:, kidx]
            vg = v[:, kidx].astype(jnp.float32)
            s = jnp.einsum('bqhd,bqkhd->bhqk', qi, kg).astype(jnp.float32) * hd ** -0.5
            s = jnp.where(valid, s, NEG_INF)
            m = jnp.max(s, axis=-1, keepdims=True)
            e = jnp.where(valid, jnp.exp(s - m), 0.0)
            den = jnp.sum(e, axis=-1, keepdims=True)
            outs.append(jnp.einsum('bhqk,bqkhd->bhqd', e, vg) / den)
            lses.append(m + jnp.log(den))
        w = jax.nn.softmax(jnp.stack(lses, 0), axis=0)
        o = jnp.sum(w * jnp.stack(outs, 0), axis=0)
        return jnp.swapaxes(o, 1, 2)

    return from_blocks(lax.map(block, (to_blocks(q, G_BLOCK), jnp.arange(s_len // G_BLOCK))))


def ssm_combine(e1, e2):
    a1r, a1i, b1r, b1i = e1
    a2r, a2i, b2r, b2i = e2
    return (a2r * a1r - a2i * a1i,
            a2r * a1i + a2i * a1r,
            a2r * b1r - a2i * b1i + b2r,
            a2r * b1i + a2i * b1r + b2i)


def s5_branch(u, lam_re, lam_im, log_dt, b_re, b_im, c_re, c_im, d_skip, w_glu, b_glu):
    bsz, s_len = u.shape[0], u.shape[1]
    f32 = jnp.float32
    u = u.astype(f32).reshape(bsz, s_len, S5_GROUPS, S5_GROUP)
    lr, li = lam_re.astype(f32), lam_im.astype(f32)
    dt = jnp.exp(log_dt.astype(f32))[:, None]
    mag = jnp.exp(lr * dt)
    a_re, a_im = mag * jnp.cos(li * dt), mag * jnp.sin(li * dt)
    den = lr * lr + li * li
    n_re, n_im = a_re - 1.0, a_im
    z_re = (n_re * lr + n_im * li) / den
    z_im = (n_im * lr - n_re * li) / den
    br, bi = b_re.astype(f32), b_im.astype(f32)
    bb_re = z_re[..., None] * br - z_im[..., None] * bi
    bb_im = z_re[..., None] * bi + z_im[..., None] * br
    bu_re = jnp.einsum('gnp,bsgp->bsgn', bb_re, u)
    bu_im = jnp.einsum('gnp,bsgp->bsgn', bb_im, u)
    a_re_t = jnp.broadcast_to(a_re, bu_re.shape)
    a_im_t = jnp.broadcast_to(a_im, bu_im.shape)
    _, _, x_re, x_im = lax.associative_scan(ssm_combine, (a_re_t, a_im_t, bu_re, bu_im), axis=1)
    y = (jnp.einsum('gpn,bsgn->bsgp', c_re.astype(f32), x_re)
         - jnp.einsum('gpn,bsgn->bsgp', c_im.astype(f32), x_im)
         + d_skip.astype(f32) * u)
    y = y.reshape(bsz, s_len, BRANCH_WIDTH)
    t = jax.nn.gelu(y) @ w_glu.astype(f32) + b_glu.astype(f32)
    return t[..., :BRANCH_WIDTH] * jax.nn.sigmoid(t[..., BRANCH_WIDTH:])


def nsa_attention(q, kc, vc, ks, vs, kw, vw, gates, pe, w1, w2):
    bsz, s_len, _, hd = q.shape
    f32 = jnp.float32
    scale = hd ** -0.5
    pos = jnp.arange(s_len)

    n_cmp = (s_len - CMP_BLOCK) // CMP_STRIDE + 1
    cidx = np.arange(n_cmp)[:, None] * CMP_STRIDE + np.arange(CMP_BLOCK)[None, :]

    def compress(t, pe_i, w1_i, w2_i):
        blk = t[:, cidx] + pe_i
        return jax.nn.gelu(blk.reshape(bsz, n_cmp, CMP_BLOCK * hd) @ w1_i) @ w2_i

    k_cmp = compress(kc, pe[0], w1[0], w2[0])
    v_cmp = compress(vc, pe[1], w1[1], w2[1]).astype(f32)
    cmask = cidx[:, -1][None, :] <= pos[:, None]
    p_cmp = masked_softmax(jnp.einsum('bshd,bnd->bhsn', q, k_cmp).astype(f32) * scale, cmask)
    o_cmp = jnp.einsum('bhsn,bnd->bshd', p_cmp, v_cmp)

    n_slc = s_len // SLC_BLOCK
    n_sel = min(N_SELECT, n_slc)
    c0 = np.arange(n_cmp)[:, None] * CMP_STRIDE
    s0 = np.arange(n_slc)[None, :] * SLC_BLOCK
    overlap = np.clip(np.minimum(c0 + CMP_BLOCK, s0 + SLC_BLOCK) - np.maximum(c0, s0), 0, None) / CMP_STRIDE
    importance = jnp.einsum('bhsn,nj->bsj', p_cmp, jnp.asarray(overlap, dtype=f32))
    qblk = (pos // SLC_BLOCK)[:, None]
    blk = jnp.arange(n_slc)[None, :]
    forced = (blk == 0) | (blk == qblk) | (blk == qblk - 1)
    score = jnp.where(blk <= qblk, jnp.where(forced, FORCE_SCORE, importance), NEG_INF)
    top_val, top_idx = lax.top_k(score, n_sel)
    top_ok = top_val > 0.5 * NEG_INF

    q_r = rope(q)
    ks_r = rope(ks)
    kw_r = rope(kw)
    vs32 = vs.astype(f32)
    gather = jax.vmap(lambda tb, ib: tb[ib])

    def sel_block(args):
        qi, ti, oki, bi = args
        qpos = bi * G_BLOCK + jnp.arange(G_BLOCK)
        tok = (ti[..., None] * SLC_BLOCK + jnp.arange(SLC_BLOCK)).reshape(bsz, G_BLOCK, n_sel * SLC_BLOCK)
        valid = jnp.repeat(oki, SLC_BLOCK, axis=-1) & (tok <= qpos[None, :, None])
        kg = gather(ks_r, tok)
        vg = gather(vs32, tok)
        s = jnp.einsum('bqhd,bqtd->bhqt', qi, kg).astype(f32) * scale
        p = masked_softmax(s, valid[:, None])
        return jnp.einsum('bhqt,bqtd->bqhd', p, vg)

    o_slc = from_blocks(lax.map(sel_block, (to_blocks(q_r, G_BLOCK), to_blocks(top_idx, G_BLOCK),
                                            to_blocks(top_ok, G_BLOCK), jnp.arange(s_len // G_BLOCK))))

    kp = jnp.pad(kw_r, ((0, 0), (WINDOW, 0), (0, 0)))
    vp = jnp.pad(vw.astype(f32), ((0, 0), (WINDOW, 0), (0, 0)))

    def win_block(args):
        qi, bi = args
        start = bi * Q_BLOCK
        kb = lax.dynamic_slice_in_dim(kp, start, Q_BLOCK + WINDOW, axis=1)
        vb = lax.dynamic_slice_in_dim(vp, start, Q_BLOCK + WINDOW, axis=1)
        kpos = start - WINDOW + jnp.arange(Q_BLOCK + WINDOW)
        dist = (start + jnp.arange(Q_BLOCK))[:, None] - kpos[None, :]
        mask = (dist >= 0) & (dist < WINDOW) & (kpos[None, :] >= 0)
        s = jnp.einsum('bqhd,bkd->bhqk', qi, kb).astype(f32) * scale
        p = masked_softmax(s, mask)
        return jnp.einsum('bhqk,bkd->bqhd', p, vb)

    o_win = from_blocks(lax.map(win_block, (to_blocks(q_r, Q_BLOCK), jnp.arange(s_len // Q_BLOCK))))

    g = jax.nn.sigmoid(gates.astype(f32))
    return g[..., 0:1] * o_cmp + g[..., 1:2] * o_slc + g[..., 2:3] * o_win


def memory_attention(q, mem, g, w_kv):
    bsz, m_len = mem.shape[0], mem.shape[1]
    width = MEM_HEADS * HEAD_DIM
    kv = rmsnorm(mem, g) @ w_kv
    k = kv[..., :width].reshape(bsz, m_len, MEM_HEADS, HEAD_DIM)
    v = kv[..., width:].reshape(bsz, m_len, MEM_HEADS, HEAD_DIM).astype(jnp.float32)
    s = jnp.einsum('bshd,bmhd->bhsm', q, k).astype(jnp.float32) * HEAD_DIM ** -0.5
    p = jax.nn.softmax(s, axis=-1)
    return jnp.einsum('bhsm,bmhd->bshd', p, v)


def setup_inputs(seed: int = 0) -> dict:
    key = jax.random.key(seed)
    ks = jax.random.split(key, 26)

    def nrm(k, shape, scale):
        return scale * jax.random.normal(k, shape, jnp.float32)

    n_ids = jnp.arange(S5_STATE, dtype=jnp.float32)
    s5_shape = (DEPTH, S5_GROUPS, S5_STATE)
    return {
        'x': nrm(ks[0], (BATCH, SEQ, D_MODEL), 1.0),
        'mem': nrm(ks[1], (BATCH, MEM_LEN, D_MODEL), 1.0),
        'norm_g': 1.0 + nrm(ks[2], (DEPTH, D_MODEL), 0.02),
        'w_in': nrm(ks[3], (DEPTH, D_MODEL, IN_COLS), D_MODEL ** -0.5),
        'diff_lambda': nrm(ks[4], (DEPTH, 4, DIFF_QK_DIM), 0.1),
        'diff_subln_g': 1.0 + nrm(ks[5], (DEPTH, HEAD_DIM), 0.02),
        's5_lambda_re': -0.5 + nrm(ks[6], s5_shape, 0.01),
        's5_lambda_im': math.pi * n_ids + nrm(ks[7], s5_shape, 0.01),
        's5_log_dt': jax.random.uniform(ks[8], (DEPTH, S5_GROUPS), jnp.float32,
                                        math.log(S5_DT_MIN), math.log(S5_DT_MAX)),
        's5_b_re': nrm(ks[9], (DEPTH, S5_GROUPS, S5_STATE, S5_GROUP), (2 * S5_GROUP) ** -0.5),
        's5_b_im': nrm(ks[10], (DEPTH, S5_GROUPS, S5_STATE, S5_GROUP), (2 * S5_GROUP) ** -0.5),
        's5_c_re': nrm(ks[11], (DEPTH, S5_GROUPS, S5_GROUP, S5_STATE), S5_STATE ** -0.5),
        's5_c_im': nrm(ks[12], (DEPTH, S5_GROUPS, S5_GROUP, S5_STATE), S5_STATE ** -0.5),
        's5_d': nrm(ks[13], (DEPTH, S5_GROUPS, S5_GROUP), 1.0),
        'w_glu': nrm(ks[14], (DEPTH, BRANCH_WIDTH, 2 * BRANCH_WIDTH), BRANCH_WIDTH ** -0.5),
        'b_glu': nrm(ks[15], (DEPTH, 2 * BRANCH_WIDTH), 0.01),
        'nsa_pe': nrm(ks[16], (DEPTH, 2, CMP_BLOCK, HEAD_DIM), 0.02),
        'nsa_w1': nrm(ks[17], (DEPTH, 2, CMP_BLOCK * HEAD_DIM, CMP_HIDDEN), (CMP_BLOCK * HEAD_DIM) ** -0.5),
        'nsa_w2': nrm(ks[18], (DEPTH, 2, CMP_HIDDEN, HEAD_DIM), CMP_HIDDEN ** -0.5),
        'mem_norm_g': 1.0 + nrm(ks[19], (DEPTH, D_MODEL), 0.02),
        'w_mem_kv': nrm(ks[20], (DEPTH, D_MODEL, 2 * MEM_HEADS * HEAD_DIM), D_MODEL ** -0.5),
        'w_merge': nrm(ks[21], (DEPTH, D_MODEL, N_BRANCHES * D_MODEL), D_MODEL ** -0.5),
        'b_merge': nrm(ks[22], (DEPTH, N_BRANCHES * D_MODEL), 0.01),
        'w_branch': nrm(ks[23], (DEPTH, N_BRANCHES, BRANCH_WIDTH, D_MODEL), BRANCH_WIDTH ** -0.5),
        'w_out': nrm(ks[24], (DEPTH, D_MODEL, D_MODEL), D_MODEL ** -0.5),
        'final_g': 1.0 + nrm(ks[25], (D_MODEL,), 0.02),
    }


def reference(x, mem, norm_g, w_in, diff_lambda, diff_subln_g, s5_lambda_re, s5_lambda_im,
              s5_log_dt, s5_b_re, s5_b_im, s5_c_re, s5_c_im, s5_d, w_glu, b_glu, nsa_pe,
              nsa_w1, nsa_w2, mem_norm_g, w_mem_kv, w_merge, b_merge, w_branch, w_out, final_g):
    bsz, s_len = x.shape[0], x.shape[1]
    splits = np.cumsum(np.array(IN_SIZES))[:-1].tolist()
    for l in range(DEPTH):
        h = rmsnorm(x, norm_g[l])
        proj = h @ w_in[l]
        (a_q, a_k, a_v, a_z, b_q, b_k, b_v, b_z, c_u, c_z,
         d_q, d_kc, d_vc, d_ks, d_vs, d_kw, d_vw, d_g, d_z, e_q, e_z) = jnp.split(proj, splits, axis=-1)

        qa = rope(a_q.reshape(bsz, s_len, N_HEADS, 2, DIFF_QK_DIM))
        ka = rope(a_k.reshape(bsz, s_len, N_HEADS, 2, DIFF_QK_DIM))
        va = a_v.reshape(bsz, s_len, N_HEADS, HEAD_DIM)
        dl = diff_lambda[l].astype(jnp.float32)
        lam_init = 0.8 - 0.6 * math.exp(-0.3 * l)
        lam = jnp.exp(jnp.sum(dl[0] * dl[1])) - jnp.exp(jnp.sum(dl[2] * dl[3])) + lam_init
        o_a = diff_attention(qa, ka, va, lam, lam_init, diff_subln_g[l]).reshape(bsz, s_len, BRANCH_WIDTH)

        qb = rope(b_q.reshape(bsz, s_len, N_HEADS, HEAD_DIM))
        kb = rope(b_k.reshape(bsz, s_len, N_HEADS, HEAD_DIM))
        vb = b_v.reshape(bsz, s_len, N_HEADS, HEAD_DIM)
        o_b = dilated_attention(qb, kb, vb).reshape(bsz, s_len, BRANCH_WIDTH)

        o_c = s5_branch(c_u, s5_lambda_re[l], s5_lambda_im[l], s5_log_dt[l], s5_b_re[l], s5_b_im[l],
                        s5_c_re[l], s5_c_im[l], s5_d[l], w_glu[l], b_glu[l])

        o_d = nsa_attention(d_q.reshape(bsz, s_len, N_HEADS, HEAD_DIM), d_kc, d_vc, d_ks, d_vs, d_kw, d_vw,
                            d_g.reshape(bsz, s_len, N_HEADS, 3), nsa_pe[l], nsa_w1[l], nsa_w2[l])
        o_d = o_d.reshape(bsz, s_len, BRANCH_WIDTH)

        o_e = memory_attention(e_q.reshape(bsz, s_len, MEM_HEADS, HEAD_DIM), mem, mem_norm_g[l], w_mem_kv[l])
        o_e = o_e.reshape(bsz, s_len, MEM_HEADS * HEAD_DIM)

        branches = jnp.stack([o_a * jax.nn.silu(a_z), o_b * jax.nn.silu(b_z), o_c * jax.nn.silu(c_z),
                              o_d * jax.nn.silu(d_z), o_e * jax.nn.silu(e_z)], axis=2)
        y = jnp.einsum('bsnc,ncd->bsnd', branches, w_branch[l])
        gate = jax.nn.sigmoid(h @ w_merge[l] + b_merge[l]).reshape(bsz, s_len, N_BRANCHES, D_MODEL)
        mixed = jnp.einsum('bsnd,bsnd->bsd', gate, y)
        x = x + (mixed @ w_out[l]).astype(x.dtype)
    return rmsnorm(x, final_g)
```

```python
import math
import numpy as np
import ml_dtypes
from contextlib import ExitStack
import concourse.bass as bass
import concourse.mybir as mybir
from concourse.bass_utils import run_bass_kernel_spmd

F32 = mybir.dt.float32
BF16 = mybir.dt.bfloat16
I32 = mybir.dt.int32
AF = mybir.ActivationFunctionType
ALU = mybir.AluOpType
AX = mybir.AxisListType

S_LEN = 2048
D = 1024
NT = 16
DEPTH = 2
NWIN = 5644
BIG = 32768.0
EPS = 1e-6
TWO_PI = 6.28318
IN_SIZES = (256, 256, 256, 256, 256, 256, 256, 256, 256, 256, 256, 64, 64, 64, 64, 64, 64, 12, 256, 256, 256)
IN_NAMES = ['a_q', 'a_k', 'a_v', 'a_z', 'b_q', 'b_k', 'b_v', 'b_z', 'c_u', 'c_z', 'd_q', 'd_kc', 'd_vc', 'd_ks',
            'd_vs', 'd_kw', 'd_vw', 'd_g', 'd_z', 'e_q', 'e_z']
OFF_A1, OFF_A2, OFF_B1, OFF_B2, OFF_C, OFF_D1, OFF_D2, OFF_D3, OFF_E = 0, 1024, 1536, 2560, 3072, 3584, 4096, 4864, 5132

ENGS = ('pe', 'act', 'dve', 'pool', 'sp')


class Sched:
    def __init__(self):
        self.q = {e: [] for e in ENGS}
        self.cnt = {e: 0 for e in ENGS}
        self.lastw = {}
        self.readers = {}
        self.dmacnt = {}
        self.seen = {e: {} for e in ENGS}
        self.group = {}

    def _need(self, eng, waits, tok):
        if tok is None:
            return
        k, v = tok
        if k == eng and eng == 'pe':
            return
        if self.seen[eng].get(k, 0) >= v:
            return
        if waits.get(k, 0) < v:
            waits[k] = v

    def op(self, eng, fn, R=(), W=(), dma=None):
        waits = {}
        for r in R:
            self._need(eng, waits, self.lastw.get(r))
        for w in W:
            self._need(eng, waits, self.lastw.get(w))
            for t in self.readers.get(w, ()):
                self._need(eng, waits, t)
        for k, v in waits.items():
            self.seen[eng][k] = v
        if dma is None:
            self.cnt[eng] += 1
            tok = (eng, self.cnt[eng])
        else:
            self.dmacnt[dma] = self.dmacnt.get(dma, 0) + 16
            tok = (dma, self.dmacnt[dma])
        for r in R:
            self.readers.setdefault(r, []).append(tok)
        for w in W:
            self.lastw[w] = tok
            self.readers[w] = []
        if dma is not None and (dma.startswith('c') and not dma.startswith('cb') or dma == 'x'):
            g = self.group.setdefault(dma, [])
            g.extend(W)
            for w in g:
                if self.lastw.get(w, (None,))[0] == dma:
                    self.lastw[w] = tok
        self.q[eng].append((waits, fn, tok))

    def barrier(self):
        self.group = {}
        for e in ENGS:
            waits = {}
            for e2 in ENGS:
                if e2 != e and e2 != 'sp' and self.cnt[e2] > 0:
                    self._need(e, waits, (e2, self.cnt[e2]))
            for k, v in self.dmacnt.items():
                self._need(e, waits, (k, v))
            for k, v in waits.items():
                self.seen[e][k] = v
            if waits:
                self.q[e].append((waits, None, None))

    def final_wait(self, eng='sp'):
        waits = {}
        for k, v in self.dmacnt.items():
            self._need(eng, waits, (k, v))
        for e2 in ENGS:
            if e2 != eng and e2 != 'sp' and self.cnt[e2] > 0:
                self._need(eng, waits, (e2, self.cnt[e2]))
        self.q[eng].append((waits, None, None))


class Rot:
    def __init__(self, items):
        self.items = list(items)
        self.i = 0

    def next(self):
        v = self.items[self.i % len(self.items)]
        self.i += 1
        return v


def build_program(debug=False, n_layers=DEPTH):
    nc = bass.Bass("TRN2", target_bir_lowering=False)
    S = Sched()

    def din(name, shape, dt=F32):
        return nc.dram_tensor(name, list(shape), dt, kind="ExternalInput").ap()

    x_d = din("x", [S_LEN, D])
    mem_d = din("mem", [256, D])
    win_d = din("win", [DEPTH, D, NWIN])
    wmerge_d = din("wmerge", [DEPTH, D, 5 * D])
    bmerge_d = din("bmerge", [DEPTH, 128, 40])
    wbranch_d = din("wbranch", [DEPTH, 5, 256, D])
    wout_d = din("wout", [DEPTH, D, D])
    normg_d = din("normg", [DEPTH, D])
    finalg_d = din("finalg", [1, D])
    memg_d = din("memg", [DEPTH, D])
    wmemkv_d = din("wmemkv", [DEPTH, D, 512])
    dlam_d = din("dlam", [DEPTH, 128])
    sublng_d = din("sublng", [DEPTH, 64])
    s5l1_d = din("s5l1", [DEPTH, 128, 24])
    s5bT_d = din("s5bT", [DEPTH, 128, 256])
    s5cT_d = din("s5cT", [DEPTH, 128, 256])
    s5dflat_d = din("s5dflat", [DEPTH, 256])
    s5k_d = din("s5k", [128, 161])
    s5mm_d = din("s5mm", [128, 512], BF16)
    wglu_d = din("wglu", [DEPTH, 256, 512])
    bglu_d = din("bglu", [DEPTH, 128, 4])
    nsape_d = din("nsape", [DEPTH, 128, 32])
    nsaw1_d = din("nsaw1", [DEPTH, 2, 64, 32 * 256])
    nsaw2_d = din("nsaw2", [DEPTH, 128, 384])
    cbf_d = din("cbf", [128, 384 + 2048 + 33 + 256], BF16)
    wstrip_d = din("wstrip", [128, 2048], BF16)
    negcmp_d = din("negcmp", [128, 2048], BF16)
    rope_d = din("rope", [4, 128, S_LEN])
    identf_d = din("identf", [128, 128])
    selc_d = din("selc", [128, 768])
    out_d = nc.dram_tensor("out", [S_LEN, D], F32, kind="ExternalOutput").ap()
    brt_d = nc.dram_tensor("brt", [10, 128, S_LEN], BF16,
                           kind=("ExternalOutput" if debug else "Internal")).ap()

    es = ExitStack()

    def sb(name, shape, dt):
        return es.enter_context(nc.sbuf_tensor(name, list(shape), dt))

    xs = sb("xs", [128, NT, D], F32)
    hT = sb("hT", [128, 8, S_LEN], BF16)
    wbuf = [sb("wbuf0", [128, 8192], BF16), sb("wbuf1", [128, 8192], BF16)]
    cbf = sb("cbf_s", [128, 384 + 2048 + 33 + 256], BF16)
    small = sb("small", [128, 256], F32)
    identf = sb("identf_s", [128, 128], F32)
    ARENA_W = 18680
    arena = sb("arena", [128, ARENA_W], F32)
    PS = [es.enter_context(nc.psum_tensor(f"ps{i}", [128, 512], F32)) for i in range(8)]

    ident = cbf[:, 0:128]
    negtri = cbf[:, 128:256]
    negtri2 = cbf[:, 256:384]
    expand = cbf[:, 384:384 + 2048]
    ovl = cbf[:, 384 + 2048:384 + 2048 + 33]
    perm32 = cbf[:, 2465:2465 + 128]
    perm64 = cbf[:, 2465 + 128:2465 + 256]

    ss = small[:, 0:16]
    rs = small[:, 16:32]
    lamt = small[:, 32:40]

    class Arena:
        def __init__(self):
            self.off = 0

        def reset(self):
            S.barrier()
            self.off = 0

        def mark(self):
            return self.off

        def release_to(self, off):
            S.barrier()
            self.off = off

        def get(self, name, free, dt):
            words = (free * (2 if dt == BF16 else 4) + 3) // 4
            assert self.off + words <= ARENA_W, (name, self.off, words)
            v = arena[:, self.off:self.off + words]
            self.off += words
            if dt != F32:
                v = v.bitcast(dt)
                if v.shape[1] != free:
                    v = v[:, 0:free]
            return v

    AR = Arena()

    def MM(out, lhsT, rhs, start, stop, R, W, sgc=False):
        if sgc:
            S.op('pe', lambda e: e.matmul(out, lhsT=lhsT, rhs=rhs, start=start, stop=stop, skip_group_check=True), R, W)
        else:
            S.op('pe', lambda e: e.matmul(out, lhsT=lhsT, rhs=rhs, start=start, stop=stop), R, W)

    def TR(out, in_, R, W):
        n = in_.shape[0]
        S.op('pe', lambda e: e.transpose(out, in_, ident[0:n, 0:n]), R, W)

    def TRF(out, in_, R, W):
        n = in_.shape[0]
        S.op('pe', lambda e: e.transpose(out, in_, identf[0:n, 0:n]), R, W)

    def ACT(out, in_, func, R, W, bias=0.0, scale=1.0, accum_out=None):
        if accum_out is None:
            S.op('act', lambda e: e.activation(out, in_, func, bias=bias, scale=scale), R, W)
        else:
            S.op('act', lambda e: e.activation(out, in_, func, bias=bias, scale=scale, accum_out=accum_out), R, W)

    def TT(eng, out, in0, in1, op, R, W):
        S.op(eng, lambda e: e.tensor_tensor(out, in0, in1, op), R, W)

    def TS(eng, out, in0, s1, s2, op0, op1, R, W):
        if op1 is None:
            S.op(eng, lambda e: e.tensor_scalar(out, in0, s1, None, op0), R, W)
        else:
            S.op(eng, lambda e: e.tensor_scalar(out, in0, s1, s2, op0, op1), R, W)

    def STT(out, in0, scalar, in1, op0, op1, R, W):
        S.op('dve', lambda e: e.scalar_tensor_tensor(out, in0, scalar, in1, op0, op1), R, W)

    def CP(eng, out, in_, R, W):
        if eng == 'act':
            S.op('act', lambda e: e.copy(out, in_), R, W)
        else:
            S.op(eng, lambda e: e.tensor_copy(out, in_), R, W)

    def RECIP(out, in_, R, W):
        S.op('dve', lambda e: e.reciprocal(out, in_), R, W)

    def MEMSET(eng, out, val, W):
        S.op(eng, lambda e: e.memset(out, val), (), W)

    def DMA(eng, out, in_, R, W, key):
        S.op(eng, lambda e: e.dma_start(out=out, in_=in_), R, W, dma=key)

    psA = Rot([0, 1])
    psB = Rot([2, 3])
    psC = Rot([4, 5])
    psD = Rot([6, 7])
    psS = Rot([0, 1, 4])
    psT = Rot([5])

    def P(i):
        return ('ps', i)

    wrot = Rot([0, 1])

    def load_w(ncols, src3):
        s = wrot.next()
        dst = wbuf[s][:, 0:8 * ncols].rearrange("p (a b) -> p a b", a=8)
        DMA('pool', dst, src3, (), [('w', s)], f'w{s}')
        return s, dst

    pending_w = {}

    def prefetch_w(tag, ncols, src3):
        if tag not in pending_w:
            pending_w[tag] = load_w(ncols, src3)

    def get_w(tag, ncols, src3):
        if tag in pending_w:
            return pending_w.pop(tag)
        return load_w(ncols, src3)

    def w_rows(dram2d, c0, ncols):
        return dram2d[:, c0:c0 + ncols].rearrange("(c p) n -> p c n", p=128)

    DMA('sp', cbf[:, :], cbf_d[:, :], (), ['cbf'], 'c0')
    DMA('sp', identf[:, :], identf_d[:, :], (), ['identf'], 'c0')
    for t4 in range(4):
        DMA('sp', xs[:, 4 * t4:4 * t4 + 4, :], x_d[512 * t4:512 * (t4 + 1), :].rearrange("(t p) d -> p t d", p=128),
            (), [('x', 4 * t4 + i) for i in range(4)], 'x')

    def rmsnorm_rows(src_tile_ap, ntiles, xres, gb_ap, gres, dstT, dstT_res, hb_bufs, junk):
        for t in range(ntiles):
            ACT(junk, src_tile_ap[:, t, :], AF.Square, [xres(t)], ['junk', ('ss', t)], accum_out=ss[:, t:t + 1])
        allss = [('ss', t) for t in range(ntiles)]
        TS('dve', rs[:, 0:ntiles], ss[:, 0:ntiles], 1.0 / D, EPS, ALU.mult, ALU.add, allss, ['rs'])
        ACT(rs[:, 0:ntiles], rs[:, 0:ntiles], AF.Sqrt, ['rs'], ['rs'])
        RECIP(rs[:, 0:ntiles], rs[:, 0:ntiles], ['rs'], ['rs'])
        for t in range(ntiles):
            hb = hb_bufs[t % 2]
            STT(hb, src_tile_ap[:, t, :], rs[:, t:t + 1], gb_ap, ALU.mult, ALU.mult, [xres(t), 'rs', gres], [('hb', t % 2)])
            b = psC.next()
            pst = PS[b][:, :].bitcast(BF16)
            for c in range(8):
                TR(pst[:, c * 128:(c + 1) * 128], hb[:, c * 128:(c + 1) * 128], [('hb', t % 2), 'cbf'], [P(b)])
            CP('act', dstT[:, :, t * 128:(t + 1) * 128], pst.rearrange("p (c k) -> p c k", c=8), [P(b)], [dstT_res])

    def proj_fm(ws, wres, c0, m, tc, src=None, srcres='hT', ntok=512, bank=None):
        src = hT if src is None else src
        b = psA.next() if bank is None else bank
        for c in range(8):
            MM(PS[b][0:m, 0:ntok], ws[:, c, c0:c0 + m], src[:, c, tc * ntok:(tc + 1) * ntok], c == 0, c == 7,
               [wres, srcres], [P(b)])
        return b

    def proj_tm(ws, wres, c0, n, t, src=None, srcres='hT'):
        src = hT if src is None else src
        b = psA.next()
        for c in range(8):
            MM(PS[b][:, 0:n], src[:, c, t * 128:(t + 1) * 128], ws[:, c, c0:c0 + n], c == 0, c == 7, [wres, srcres], [P(b)])
        return b

    QA_ROT = Rot([0, 1])

    def rope_proj(ws, wres, c_orig, c_sw, dst, dstres, ropeC, ropeS, t1, t2, m=128, perm=None, qa=None):
        for tc in range(4):
            b1 = proj_fm(ws, wres, c_orig, m, tc, bank=psA.next())
            b2 = psB.next()
            ai = QA_ROT.next() % len(qa)
            CP('act', qa[ai][0:m, :], PS[b1][0:m, :], [P(b1)], [('qa', ai)])
            MM(PS[b2][0:m, :], perm[0:m, 0:m], qa[ai][0:m, :], True, True, ['cbf', ('qa', ai)], [P(b2)])
            sl = slice(tc * 512, (tc + 1) * 512)
            TT('dve', t1[0:m, :], PS[b1][0:m, :], ropeC[0:m, sl], ALU.mult, [P(b1), 'ropeC', ('qa', ai)], ['t1'])
            TT('dve', t2[0:m, :], PS[b2][0:m, :], ropeS[0:m, sl], ALU.mult, [P(b2), 'ropeS'], ['t2'])
            if isinstance(dst, tuple):
                TT('pool', dst[0][0:64, sl], t1[0:64, :], t2[0:64, :], ALU.add, ['t1', 't2'], [dstres])
                TT('pool', dst[1][64:128, sl], t1[64:128, :], t2[64:128, :], ALU.add, ['t1', 't2'], [dstres])
            else:
                TT('pool', dst[0:m, sl], t1[0:m, :], t2[0:m, :], ALU.add, ['t1', 't2'], [dstres])

    E_ROT = Rot([0, 1, 2])
    OT_ROT = Rot([0, 1])
    RD_ROT = Rot([0, 1, 2, 3])

    def attention(nvh, qf, kf, vf, scale, pairs, outcb, Ebufs, OTs, vw=65, kparts=128, post=None, qcs=range(4), vwm=None,
                  erot=None, otrot=None, batch=1, srot=None, botbank=None, prep=None):
        vwm = vw if vwm is None else vwm
        srot = psS if srot is None else srot
        erot = E_ROT if erot is None else erot
        otrot = OT_ROT if otrot is None else otrot
        items = []
        for qc in qcs:
            for vh in range(nvh):
                plist = pairs(qc)
                grp = {'bot': None}
                for pi, pr in enumerate(plist):
                    items.append((qc, vh, pi, len(plist), pr, grp))
        st = {}

        def doS(i):
            qc, vh, pi, npl, (kt, c0, c1, adds, mul), grp = items[i]
            bs = srot.next()
            st[i] = bs
            if pi == 0 and prep is not None:
                prep(vh, qc)
            qr_ = qf(vh) if prep is None else qf(vh, qc)
            q_ap, qres = qr_[0], qr_[1]
            k_ap, kres = kf(vh)
            ncol = c1 - c0
            q0 = qc * 512 + c0 - (qr_[2] if len(qr_) > 2 else 0)
            MM(PS[bs][0:kparts, 0:ncol], k_ap[:, kt * 128:kt * 128 + kparts], q_ap[:, q0:q0 + ncol], True, len(adds) == 0,
               [qres, kres], [P(bs)])
            for ai, (lT, rh, co, ncl, ares) in enumerate(adds):
                MM(PS[bs][0:kparts, co:co + ncl], lT, rh, False, ai == len(adds) - 1, ares, [P(bs)])

        def doEV(i):
            qc, vh, pi, npl, (kt, c0, c1, adds, mul), grp = items[i]
            bs = st.pop(i)
            ncol = c1 - c0
            ei = erot.next()
            Eb = Ebufs[ei]
            ACT(Eb[0:kparts, 0:ncol], PS[bs][0:kparts, 0:ncol], AF.Exp, [P(bs)], [('E', ei)], scale=scale)
            if mul is not None:
                m_ap, mres = mul
                TT('dve', Eb[0:kparts, 0:ncol], Eb[0:kparts, 0:ncol], m_ap, ALU.mult, [('E', ei), mres], [('E', ei)])
            if grp['bot'] is None:
                grp['bot'] = psD.next() if botbank is None else botbank
            bot = grp['bot']
            v_ap, vres = vf(vh, kt)
            MM(PS[bot][0:vwm, c0:c1], v_ap, Eb[0:kparts, 0:ncol], pi == 0, pi == npl - 1, [('E', ei), vres], [P(bot)], sgc=True)
            if pi == npl - 1:
                bo = psB.next()
                oi = otrot.next()
                ots = OTs[oi]
                CP('dve', ots[0:vw, :], PS[bot][0:vw, :], [P(bot)], [('ots', oi)])
                for jl in range(4):
                    TRF(PS[bo][:, jl * vw:(jl + 1) * vw], ots[0:vw, jl * 128:(jl + 1) * 128], [('ots', oi), 'identf'], [P(bo)])
                outcb(vh, qc, bo)
                if vh == nvh - 1 and post is not None:
                    post(qc)

        n = len(items)
        LA = 2
        for i in range(min(LA, n)):
            doS(i)
        if batch == 1:
            for i in range(n):
                if i + LA < n:
                    doS(i + LA)
                doEV(i)
        else:
            for i0 in range(0, n, 2):
                for i in (i0 + 2, i0 + 3):
                    if i < n:
                        doS(i)
                for i in (i0, i0 + 1):
                    if i < n:
                        doEV(i)

    rdbuf = [small[:, 64 + 4 * i:68 + 4 * i] for i in range(4)]

    def norm_out(bo, vw, dst, dstres, coef=None, coefres=None, accumulate=False, tmp=None):
        O = PS[bo][:, 0:4 * vw].rearrange("p (j w) -> p j w", j=4)
        ri = RD_ROT.next()
        rd = rdbuf[ri]
        TS('dve', rd, O[:, :, 64], 1e-30, None, ALU.max, None, [P(bo)], [('rd', ri)])
        RECIP(rd, rd, [('rd', ri)], [('rd', ri)])
        if coef is not None:
            TT('dve', rd, rd, coef, ALU.mult, [('rd', ri), coefres], [('rd', ri)])
        rb = rd.unsqueeze(2).to_broadcast([128, 4, 64])
        if not accumulate:
            TT('dve', dst, O[:, :, 0:64], rb, ALU.mult, [P(bo), ('rd', ri)], [dstres])
        else:
            tmp_ap, tmp_res = tmp
            TT('dve', tmp_ap, O[:, :, 0:64], rb, ALU.mult, [P(bo), ('rd', ri)], [tmp_res])
            TT('pool', dst, dst, tmp_ap, ALU.add, [tmp_res, dstres], [dstres])
        return rd, ('rd', ri)

    def causal_pairs(qc, extra=None):
        pl = []
        for kt in range(4 * qc + 4):
            c0 = max(0, kt - 4 * qc) * 128
            adds = []
            if kt >= 4 * qc:
                adds.append((ident, negtri, 0, 128, ['cbf']))
            if extra is not None:
                adds += extra(qc, kt, c0)
            pl.append((kt, c0, 512, adds, None))
        return pl

    def store_chunk(n, qc, brbk, k):
        for hf in range(2):
            DMA('sp', brt_d[2 * n + hf, :, qc * 512:(qc + 1) * 512], brbk[:, hf, :], [('brb', k)], [('brt', 2 * n + hf, qc)], f'brt{k}')

    def finish_branch(n, obf, qc, zsT, brb):
        b = psT.next()
        pst = PS[b][:, :].bitcast(BF16)
        for jl in range(4):
            for hf in range(2):
                i = jl * 2 + hf
                TR(pst[:, i * 128:(i + 1) * 128], obf[:, jl, hf * 128:(hf + 1) * 128], ['obf', 'cbf'], [P(b)])
        sl = slice(qc * 512, (qc + 1) * 512)
        k = qc % len(brb)
        TT('dve', brb[k].rearrange("p h (j k) -> p h j k", j=4),
           pst.rearrange("p (j h k) -> p h j k", j=4, h=2),
           zsT[:, :, sl].rearrange("p h (j k) -> p h j k", j=4), ALU.mult, [P(b), 'zsT'], [('brb', k)])
        store_chunk(n, qc, brb[k], k)

    def zs_proj(ws, wres, c0, zsT):
        for ch in range(2):
            for tc in range(4):
                b = proj_fm(ws, wres, c0 + ch * 128, 128, tc)
                ACT(zsT[:, ch, tc * 512:(tc + 1) * 512], PS[b][:, :], AF.Silu, [P(b)], ['zsT'])

    def v_proj(ws, wres, c0, nh, vaug, vres):
        for t in range(NT):
            b = proj_tm(ws, wres, c0, nh * 64, t)
            CP('act', vaug[:, t, :, 0:64], PS[b][:, 0:nh * 64].rearrange("p (h d) -> p h d", h=nh), [P(b)], [vres])

    def r3(ap, c):
        return ap.rearrange("p (c t) -> p c t", c=c)

    for l in range(n_layers):
        lam_init = 0.8 - 0.6 * math.exp(-0.3 * l)
        AR.reset()
        hb0 = AR.get("hb0", D, BF16)
        hb1 = AR.get("hb1", D, BF16)
        junk = AR.get("junk", D, BF16)
        gbc = AR.get("gbc", D, F32)
        DMA('sp', gbc, normg_d[l:l + 1, :].partition_broadcast(128), (), ['gbc'], 'c1')
        rmsnorm_rows(xs, NT, lambda t: ('x', t), gbc, 'gbc', hT, 'hT', [hb0, hb1], junk)

        AR.reset()
        Xc = AR.get("Xc", 16 * 256, BF16).rearrange("p (s c) -> p s c", s=16)
        Yc = AR.get("Yc", 16 * 256, F32).rearrange("p (s c) -> p s c", s=16)
        brb = [r3(AR.get("brb0", 2 * 512, BF16), 2)]
        zsc = r3(AR.get("zsc", 2 * 512, BF16), 2)
        dbc = AR.get("dbc", 256, F32)
        bgl = AR.get("bgl", 4, F32)
        wgl = AR.get("wgl", 2 * 512, BF16).rearrange("p (c n) -> p c n", c=2)
        mark5 = AR.mark()
        s5k = AR.get("s5k", 17 + 16 + 128, F32)
        kidx, krev, iotac = s5k[:, 0:17], s5k[:, 17:33], s5k[:, 33:161]
        maskM = AR.get("maskM", 512, BF16).rearrange("p (h c) -> p h c", h=2)
        l1 = AR.get("l1", 24, F32)
        bT = AR.get("bT", 256, F32).rearrange("p (j r k) -> p j r k", j=8, r=2)
        cT = AR.get("cT", 256, F32).rearrange("p (j r k) -> p j r k", j=8, r=2)
        p1 = AR.get("p1", 16 * 8, F32).rearrange("p (s n) -> p s n", s=16)
        pti = AR.get("pti", 136, I32)
        PW = AR.get("PW", 12 * 136, F32).rearrange("p (s n) -> p s n", s=12)
        bb = AR.get("bb", 2 * 128, F32).rearrange("p (r j k) -> p r j k", r=2, j=8)
        tmpa = AR.get("tmpa", 256, F32)
        tmpb = AR.get("tmpb", 256, F32)
        mats2 = [AR.get(f"mats{i}", 8 * 256, BF16).rearrange("p (m k) -> p m k", m=8) for i in range(2)]
        Mg2 = [AR.get(f"Mg{i}", 2 * 512, BF16).rearrange("p (g h c) -> p g h c", g=2, h=2) for i in range(2)]
        Rtr2 = [AR.get(f"Rtr{i}", 4 * 128, BF16).rearrange("p (a c) -> p a c", a=4) for i in range(2)]
        Ut2 = [AR.get(f"Ut{i}", 4 * 128, BF16).rearrange("p (a c) -> p a c", a=4) for i in range(2)]
        Xg = AR.get("Xg", 2 * 256, BF16).rearrange("p (g k) -> p g k", g=2)
        scb = [AR.get("sc0", 11 * 128, F32).rearrange("p (s n) -> p s n", s=11)]
        sci2 = [AR.get("sci0", 128, I32)]
        Xp = AR.get("Xp", 2 * 128, BF16).rearrange("p (r c) -> p r c", r=2)
        Ysb = PW.rearrange("p s n -> p (s n)")[:, 6 * 136:6 * 136 + 256]
        tmpc = PW.rearrange("p s n -> p (s n)")[:, 8 * 136:8 * 136 + 256]
        tmpd = PW.rearrange("p s n -> p (s n)")[:, 8 * 136 + 256:8 * 136 + 512]

        DMA('sp', s5k, s5k_d[:, :], (), ['s5k'], 'c1')
        DMA('sp', maskM, s5mm_d[:, :].rearrange("p (h c) -> p h c", h=2), (), ['maskM'], 'c1')
        DMA('sp', dbc, s5dflat_d[l:l + 1, :].partition_broadcast(128), (), ['dbc'], 'c1')
        DMA('sp', l1, s5l1_d[l, :, :], (), ['l1'], 'c1')
        DMA('sp', bT, s5bT_d[l].rearrange("p (j r k) -> p j r k", j=8, r=2), (), ['bT'], 'c1')
        DMA('sp', cT, s5cT_d[l].rearrange("p (j r k) -> p j r k", j=8, r=2), (), ['cT'], 'c1')
        DMA('sp', bgl, bglu_d[l, :, :], (), ['bgl'], 'c1')
        DMA('pool', wgl, wglu_d[l].rearrange("(c p) n -> p c n", p=128), (), ['wgl'], 'c3')
        s, ws = get_w(('C', l), 512, w_rows(win_d[l], OFF_C, 512))
        MEMSET('pool', Xp[:, :, 0:1], 0.0, ['Xp'])
        prefetch_w(('AB1', l, 0), 1024, w_rows(win_d[l], OFF_A1, 1024))

        hT16 = hT.rearrange("p k (c s) -> p k c s", s=16)
        for sg in range(8):
            b = psA.next()
            for s2_ in range(2):
                st_ = 2 * sg + s2_
                for c in range(8):
                    MM(PS[b][:, s2_ * 256:(s2_ + 1) * 256], hT16[:, c, :, st_], ws[:, c, 0:256], c == 0, c == 7, [('w', s), 'hT'], [P(b)])
            CP('act', Xc[:, 2 * sg:2 * sg + 2, :], PS[b][:, :].rearrange("p (s c) -> p s c", s=2), [P(b)], ['Xc'])

        def frac_sincos(turns, n, sin_o, cos_o, ta, tb, ti, res):
            CP('dve', ti, turns, [res], [res])
            TT('dve', ta, turns, ti, ALU.subtract, [res], [res])
            ACT(sin_o, ta, AF.Sin, [res], [res], scale=TWO_PI)
            TS('dve', tb, turns, 0.25, None, ALU.add, None, [res], [res])
            CP('dve', ti, tb, [res], [res])
            TT('dve', ta, tb, ti, ALU.subtract, [res], [res])
            ACT(cos_o, ta, AF.Sin, [res], [res], scale=TWO_PI)

        R1 = 'p1'
        T8 = lambda i: p1[:, i, :]
        lr1, li1, ldt1 = l1[:, 0:8], l1[:, 8:16], l1[:, 16:24]
        ACT(T8(0), ldt1, AF.Exp, ['l1'], [R1])
        TT('dve', T8(1), lr1, T8(0), ALU.mult, ['l1', R1], [R1])
        TT('dve', T8(2), li1, T8(0), ALU.mult, ['l1', R1], [R1])
        TS('dve', T8(3), T8(2), 1.0 / (2 * math.pi), None, ALU.mult, None, [R1], [R1])
        CP('dve', pti[:, 0:8], T8(3), [R1], [R1])
        TT('dve', T8(4), T8(3), pti[:, 0:8], ALU.subtract, [R1], [R1])
        frac_sincos(T8(4), 8, T8(5), T8(6), T8(7), T8(8), pti[:, 0:8], R1)
        ACT(T8(7), T8(1), AF.Exp, [R1], [R1])
        TT('dve', T8(8), T8(7), T8(6), ALU.mult, [R1], [R1])
        TT('dve', T8(9), T8(7), T8(5), ALU.mult, [R1], [R1])
        TS('dve', T8(8), T8(8), -1.0, None, ALU.add, None, [R1], [R1])
        TT('dve', T8(10), lr1, lr1, ALU.mult, ['l1'], [R1])
        TT('dve', T8(11), li1, li1, ALU.mult, ['l1'], [R1])
        TT('dve', T8(10), T8(10), T8(11), ALU.add, [R1], [R1])
        RECIP(T8(10), T8(10), [R1], [R1])
        TT('dve', T8(11), T8(8), lr1, ALU.mult, ['l1', R1], [R1])
        TT('dve', T8(12), T8(9), li1, ALU.mult, ['l1', R1], [R1])
        TT('dve', T8(11), T8(11), T8(12), ALU.add, [R1], [R1])
        TT('dve', T8(11), T8(11), T8(10), ALU.mult, [R1], [R1])
        TT('dve', T8(12), T8(9), lr1, ALU.mult, ['l1', R1], [R1])
        TT('dve', T8(13), T8(8), li1, ALU.mult, ['l1', R1], [R1])
        TT('dve', T8(12), T8(12), T8(13), ALU.subtract, [R1], [R1])
        TT('dve', T8(12), T8(12), T8(10), ALU.mult, [R1], [R1])
        zre_b = T8(11).unsqueeze(2).to_broadcast([128, 8, 16])
        zim_b = T8(12).unsqueeze(2).to_broadcast([128, 8, 16])
        b3 = lambda ap: ap.rearrange("p (j k) -> p j k", j=8)
        TT('dve', b3(tmpa[:, 0:128]), zre_b, bT[:, :, 0, :], ALU.mult, [R1, 'bT'], ['tmpa'])
        TT('dve', b3(tmpb[:, 0:128]), zim_b, bT[:, :, 1, :], ALU.mult, [R1, 'bT'], ['tmpb'])
        TT('dve', bb[:, 0, :, :], b3(tmpa[:, 0:128]), b3(tmpb[:, 0:128]), ALU.subtract, ['tmpa', 'tmpb'], ['bb'])
        TT('dve', b3(tmpa[:, 0:128]), zre_b, bT[:, :, 1, :], ALU.mult, [R1, 'bT'], ['tmpa'])
        TT('dve', b3(tmpb[:, 0:128]), zim_b, bT[:, :, 0, :], ALU.mult, [R1, 'bT'], ['tmpb'])
        TT('dve', bb[:, 1, :, :], b3(tmpa[:, 0:128]), b3(tmpb[:, 0:128]), ALU.add, ['tmpa', 'tmpb'], ['bb'])
        TS('dve', T8(13), T8(4), 16.0, None, ALU.mult, None, [R1], [R1])
        CP('dve', pti[:, 0:8], T8(13), [R1], [R1])
        TT('dve', T8(14), T8(13), pti[:, 0:8], ALU.subtract, [R1], [R1])
        ACT(T8(15), T8(1), AF.Exp, [R1], [R1], scale=16.0)
        RP = 'PW'
        W17 = lambda i: PW[:, i, :].rearrange("p (j k) -> p j k", j=8)
        W16 = lambda i: PW[:, i, 0:128].rearrange("p (j k) -> p j k", j=8)
        kk17 = kidx.unsqueeze(1).to_broadcast([128, 8, 17])
        kk16r = krev.unsqueeze(1).to_broadcast([128, 8, 16])
        lm17 = T8(1).unsqueeze(2).to_broadcast([128, 8, 17])
        lm16 = T8(1).unsqueeze(2).to_broadcast([128, 8, 16])
        ph17 = T8(4).unsqueeze(2).to_broadcast([128, 8, 17])
        ph16 = T8(4).unsqueeze(2).to_broadcast([128, 8, 16])
        TT('dve', W17(4), kk17, lm17, ALU.mult, ['s5k', R1], [RP])
        ACT(PW[:, 5, :], PW[:, 4, :], AF.Exp, [RP], [RP])
        ACT(PW[:, 6, :], PW[:, 4, :], AF.Exp, [RP], [RP], scale=-1.0)
        TT('dve', W17(7), kk17, ph17, ALU.mult, ['s5k', R1], [RP])
        frac_sincos(PW[:, 7, :], 136, PW[:, 8, :], PW[:, 9, :], PW[:, 10, :], PW[:, 11, :], pti, RP)
        TT('dve', PW[:, 0, :], PW[:, 5, :], PW[:, 9, :], ALU.mult, [RP], [RP])
        TT('dve', PW[:, 1, :], PW[:, 5, :], PW[:, 8, :], ALU.mult, [RP], [RP])
        TT('dve', PW[:, 2, :], PW[:, 6, :], PW[:, 9, :], ALU.mult, [RP], [RP])
        STT(PW[:, 3, :], PW[:, 8, :], -1.0, PW[:, 6, :], ALU.mult, ALU.mult, [RP], [RP])
        TT('dve', W16(10), kk16r, lm16, ALU.mult, ['s5k', R1], [RP])
        ACT(PW[:, 11, 0:128], PW[:, 10, 0:128], AF.Exp, [RP], [RP])
        TT('dve', W16(10), kk16r, ph16, ALU.mult, ['s5k', R1], [RP])
        frac_sincos(PW[:, 10, 0:128], 128, PW[:, 6, 0:128], PW[:, 7, 0:128], PW[:, 8, 0:128], PW[:, 9, 0:128], pti[:, 0:128], RP)
        TT('dve', PW[:, 4, 0:128], PW[:, 11, 0:128], PW[:, 7, 0:128], ALU.mult, [RP], [RP])
        TT('dve', PW[:, 5, 0:128], PW[:, 11, 0:128], PW[:, 6, 0:128], ALU.mult, [RP], [RP])

        def outer(eng, dst, tab, vec, W):
            TT(eng, dst.rearrange("p (a b) -> p a b", a=16), tab.unsqueeze(2).to_broadcast([128, 16, 16]),
               vec.unsqueeze(1).to_broadcast([128, 16, 16]), ALU.mult, [RP, 'bb', 'cT'], W)

        def s5_stage1(j):
            pj = j % 2
            mats, Mg, Rtr, Ut = mats2[pj], Mg2[pj], Rtr2[pj], Ut2[pj]
            Pre_j, Pim_j = W17(0)[:, j, :], W17(1)[:, j, :]
            PIre_j, PIim_j = W17(2)[:, j, 0:16], W17(3)[:, j, 0:16]
            PRre_j, PRim_j = W16(4)[:, j, :], W16(5)[:, j, :]
            bre_j, bim_j = bb[:, 0, j, :], bb[:, 1, j, :]
            cre_j, cim_j = cT[:, j, 0, :], cT[:, j, 1, :]
            specs = [(PIre_j, PIim_j, bre_j, bim_j, 0, 1, False),
                     (Pre_j[:, 0:16], Pim_j[:, 0:16], cre_j, cim_j, 2, 3, True),
                     (PRre_j, PRim_j, bre_j, bim_j, 4, 5, False),
                     (Pre_j[:, 1:17], Pim_j[:, 1:17], cre_j, cim_j, 6, 7, True)]
            for (tr_, ti_, vr_, vi_, o_re, o_im, neg) in specs:
                if not neg:
                    outer('dve', tmpa, tr_, vr_, ['tmpa'])
                    outer('dve', tmpb, ti_, vi_, ['tmpb'])
                    TT('dve', mats[:, o_re, :], tmpa, tmpb, ALU.subtract, ['tmpa', 'tmpb'], [('mats', o_re, pj)])
                    outer('pool', tmpc, tr_, vi_, ['tmpc'])
                    outer('pool', tmpd, ti_, vr_, ['tmpd'])
                    TT('pool', mats[:, o_im, :], tmpc, tmpd, ALU.add, ['tmpc', 'tmpd'], [('mats', o_im, pj)])
                else:
                    outer('pool', tmpc, tr_, vr_, ['tmpc'])
                    outer('pool', tmpd, ti_, vi_, ['tmpd'])
                    TT('pool', mats[:, o_re, :], tmpc, tmpd, ALU.subtract, ['tmpc', 'tmpd'], [('mats', o_re, pj)])
                    outer('dve', tmpa, tr_, vi_, ['tmpa'])
                    outer('dve', tmpb, ti_, vr_, ['tmpb'])
                    STT(mats[:, o_im, :], tmpa, -1.0, tmpb, ALU.mult, ALU.subtract, ['tmpa', 'tmpb'], [('mats', o_im, pj)])
            for g2 in range(2):
                rows = slice(g2 * 64, (g2 + 1) * 64)
                for sh in range(2):
                    b = psA.next()
                    MM(PS[b][:, 0:256], mats[rows, 0, sh * 128:(sh + 1) * 128], mats[rows, 2, :], True, False,
                       [('mats', 0, pj), ('mats', 2, pj)], [P(b)])
                    MM(PS[b][:, 0:256], mats[rows, 1, sh * 128:(sh + 1) * 128], mats[rows, 3, :], False, True,
                       [('mats', 1, pj), ('mats', 3, pj)], [P(b)])
                    TT('dve', Mg[:, g2, sh, :], PS[b][:, 0:256], maskM[:, sh, :], ALU.mult, [P(b), 'maskM'], [('Mg', g2, pj)])
            b = psC.next()
            pst = PS[b][:, :].bitcast(BF16)
            for ri in range(2):
                for sh in range(2):
                    a_ = ri * 2 + sh
                    TR(pst[:, a_ * 128:(a_ + 1) * 128], mats[:, 4 + ri, sh * 128:(sh + 1) * 128], [('mats', 4 + ri, pj), 'cbf'], [P(b)])
            CP('act', Rtr, pst[:, 0:512].rearrange("p (a c) -> p a c", a=4), [P(b)], [('Rtr', pj)])
            b = psC.next()
            pst = PS[b][:, :].bitcast(BF16)
            for g2 in range(2):
                ch0 = (2 * j + g2) * 16
                CP('pool', Xg[:, g2, :].rearrange("p (s k) -> p s k", s=16), Xc[:, :, ch0:ch0 + 16], ['Xc'], ['Xg'])
                for sh in range(2):
                    a_ = g2 * 2 + sh
                    TR(pst[:, a_ * 128:(a_ + 1) * 128], Xg[:, g2, sh * 128:(sh + 1) * 128], ['Xg', 'cbf'], [P(b)])
            CP('act', Ut, pst[:, 0:512].rearrange("p (a c) -> p a c", a=4), [P(b)], [('Ut', pj)])

        def s5_stage2(j):
            pj = j % 2
            mats, Mg, Rtr, Ut = mats2[pj], Mg2[pj], Rtr2[pj], Ut2[pj]
            bw = psA.next()
            for ri in range(2):
                for g2 in range(2):
                    for sh in range(2):
                        MM(PS[bw][g2 * 64:(g2 + 1) * 64, ri * 128:(ri + 1) * 128], Rtr[:, ri * 2 + sh, g2 * 64:(g2 + 1) * 64],
                           Ut[:, g2 * 2 + sh, :], sh == 0, sh == 1, [('Rtr', pj), ('Ut', pj)], [P(bw)])
            pj2 = 0
            SCb = scb[pj2]
            SC = lambda i: SCb[:, i, :]
            r = lambda i: ('sc', i, pj2)
            sci_, rsi = sci2[pj2], ('sci', pj2)
            TS('dve', SC(0), iotac, T8(14)[:, j:j + 1], None, ALU.mult, None, ['s5k', R1], [r(0)])
            CP('dve', sci_, SC(0), [r(0)], [rsi])
            TT('dve', SC(3), SC(0), sci_, ALU.subtract, [r(0), rsi], [r(3)])
            ACT(SC(1), SC(3), AF.Sin, [r(3)], [r(1)], scale=TWO_PI)
            TS('dve', SC(4), SC(0), 0.25, None, ALU.add, None, [r(0)], [r(4)])
            CP('dve', sci_, SC(4), [r(4)], [rsi])
            TT('dve', SC(5), SC(4), sci_, ALU.subtract, [r(4), rsi], [r(5)])
            ACT(SC(2), SC(5), AF.Sin, [r(5)], [r(2)], scale=TWO_PI)
            Wre, Wim = PS[bw][:, 0:128], PS[bw][:, 128:256]
            TT('dve', SC(3), Wre, SC(2), ALU.mult, [P(bw), r(2)], [r(3)])
            TT('dve', SC(4), Wim, SC(1), ALU.mult, [P(bw), r(1)], [r(4)])
            TT('dve', SC(5), Wim, SC(2), ALU.mult, [P(bw), r(2)], [r(5)])
            TT('dve', SC(6), Wre, SC(1), ALU.mult, [P(bw), r(1)], [r(6)])
            TT('pool', SC(7), SC(3), SC(4), ALU.add, [r(3), r(4)], [r(7)])
            TT('pool', SC(8), SC(5), SC(6), ALU.subtract, [r(5), r(6)], [r(8)])
            magA = T8(15)[:, j:j + 1].to_broadcast([128, 128])
            S.op('dve', lambda e, magA=magA, o=SC(9), i=SC(7): e.tensor_tensor_scan(o, magA, i, 0.0, ALU.mult, ALU.add), [r(7), R1], [r(9)])
            S.op('dve', lambda e, magA=magA, o=SC(10), i=SC(8): e.tensor_tensor_scan(o, magA, i, 0.0, ALU.mult, ALU.add), [r(8), R1], [r(10)])
            TT('dve', SC(3), SC(9), SC(2), ALU.mult, [r(9), r(2)], [r(3)])
            TT('pool', SC(4), SC(10), SC(1), ALU.mult, [r(10), r(1)], [r(4)])
            TT('dve', Xp[:, 0, 1:128], SC(3)[:, 0:127], SC(4)[:, 0:127], ALU.subtract, [r(3), r(4)], ['Xp'])
            TT('dve', SC(5), SC(9), SC(1), ALU.mult, [r(9), r(1)], [r(5)])
            TT('pool', SC(6), SC(10), SC(2), ALU.mult, [r(10), r(2)], [r(6)])
            TT('dve', Xp[:, 1, 1:128], SC(5)[:, 0:127], SC(6)[:, 0:127], ALU.add, [r(5), r(6)], ['Xp'])
            for g2 in range(2):
                rows = slice(g2 * 64, (g2 + 1) * 64)
                g = 2 * j + g2
                by = psB.next()
                for th in range(2):
                    osl = PS[by][:, th * 128:(th + 1) * 128]
                    csl = slice(th * 128, (th + 1) * 128)
                    MM(osl, Mg[:, g2, 0, csl], Ut[:, g2 * 2 + 0, :], True, False, [('Mg', g2, pj), ('Ut', pj)], [P(by)])
                    if th == 1:
                        MM(osl, Mg[:, g2, 1, csl], Ut[:, g2 * 2 + 1, :], False, False, [('Mg', g2, pj), ('Ut', pj)], [P(by)])
                    MM(osl, mats[rows, 6, csl], Xp[rows, 0, :], False, False, [('mats', 6, pj), 'Xp'], [P(by)])
                    MM(osl, mats[rows, 7, csl], Xp[rows, 1, :], False, True, [('mats', 7, pj), 'Xp'], [P(by)])
                CP('act', Ysb, PS[by][:, 0:256], [P(by)], ['Ysb'])
                bt = psD.next()
                for th in range(2):
                    TRF(PS[bt][:, th * 128:(th + 1) * 128], Ysb[:, th * 128:(th + 1) * 128], ['Ysb', 'identf'], [P(bt)])
                CP('dve', Yc[:, :, g * 16:(g + 1) * 16], PS[bt][:, 0:256].rearrange("p (s k) -> p s k", s=16), [P(bt)], [('Yc', g)])
        s5_stage1(0)
        for j in range(8):
            if j + 1 < 8:
                s5_stage1(j + 1)
            s5_stage2(j)
        AR.release_to(mark5)
        gyT = r3(AR.get("gyT", 2 * S_LEN, BF16), 2)
        gq = [AR.get(f"gq{i}", 512, F32) for i in range(2)]
        allY = [('Yc', g) for g in range(16)]
        for qd in range(8):
            ssl = slice(2 * qd, 2 * qd + 2)
            g0, g1_ = gq
            v3 = lambda ap: ap.rearrange("p (s c) -> p s c", s=2)
            TT('dve', v3(g0), Xc[:, ssl, :], dbc.unsqueeze(1).to_broadcast([128, 2, 256]), ALU.mult, ['Xc', 'dbc'], ['gq0'])
            TT('pool', Yc[:, ssl, :], Yc[:, ssl, :], v3(g0), ALU.add, ['gq0'] + allY, [('Yq', qd)])
            yq = Yc[:, ssl, :]
            TT('dve', v3(g0), yq, yq, ALU.mult, [('Yq', qd)], ['gq0'])
            TS('dve', g0, g0, 0.044715, 1.0, ALU.mult, ALU.add, ['gq0'], ['gq0'])
            TT('pool', v3(g1_), v3(g0), yq, ALU.mult, ['gq0', ('Yq', qd)], ['gq1'])
            ACT(g1_, g1_, AF.Sigmoid, ['gq1'], ['gq1'], scale=1.5957691216)
            TT('dve', Xc[:, ssl, :], v3(g1_), yq, ALU.mult, ['gq1', ('Yq', qd)], ['Xc'])
        gy16 = gyT.rearrange("p h (c s) -> p h s c", s=16)
        for hf in range(2):
            for sg in range(2):
                b = psC.next()
                pst = PS[b][:, :].bitcast(BF16)
                for s8 in range(8):
                    TR(pst[:, s8 * 128:(s8 + 1) * 128], Xc[:, sg * 8 + s8, hf * 128:(hf + 1) * 128], ['Xc', 'cbf'], [P(b)])
                CP('act', gy16[:, hf, sg * 8:(sg + 1) * 8, :], pst.rearrange("p (s c) -> p s c", s=8), [P(b)], ['gyT'])
        for cc in range(4):
            csl = slice(cc * 512, (cc + 1) * 512)
            for ch in range(2):
                b = proj_fm(ws, ('w', s), 256 + ch * 128, 128, cc)
                ACT(zsc[:, ch, :], PS[b][:, :], AF.Silu, [P(b)], ['zsc'])
            for ch in range(2):
                bv = psA.next()
                bg = psB.next()
                ga, gb = gq[0][:, 0:512], gq[1][:, 0:512]
                for c in range(2):
                    MM(PS[bv][:, :], wgl[:, c, ch * 128:(ch + 1) * 128], gyT[:, c, csl], c == 0, c == 1, ['wgl', 'gyT'], [P(bv)])
                for c in range(2):
                    MM(PS[bg][:, :], wgl[:, c, 256 + ch * 128:256 + (ch + 1) * 128], gyT[:, c, csl], c == 0, c == 1, ['wgl', 'gyT'], [P(bg)])
                ACT(ga, PS[bv][:, :], AF.Identity, [P(bv), 'bgl'], ['gq0'], bias=bgl[:, ch:ch + 1])
                ACT(gb, PS[bg][:, :], AF.Sigmoid, [P(bg), 'bgl'], ['gq1'], bias=bgl[:, 2 + ch:3 + ch])
                TT('pool', ga, ga, gb, ALU.mult, ['gq0', 'gq1'], ['gq0'])
                TT('dve', brb[0][:, ch, :], ga, zsc[:, ch, :], ALU.mult, ['gq0', 'zsc'], [('brb', 0)])
            store_chunk(2, cc, brb[0], 0)
        if debug == 'C':
            break

        for br in range(2):
            AR.reset()
            nq = 3 if br == 0 else 2
            qT = r3(AR.get("qT", nq * S_LEN, BF16), nq)
            kT = r3(AR.get("kT", (3 if br == 0 else 4) * S_LEN, BF16), 3 if br == 0 else 4)
            VW = 128
            vaug = AR.get("vaug", NT * 4 * VW, BF16).rearrange("p (t h w) -> p t h w", t=NT, h=4)
            MEMSET('pool', kT[:, :, :], 0.0, ['kT'])
            MEMSET('pool', vaug[:, :, :, 65:128], 0.0, ['vaug'])
            zsT = r3(AR.get("zsT", 2 * S_LEN, BF16), 2)
            sgb = AR.get("sgb", 64, F32)
            dl = AR.get("dl", 128, F32)
            ssq = AR.get("ssq", 16, F32)
            mark = AR.mark()
            ropeC = AR.get("ropeC", S_LEN, F32)
            ropeS = AR.get("ropeS", S_LEN, F32)
            t1 = AR.get("t1", 512, F32)
            t2 = AR.get("t2", 512, F32)
            qa = [AR.get("qa0", 512, BF16)]
            permAB = perm32 if br == 0 else perm64
            DMA('sp', ropeC, rope_d[2 * br, :, :], (), ['ropeC'], 'c2')
            DMA('sp', ropeS, rope_d[2 * br + 1, :, :], (), ['ropeS'], 'c2')
            MEMSET('pool', vaug[:, :, :, 64:65], 1.0, ['vaug'])
            off1, off2 = (OFF_A1, OFF_A2) if br == 0 else (OFF_B1, OFF_B2)
            s, ws = get_w(('AB1', l, br), 1024, w_rows(win_d[l], off1, 1024))
            s2, ws2 = get_w(('AB2', l, br), 512, w_rows(win_d[l], off2, 512))
            if br == 0:
                for ti in range(3):
                    m = 96 if ti < 2 else 64
                    rope_proj(ws, ('w', s), ti * 96, 256 + ti * 96, qT[:, ti, :], 'qT', ropeC, ropeS, t1, t2, m=m, perm=permAB, qa=qa)
                    rope_proj(ws, ('w', s), 512 + ti * 96, 768 + ti * 96, kT[:, ti, :], 'kT', ropeC, ropeS, t1, t2, m=m, perm=permAB, qa=qa)
            else:
                for ch in range(2):
                    rope_proj(ws, ('w', s), ch * 128, 256 + ch * 128, qT[:, ch, :], 'qT', ropeC, ropeS, t1, t2, perm=permAB, qa=qa)
                    rope_proj(ws, ('w', s), 512 + ch * 128, 768 + ch * 128, (kT[:, 2 * ch, :], kT[:, 2 * ch + 1, :]), 'kT',
                              ropeC, ropeS, t1, t2, perm=permAB, qa=qa)
            v_proj(ws2, ('w', s2), 0, 4, vaug, 'vaug')
            zs_proj(ws2, ('w', s2), 256, zsT)
            AR.release_to(mark)
            if br == 0:
                prefetch_w(('AB1', l, 1), 1024, w_rows(win_d[l], OFF_B1, 1024))
                prefetch_w(('AB2', l, 1), 512, w_rows(win_d[l], OFF_B2, 512))
            else:
                prefetch_w(('D1', l), 512, w_rows(win_d[l], OFF_D1, 512))
                prefetch_w(('D2', l), 768, w_rows(win_d[l], OFF_D2, 768))
            brb = [r3(AR.get(f"brb{i}", 2 * 512, BF16), 2) for i in range(2 if br == 1 else 1)]
            Ebufs = [AR.get(f"E{i}", 512, BF16) for i in range(3)]
            OTs = [AR.get(f"ots{i}", 512, F32) for i in range(2 if br == 1 else 1)]
            obf = AR.get("obf", 4 * 256, BF16).rearrange("p (j c) -> p j c", j=4)
            if br == 0:
                tmpA = AR.get("tmpA", 4 * 8 * 64, F32).rearrange("p (j v d) -> p j v d", j=4, v=8)
                ocomb = AR.get("ocomb", 4 * 256, F32)
                osq = tmpA.rearrange("p j v d -> p (j v d)")[:, 0:1024]
                qpad = [AR.get(f"qpad{i}", 512, BF16) for i in range(3)]
                for i_ in range(3):
                    MEMSET('pool', qpad[i_], 0.0, [('qpad', i_)])
                DMA('sp', sgb, sublng_d[l:l + 1, :].partition_broadcast(128), (), ['sgb'], 'c1')
                DMA('sp', dl, dlam_d[l:l + 1, :].partition_broadcast(128), (), ['dl'], 'c1')
                TT('dve', dl[:, 0:32], dl[:, 0:32], dl[:, 32:64], ALU.mult, ['dl'], ['dl'])
                TT('dve', dl[:, 64:96], dl[:, 64:96], dl[:, 96:128], ALU.mult, ['dl'], ['dl'])
                S.op('dve', lambda e, dl=dl: e.tensor_reduce(lamt[:, 0:1], dl[:, 0:32], AX.X, ALU.add), ['dl'], ['lamt'])
                S.op('dve', lambda e, dl=dl: e.tensor_reduce(lamt[:, 1:2], dl[:, 64:96], AX.X, ALU.add), ['dl'], ['lamt'])
                ACT(lamt[:, 0:2], lamt[:, 0:2], AF.Exp, ['lamt'], ['lamt'])
                TT('dve', lamt[:, 2:3], lamt[:, 0:1], lamt[:, 1:2], ALU.subtract, ['lamt'], ['lamt'])
                TS('dve', lamt[:, 3:4], lamt[:, 2:3], lam_init, -1.0, ALU.add, ALU.mult, ['lamt'], ['lamt'])

                def prepA(vh, qc, qT=qT, qpad=qpad):
                    pos = vh % 3
                    CP('pool', qpad[pos][pos * 32:pos * 32 + 32, :], qT[pos * 32:pos * 32 + 32, vh // 3, qc * 512:(qc + 1) * 512],
                       ['qT'], [('qpad', pos)])

                def qfA(vh, qc, qpad=qpad):
                    return qpad[vh % 3], ('qpad', vh % 3), qc * 512

                def kfA(vh, kT=kT):
                    return kT[:, vh // 3, :], 'kT'

                def vfA(vh, kt, vaug=vaug):
                    return vaug[:, kt, vh // 2, :], 'vaug'

                def outA(vh, qc, bo, tmpA=tmpA):
                    norm_out(bo, 65, tmpA[:, :, vh, :], ('tmpA', vh))

                def postA(qc, tmpA=tmpA, ocomb=ocomb, osq=osq, obf=obf, zsT=zsT, brb=brb, ssq=ssq, sgb=sgb, lam_init=lam_init):
                    tv = tmpA.rearrange("p j (h c) d -> p j h c d", c=2)
                    oc = ocomb.rearrange("p (j h d) -> p j h d", j=4, h=4)
                    allt = [('tmpA', v) for v in range(8)]
                    for j in range(4):
                        STT(oc[:, j], tv[:, j, :, 1, :], lamt[:, 3:4], tv[:, j, :, 0, :], ALU.mult, ALU.add, allt + ['lamt'], ['ocomb'])
                    TT('pool', osq, ocomb, ocomb, ALU.mult, ['ocomb'], allt)
                    S.op('dve', lambda e: e.tensor_reduce(ssq, osq.rearrange("p (g d) -> p g d", d=64), AX.X, ALU.add), allt, ['ssq'])
                    TS('dve', ssq, ssq, 1.0 / 64, EPS, ALU.mult, ALU.add, ['ssq'], ['ssq'])
                    ACT(ssq, ssq, AF.Sqrt, ['ssq'], ['ssq'])
                    RECIP(ssq, ssq, ['ssq'], ['ssq'])
                    o3 = ocomb.rearrange("p (g d) -> p g d", d=64)
                    TT('dve', o3, o3, ssq.unsqueeze(2).to_broadcast([128, 16, 64]), ALU.mult, ['ocomb', 'ssq'], ['ocomb'])
                    STT(obf.rearrange("p j (h d) -> p (j h) d", d=64), o3, 1.0 - lam_init,
                        sgb.unsqueeze(1).to_broadcast([128, 16, 64]), ALU.mult, ALU.mult, ['ocomb', 'sgb'], ['obf'])
                    finish_branch(0, obf, qc, zsT, brb)

                attention(8, qfA, kfA, vfA, 32 ** -0.5, causal_pairs, outA, Ebufs, OTs, post=postA, vwm=128,
                          otrot=Rot([0]), prep=prepA)
            else:
                wstrip = AR.get("wstrip", 2048, BF16)
                tmpB = AR.get("tmpB", 4 * 256, F32).rearrange("p (j h d) -> p j h d", j=4, h=4)
                DMA('sp', wstrip, wstrip_d[:, :], (), ['wstrip'], 'c1')

                def qfB(h, qT=qT):
                    return qT[:, h // 2, :], 'qT'

                def kfB(h, kT=kT):
                    return kT[:, h, :], 'kT'

                def vfB(h, kt, vaug=vaug):
                    return vaug[:, kt, h, :], 'vaug'

                def pairsB(qc, wstrip=wstrip):
                    pl = []
                    for kt in range(4 * qc + 4):
                        c0 = max(0, kt - 4 * qc) * 128
                        x0 = qc * 512 + c0 - kt * 128
                        pl.append((kt, c0, 512, [], (wstrip[:, x0:x0 + 512 - c0], 'wstrip')))
                    return pl

                def outB(h, qc, bo, tmpB=tmpB):
                    norm_out(bo, 65, tmpB[:, :, h, :], 'tmpB')

                def postB(qc, tmpB=tmpB, obf=obf, zsT=zsT, brb=brb):
                    CP('act', obf, tmpB.rearrange("p j h d -> p j (h d)"), ['tmpB'], ['obf'])
                    finish_branch(1, obf, qc, zsT, brb)

                attention(4, qfB, kfB, vfB, 64 ** -0.5, pairsB, outB, Ebufs, OTs, post=postB, vwm=128)

        AR.reset()
        zsT = r3(AR.get("zsT", 2 * S_LEN, BF16), 2)
        brb = [r3(AR.get("brb0", 2 * 512, BF16), 2)]
        qT = r3(AR.get("qT", 2 * S_LEN, BF16), 2)
        qrT = r3(AR.get("qrT", 4 * S_LEN, BF16), 4)
        ksT = AR.get("ksT", S_LEN, BF16)
        kwT = AR.get("kwT", S_LEN, BF16)
        vsw = AR.get("vsw", NT * 2 * 128, BF16).rearrange("p (t h w) -> p t h w", t=NT, h=2)
        gts = AR.get("gts", NT * 12, F32).rearrange("p (t g) -> p t g", t=NT)
        kcmpT = AR.get("kcmpT", 128, BF16)
        vca = AR.get("vca", 97, BF16)
        mark = AR.mark()
        ropeC = AR.get("ropeC", S_LEN, F32)
        ropeS = AR.get("ropeS", S_LEN, F32)
        t1 = AR.get("t1", 512, F32)
        t2 = AR.get("t2", 512, F32)
        qa = [AR.get("qa0", 512, BF16)]
        DMA('sp', ropeC, rope_d[2, :, :], (), ['ropeC'], 'c2')
        DMA('sp', ropeS, rope_d[3, :, :], (), ['ropeS'], 'c2')
        MEMSET('pool', vsw[:, :, :, 65:128], 0.0, ['vsw'])
        MEMSET('pool', vsw[:, :, :, 64:65], 1.0, ['vsw'])
        MEMSET('pool', qrT[:, :, :], 0.0, ['qrT'])
        CP('dve', vca[:, 64:97], ovl, ['cbf'], ['vca'])
        s, ws = get_w(('D1', l), 512, w_rows(win_d[l], OFF_D1, 512))
        s2, ws2 = get_w(('D2', l), 768, w_rows(win_d[l], OFF_D2, 768))
        for ch in range(2):
            for tc in range(4):
                b = proj_fm(ws, ('w', s), ch * 128, 128, tc)
                CP('act', qT[:, ch, tc * 512:(tc + 1) * 512], PS[b][:, :], [P(b)], ['qT'])
            rope_proj(ws, ('w', s), ch * 128, 256 + ch * 128, (qrT[:, 2 * ch, :], qrT[:, 2 * ch + 1, :]), 'qrT', ropeC, ropeS, t1, t2, perm=perm64, qa=qa)
        rope_proj(ws2, ('w', s2), 128, 256, ksT, 'ksT', ropeC, ropeS, t1, t2, perm=perm64, qa=qa)
        rope_proj(ws2, ('w', s2), 384, 512, kwT, 'kwT', ropeC, ropeS, t1, t2, perm=perm64, qa=qa)
        for t in range(NT):
            b = proj_tm(ws2, ('w', s2), 640, 128, t)
            CP('act', vsw[:, t, :, 0:64], PS[b][:, 0:128].rearrange("p (h d) -> p h d", h=2), [P(b)], ['vsw'])
        AR.release_to(mark)
        kvA = AR.get("kvA", S_LEN + 32, BF16)
        kvB = AR.get("kvB", S_LEN + 32, BF16)
        peT = AR.get("peT", 32, F32)
        hidT = AR.get("hidT", 4 * 128, BF16).rearrange("p (a n) -> p a n", a=4)
        gh1 = AR.get("gh1", 128, F32)
        gh2 = AR.get("gh2", 128, F32)
        gh3 = AR.get("gh3", 128, F32)
        w2all = AR.get("w2", 384, BF16)
        w2k = w2all[:, 0:256].rearrange("p (a d) -> p a d", a=2)
        w2v = w2all[:, 256:384].rearrange("p (a d) -> p a d", a=2)
        DMA('sp', peT, nsape_d[l, :, :], (), ['peT'], 'c1')
        DMA('pool', w2all, nsaw2_d[l], (), ['w2'], 'c3')
        for tc in range(4):
            b = proj_fm(ws2, ('w', s2), 0, 128, tc)
            sl = slice(tc * 512, (tc + 1) * 512)
            TT('dve', kvA[:, sl].rearrange("p (g r) -> p g r", r=16), PS[b][:, :].rearrange("p (g r) -> p g r", r=16),
               peT[:, 0:16].unsqueeze(1).to_broadcast([128, 32, 16]), ALU.add, [P(b), 'peT'], ['kvA'])
            TT('dve', kvB[:, sl].rearrange("p (g r) -> p g r", r=16), PS[b][:, :].rearrange("p (g r) -> p g r", r=16),
               peT[:, 16:32].unsqueeze(1).to_broadcast([128, 32, 16]), ALU.add, [P(b), 'peT'], ['kvB'])
        s3, ws3 = load_w(268, w_rows(win_d[l], OFF_D3, 268))
        zs_proj(ws3, ('w', s3), 0, zsT)
        for t in range(NT):
            b = proj_tm(ws3, ('w', s3), 256, 12, t)
            ACT(gts[:, t, :], PS[b][:, 0:12], AF.Sigmoid, [P(b)], ['gts'])
        sw = []
        for kv in range(2):
            sl_ = wrot.next()
            rows = slice(kv * 64, kv * 64 + 64)
            DMA('pool', wbuf[sl_][rows, 0:8192], nsaw1_d[l, kv], (), [('w', sl_)], f'w{sl_}')
            sw.append(sl_)
        for kv in range(2):
            rows = slice(kv * 64, kv * 64 + 64)
            w1v = wbuf[sw[kv]][:, 0:8192].rearrange("p (j h) -> p j h", j=32)
            for hc in range(2):
                b = psA.next()
                for j in range(32):
                    srcT = kvA if j < 16 else kvB
                    srcv = srcT[rows, j:j + 2032].rearrange("p (n r) -> p n r", r=16)[:, :, 0]
                    MM(PS[b][:, 0:127], w1v[rows, j, hc * 128:(hc + 1) * 128], srcv, j == 0, j == 31,
                       [('w', sw[kv]), 'kvA', 'kvB'], [P(b)])
                CP('act', gh3[:, 0:127], PS[b][:, 0:127], [P(b)], ['gh3'])
                TT('dve', gh1[:, 0:127], gh3[:, 0:127], gh3[:, 0:127], ALU.mult, ['gh3'], ['gh1'])
                TS('dve', gh1[:, 0:127], gh1[:, 0:127], 0.044715, 1.0, ALU.mult, ALU.add, ['gh1'], ['gh1'])
                TT('dve', gh2[:, 0:127], gh1[:, 0:127], gh3[:, 0:127], ALU.mult, ['gh1', 'gh3'], ['gh2'])
                ACT(gh2[:, 0:127], gh2[:, 0:127], AF.Sigmoid, ['gh2'], ['gh2'], scale=1.5957691216)
                TT('dve', hidT[:, kv * 2 + hc, 0:127], gh2[:, 0:127], gh3[:, 0:127], ALU.mult, ['gh2', 'gh3'], ['hidT'])
        b = psA.next()
        for c in range(2):
            MM(PS[b][:, 0:127], w2k[:, c, :], hidT[:, c, 0:127], c == 0, c == 1, ['w2', 'hidT'], [P(b)])
        CP('act', kcmpT[:, 0:127], PS[b][:, 0:127], [P(b)], ['kcmpT'])
        b = psA.next()
        for c in range(2):
            MM(PS[b][0:127, 0:64], hidT[:, 2 + c, 0:127], w2v[:, c, :], c == 0, c == 1, ['w2', 'hidT'], [P(b)])
        CP('act', vca[0:127, 0:64], PS[b][0:127, 0:64], [P(b)], ['vca'])

        AR.release_to(mark)
        Ebufs = [AR.get(f"E{i}", 512, BF16) for i in range(2)]
        OTs = [AR.get("ots0", 512, F32)]
        erotD, otrotD = Rot([0, 1]), Rot([0])
        oacc = AR.get("oacc", 4 * 256, F32).rearrange("p (j h d) -> p j h d", j=4, h=4)
        ntmps = [AR.get("ntmp0", 4 * 64, F32).rearrange("p (j d) -> p j d", j=4)]
        NT_ROT = Rot(range(len(ntmps)))
        ntmp2 = AR.get("ntmp2", 4 * 32, F32).rearrange("p (j k) -> p j k", j=4)
        obf = AR.get("obf", 4 * 256, BF16).rearrange("p (j c) -> p j c", j=4)
        MnegT = AR.get("MnegT", 1024, BF16)
        negcmp = AR.get("negcmp", 2048, BF16)
        selc = AR.get("selc", 768, F32).rearrange("p (k t j) -> p k t j", k=3, t=8)
        imp = AR.get("imp", 4 * 32, F32).rearrange("p (j k) -> p j k", j=4)
        impm = AR.get("impm", 32, F32)
        impm2 = AR.get("impm2", 32, F32)
        mx8 = AR.get("mx8", 16, F32)
        selm = AR.get("selm", 32, F32)
        selb = AR.get("selb", 32, BF16)
        prefetch_w(('E1', l), 512, w_rows(win_d[l], OFF_E, 512))
        prefetch_w(('E2', l), 512, w_rows(wmemkv_d[l], 0, 512))
        MEMSET('pool', MnegT, 0.0, ['MnegT'])
        DMA('sp', negcmp, negcmp_d[:, :], (), ['negcmp'], 'c1')
        DMA('sp', selc, selc_d[:, :].rearrange("p (k t j) -> p k t j", k=3, t=8), (), ['selc'], 'c1')

        def qfD(h):
            pb = (h % 2) * 64
            return qT[pb:pb + 64, h // 2, :], 'qT'

        def qfDr(h):
            return qrT[:, h, :], 'qrT'

        def hrows(h):
            return slice((h % 2) * 64, (h % 2) * 64 + 64)

        for qc in range(4):
            def pairs_cmp(qc_):
                return [(0, 0, 512, [(ident[0:127, 0:127], negcmp[0:127, qc_ * 512:(qc_ + 1) * 512], 0, 512, ['cbf', 'negcmp'])], None)]

            def out_cmp(h, qc_, bo):
                rd, rdres = norm_out(bo, 97, oacc[:, :, h, :], ('oacc', h))
                O = PS[bo][:, 0:4 * 97].rearrange("p (j w) -> p j w", j=4)
                if qc_ >= 2:
                    rb = rd.unsqueeze(2).to_broadcast([128, 4, 32])
                    if h == 0:
                        TT('dve', imp, O[:, :, 65:97], rb, ALU.mult, [P(bo), rdres], ['imp'])
                    else:
                        TT('dve', ntmp2, O[:, :, 65:97], rb, ALU.mult, [P(bo), rdres], ['ntmp2'])
                        TT('dve', imp, imp, ntmp2, ALU.add, ['ntmp2', 'imp'], ['imp'])
                coef = gts[:, 4 * qc_:4 * qc_ + 4, 3 * h]
                TT('dve', oacc[:, :, h, :], oacc[:, :, h, :], coef.unsqueeze(2).to_broadcast([128, 4, 64]), ALU.mult,
                   [('oacc', h), 'gts'], [('oacc', h)])

            attention(4, qfD, lambda h: (kcmpT[hrows(h), :], 'kcmpT'), lambda h, kt: (vca[0:127, :], 'vca'), 64 ** -0.5,
                      pairs_cmp, out_cmp, Ebufs, OTs, vw=97, kparts=127, qcs=[qc], erot=erotD, otrot=otrotD)
            if qc >= 2:
                for jl in range(4):
                    qt8 = 4 * qc + jl - 8
                    cand, candm1, forced = selc[:, 0, qt8, :], selc[:, 1, qt8, :], selc[:, 2, qt8, :]
                    TT('dve', impm, imp[:, jl, :], cand, ALU.mult, ['imp', 'selc'], ['impm'])
                    TT('dve', impm, impm, candm1, ALU.add, ['impm', 'selc'], ['impm'])
                    S.op('dve', lambda e: e.max(mx8[:, 0:8], impm), ['impm'], ['mx8'])
                    S.op('dve', lambda e: e.match_replace(impm2, mx8[:, 0:8], impm, -1e9), ['mx8', 'impm'], ['impm2'])
                    S.op('dve', lambda e: e.max(mx8[:, 8:16], impm2), ['impm2'], ['mx8'])
                    TS('dve', selm, impm, mx8[:, 12:13], None, ALU.is_ge, None, ['impm', 'mx8'], ['selm'])
                    TT('dve', selm, selm, cand, ALU.mult, ['selm', 'selc'], ['selm'])
                    TT('dve', selm, selm, forced, ALU.max, ['selm', 'selc'], ['selm'])
                    TS('dve', selb, selm, -1.0, BIG, ALU.add, ALU.mult, ['selm'], ['selb'])
                    b = psT.next()
                    pst = PS[b][:, :].bitcast(BF16)
                    TR(pst[0:32, 0:128], selb, ['selb', 'cbf'], [P(b)])
                    CP('act', MnegT[0:32, (4 * qc + jl - 8) * 128:(4 * qc + jl - 7) * 128], pst[0:32, 0:128], [P(b)], ['MnegT'])

            def extra_slc(qc_, kt, c0):
                if qc_ < 2:
                    return []
                q0 = qc_ * 512 + c0 - 1024
                return [(expand[:, kt * 128:(kt + 1) * 128], MnegT[:, q0:q0 + 512 - c0], 0, 512 - c0, ['cbf', 'MnegT'])]

            def out_slc(h, qc_, bo):
                ti_ = NT_ROT.next()
                norm_out(bo, 65, oacc[:, :, h, :], ('oacc', h), coef=gts[:, 4 * qc_:4 * qc_ + 4, 3 * h + 1], coefres='gts',
                         accumulate=True, tmp=(ntmps[ti_], ('ntmp', ti_)))

            attention(4, qfDr, lambda h: (ksT[:, :], 'ksT'), lambda h, kt: (vsw[:, kt, 0, :], 'vsw'), 64 ** -0.5,
                      lambda q_: causal_pairs(q_, extra_slc), out_slc, Ebufs, OTs, qcs=[qc], vwm=128, erot=erotD, otrot=otrotD)

            def pairs_win(qc_):
                pl = []
                for kt in range(max(0, 4 * qc_ - 4), 4 * qc_ + 4):
                    c0 = max(0, kt - 4 * qc_) * 128
                    c1 = min(4, kt + 5 - 4 * qc_) * 128
                    adds = []
                    if kt >= 4 * qc_:
                        adds.append((ident, negtri, 0, 128, ['cbf']))
                    if kt + 4 <= 4 * qc_ + 3:
                        jl = kt + 4 - 4 * qc_
                        adds.append((ident, negtri2, jl * 128 - c0, 128, ['cbf']))
                    pl.append((kt, c0, c1, adds, None))
                return pl

            def out_win(h, qc_, bo):
                ti_ = NT_ROT.next()
                norm_out(bo, 65, oacc[:, :, h, :], ('oacc', h), coef=gts[:, 4 * qc_:4 * qc_ + 4, 3 * h + 2], coefres='gts',
                         accumulate=True, tmp=(ntmps[ti_], ('ntmp', ti_)))

            attention(4, qfDr, lambda h: (kwT[:, :], 'kwT'), lambda h, kt: (vsw[:, kt, 1, :], 'vsw'), 64 ** -0.5,
                      pairs_win, out_win, Ebufs, OTs, qcs=[qc], vwm=128, erot=erotD, otrot=otrotD)
            CP('act', obf, oacc.rearrange("p j h d -> p j (h d)"), [('oacc', h_) for h_ in range(4)], ['obf'])
            finish_branch(3, obf, qc, zsT, brb)

        AR.reset()
        zsT = r3(AR.get("zsT", 2 * S_LEN, BF16), 2)
        brb = [r3(AR.get(f"brb{i}", 2 * 512, BF16), 2) for i in range(2)]
        qT = r3(AR.get("qT", 2 * S_LEN, BF16), 2)
        mems = AR.get("mems", 2 * D, F32).rearrange("p (t d) -> p t d", t=2)
        mgb = AR.get("mgb", D, F32)
        hb0 = AR.get("hb0", D, BF16)
        hb1 = AR.get("hb1", D, BF16)
        junk = AR.get("junk", D, BF16)
        memhT = r3(AR.get("memhT", 8 * 256, BF16), 8)
        kmT = r3(AR.get("kmT", 4 * 256, BF16), 4)
        vm = AR.get("vm", 2 * 4 * 128, BF16).rearrange("p (t h w) -> p t h w", t=2, h=4)
        Ebufs = [AR.get(f"E{i}", 512, BF16) for i in range(3)]
        OTs = [AR.get(f"ots{i}", 512, F32) for i in range(2)]
        tmpE = AR.get("tmpE", 4 * 256, F32).rearrange("p (j h d) -> p j h d", j=4, h=4)
        obf = AR.get("obf", 4 * 256, BF16).rearrange("p (j c) -> p j c", j=4)
        DMA('sp', mems, mem_d.rearrange("(t p) d -> p t d", p=128), (), ['mems'], 'c1')
        DMA('sp', mgb, memg_d[l:l + 1, :].partition_broadcast(128), (), ['mgb'], 'c1')
        MEMSET('pool', vm[:, :, :, 65:128], 0.0, ['vm'])
        MEMSET('pool', vm[:, :, :, 64:65], 1.0, ['vm'])
        MEMSET('pool', kmT[:, :, :], 0.0, ['kmT'])
        rmsnorm_rows(mems, 2, lambda t: 'mems', mgb, 'mgb', memhT, 'memhT', [hb0, hb1], junk)
        s, ws = get_w(('E1', l), 512, w_rows(win_d[l], OFF_E, 512))
        s2, ws2 = get_w(('E2', l), 512, w_rows(wmemkv_d[l], 0, 512))
        for ch in range(2):
            for tc in range(4):
                b = proj_fm(ws, ('w', s), ch * 128, 128, tc)
                CP('act', qT[:, ch, tc * 512:(tc + 1) * 512], PS[b][:, :], [P(b)], ['qT'])
        zs_proj(ws, ('w', s), 256, zsT)
        for ch in range(2):
            b = proj_fm(ws2, ('w', s2), ch * 128, 128, 0, src=memhT, srcres='memhT', ntok=256)
            CP('act', kmT[0:64, 2 * ch, :], PS[b][0:64, 0:256], [P(b)], ['kmT'])
            CP('act', kmT[64:128, 2 * ch + 1, :], PS[b][64:128, 0:256], [P(b)], ['kmT'])
        for t in range(2):
            b = proj_tm(ws2, ('w', s2), 256, 256, t, src=memhT, srcres='memhT')
            CP('act', vm[:, t, :, 0:64], PS[b][:, 0:256].rearrange("p (h d) -> p h d", h=4), [P(b)], ['vm'])

        def qfE(h):
            return qT[:, h // 2, :], 'qT'

        def kfE(h):
            return kmT[:, h, :], 'kmT'

        def outE(h, qc, bo):
            norm_out(bo, 65, tmpE[:, :, h, :], 'tmpE')

        def postE(qc):
            CP('act', obf, tmpE.rearrange("p j h d -> p j (h d)"), ['tmpE'], ['obf'])
            finish_branch(4, obf, qc, zsT, brb)

        if not debug:
            prefetch_w(('M', l, 0, 0), 1024, w_rows(wmerge_d[l], 0, 1024))
        attention(4, qfE, kfE, lambda h, kt: (vm[:, kt, h, :], 'vm'), 64 ** -0.5,
                  lambda qc: [(0, 0, 512, [], None), (1, 0, 512, [], None)], outE, Ebufs, OTs, post=postE, vwm=128)

        if debug:
            break

        AR.reset()
        bm = AR.get("bm", 40, F32)
        DMA('sp', bm, bmerge_d[l, :, :], (), ['bm'], 'c1')
        wbr = [r3(AR.get(f"wbr{i}", 2 * D, BF16), 2) for i in range(2)]
        mixed = r3(AR.get("mixed", 8 * 1024, F32), 8)
        brc = [r3(AR.get(f"brc{i}", 2 * 1024, BF16), 2) for i in range(2)]
        gate = [AR.get(f"gate{i}", 512, BF16) for i in range(2)]
        prod = [AR.get(f"prod{i}", 512, F32) for i in range(2)]
        mbf = [r3(AR.get(f"mbf{i}", 8 * 128, BF16), 8) for i in range(2)]
        psMg = Rot([0, 1, 4, 6])
        psMy = Rot([2, 3, 5, 7])
        for tp in range(2):
            tsl = slice(tp * 1024, (tp + 1) * 1024)
            for n in range(5):
                s, ws = get_w(('M', l, tp, n), 1024, w_rows(wmerge_d[l], n * 1024, 1024))
                if n < 4:
                    prefetch_w(('M', l, tp, n + 1), 1024, w_rows(wmerge_d[l], (n + 1) * 1024, 1024))
                else:
                    prefetch_w(('O', l, tp), 1024, w_rows(wout_d[l], 0, 1024))
                wb = n % 2
                if n == 0 and tp == 0:
                    DMA('pool', wbr[0], wbranch_d[l, 0].rearrange("(c p) n -> p c n", p=128), (), [('wbr', 0)], 'wbr0')
                if n < 4:
                    nb = (n + 1) % 2
                    DMA('pool', wbr[nb], wbranch_d[l, n + 1].rearrange("(c p) n -> p c n", p=128), (), [('wbr', nb)], f'wbr{nb}')
                def load_brc(n_, tp_):
                    wb_ = n_ % 2
                    tsl_ = slice(tp_ * 1024, (tp_ + 1) * 1024)
                    DMA('sp', brc[wb_], brt_d[2 * n_:2 * n_ + 2, :, tsl_].rearrange("c p t -> p c t"),
                        [('brt', 2 * n_ + hf_, 2 * tp_ + q_) for hf_ in range(2) for q_ in range(2)], [('brc', wb_)], f'brc{wb_}')
                if n == 0 and tp == 0:
                    load_brc(0, 0)
                if n < 4:
                    load_brc(n + 1, tp)
                for dc in range(8):
                    for t2_ in range(2):
                        tsub = slice(tp * 1024 + t2_ * 512, tp * 1024 + (t2_ + 1) * 512)
                        bg = psMg.next()
                        for c in range(8):
                            MM(PS[bg][:, :], ws[:, c, dc * 128:(dc + 1) * 128], hT[:, c, tsub], c == 0, c == 7, [('w', s), 'hT'], [P(bg)])
                        by = psMy.next()
                        for c in range(2):
                            MM(PS[by][:, :], wbr[wb][:, c, dc * 128:(dc + 1) * 128], brc[wb][:, c, t2_ * 512:(t2_ + 1) * 512],
                               c == 0, c == 1, [('wbr', wb), ('brc', wb)], [P(by)])
                        gi = (dc * 2 + t2_) % 2
                        ACT(gate[gi], PS[bg][:, :], AF.Sigmoid, [P(bg), 'bm'], [('gate', gi)], bias=bm[:, n * 8 + dc:n * 8 + dc + 1])
                        msl = mixed[:, dc, t2_ * 512:(t2_ + 1) * 512]
                        if n == 0:
                            TT('dve', msl, PS[by][:, :], gate[gi], ALU.mult, [P(by), ('gate', gi)], [('mixed', dc, t2_)])
                        else:
                            TT('dve', prod[gi], PS[by][:, :], gate[gi], ALU.mult, [P(by), ('gate', gi)], [('prod', gi)])
                            TT('pool', msl, msl, prod[gi], ALU.add, [('prod', gi), ('mixed', dc, t2_)], [('mixed', dc, t2_)])
            s, ws = get_w(('O', l, tp), 1024, w_rows(wout_d[l], 0, 1024))
            if tp == 0:
                prefetch_w(('M', l, 1, 0), 1024, w_rows(wmerge_d[l], 0, 1024))
                DMA('pool', wbr[0], wbranch_d[l, 0].rearrange("(c p) n -> p c n", p=128), (), [('wbr', 0)], 'wbr0')
                load_brc(0, 1)
            elif l + 1 < n_layers:
                prefetch_w(('C', l + 1), 512, w_rows(win_d[l + 1], OFF_C, 512))
            for tt in range(8):
                t = tp * 8 + tt
                mi = tt % 2
                CP('act', mbf[mi], mixed[:, :, tt * 128:(tt + 1) * 128],
                   [('mixed', dc, tt // 4) for dc in range(8)], [('mbf', mi)])
                for half in range(2):
                    b = psC.next()
                    for c in range(8):
                        MM(PS[b][:, :], mbf[mi][:, c, :], ws[:, c, half * 512:(half + 1) * 512], c == 0, c == 7, [('mbf', mi), ('w', s)], [P(b)])
                    TT('dve', xs[:, t, half * 512:(half + 1) * 512], xs[:, t, half * 512:(half + 1) * 512], PS[b][:, :], ALU.add,
                       [P(b), ('x', t)], [('x', t)])

    if not debug:
        AR.reset()
        gbc = AR.get("gbc", D, F32)
        DMA('sp', gbc, finalg_d[0:1, :].partition_broadcast(128), (), ['gbc'], 'c1')
        junk = AR.get("junk", D, BF16)
        ob = [AR.get(f"ob{i}", D, F32) for i in range(2)]
        for t in range(NT):
            ACT(junk, xs[:, t, :], AF.Square, [('x', t)], ['junk', ('ss', t)], accum_out=ss[:, t:t + 1])
        allss = [('ss', t) for t in range(NT)]
        TS('dve', rs, ss, 1.0 / D, EPS, ALU.mult, ALU.add, allss, ['rs'])
        ACT(rs, rs, AF.Sqrt, ['rs'], ['rs'])
        RECIP(rs, rs, ['rs'], ['rs'])
        for t in range(NT):
            STT(ob[t % 2], xs[:, t, :], rs[:, t:t + 1], gbc, ALU.mult, ALU.mult, [('x', t), 'rs', 'gbc'], [('ob', t % 2)])
            DMA('sp', out_d[t * 128:(t + 1) * 128, :], ob[t % 2], [('ob', t % 2)], [('out', t)], f'o{t % 2}')
    S.final_wait('sp')

    sem_names = list(ENGS[:4]) + sorted(S.dmacnt.keys())
    sems = {k: es.enter_context(nc.semaphore(f"s_{k}")) for k in sem_names}
    block = es.enter_context(nc.Block())

    def mk(eng):
        def body(engine):
            for waits, fn, tok in S.q[eng]:
                for k, v in waits.items():
                    engine.wait_ge(sems[k], v)
                if fn is None:
                    continue
                ins = fn(engine)
                ins.then_inc(sems[tok[0]], 1 if tok[0] in ENGS else 16)
        return body

    block.tensor(mk('pe'))
    block.scalar(mk('act'))
    block.vector(mk('dve'))
    block.gpsimd(mk('pool'))
    block.sync(mk('sp'))
    es.close()
    return nc


def _host_consts():
    bf = ml_dtypes.bfloat16
    k = np.arange(128)[:, None]
    q = np.arange(128)[None, :]
    ident = (k == q).astype(np.float32)
    negtri = np.where(k > q, -BIG, 0.0).astype(np.float32)
    negtri2 = np.where(q >= k, -BIG, 0.0).astype(np.float32)
    expand = np.zeros((128, 16, 128), np.float32)
    for kt in range(16):
        for kk in range(128):
            expand[2 * kt + kk // 64, kt, kk] = 1.0
    n_cmp = 127
    c0 = np.arange(n_cmp)[:, None] * 16
    s0 = np.arange(32)[None, :] * 64
    overlap = np.clip(np.minimum(c0 + 32, s0 + 64) - np.maximum(c0, s0), 0, None) / 16
    ovl = np.zeros((128, 33), np.float32)
    ovl[:127, 0] = 1.0
    ovl[:127, 1:] = overlap
    perms = []
    for dh in (32, 64):
        pm = np.zeros((128, 128), np.float32)
        for m_ in range(128):
            blk, j = m_ // dh, m_ % dh
            pm[blk * dh + (j + dh // 2) % dh, m_] = 1.0
        perms.append(pm)
    cbf = np.concatenate([ident, negtri, negtri2, expand.reshape(128, 2048), ovl] + perms, axis=1).astype(bf)
    x = np.arange(2048)[None, :]
    dlt = x - k
    wst = ((dlt >= 0) & (dlt <= 128)).astype(np.float32) + ((dlt >= 0) & (dlt % 4 == 0) & (dlt <= 512)).astype(np.float32) \
        + ((dlt >= 0) & (dlt % 16 == 0) & (dlt <= 2048)).astype(np.float32)
    wstrip = wst.astype(bf)
    negcmp = np.where(16 * k + 31 <= x, 0.0, -BIG).astype(np.float32)
    negcmp[127, :] = -BIG
    negcmp = negcmp.astype(bf)
    t = np.arange(S_LEN, dtype=np.float32)
    rope = np.zeros((4, 128, S_LEN), np.float32)
    for ti, dh in enumerate((32, 64)):
        half = dh // 2
        inv = (np.float32(10000.0) ** (-np.arange(half, dtype=np.float32) / np.float32(half))).astype(np.float32)
        ang = (t[:, None] * inv[None, :]).astype(np.float32)
        cs, sn = np.cos(ang).astype(np.float32), np.sin(ang).astype(np.float32)
        for p in range(128):
            j = p % dh
            rope[2 * ti, p] = cs[:, j % half]
            rope[2 * ti + 1, p] = sn[:, j % half] * (-1.0 if j < half else 1.0)
    s5k = np.concatenate([np.arange(17), 15 - np.arange(16), np.arange(128)]).astype(np.float32)
    s5k = np.ascontiguousarray(np.broadcast_to(s5k[None, :], (128, 161)))
    s5mm = np.zeros((128, 2, 16, 16), np.float32)
    for s8 in range(8):
        for sh in range(2):
            s5mm[s8 * 16:(s8 + 1) * 16, sh, sh * 8 + s8:, :] = 1.0
    s5mm = np.ascontiguousarray(s5mm.reshape(128, 512).astype(bf))
    sel = np.zeros((128, 3, 8, 32), np.float32)
    for qt in range(8, 16):
        for qq in range(128):
            qblk = (qt * 128 + qq) // 64
            j = np.arange(32)
            cand = ((j >= 1) & (j <= qblk - 2)).astype(np.float32)
            forced = ((j == 0) | (j == qblk) | (j == qblk - 1)).astype(np.float32)
            sel[qq, 0, qt - 8] = cand
            sel[qq, 1, qt - 8] = cand - 1.0
            sel[qq, 2, qt - 8] = forced
    return dict(cbf=np.ascontiguousarray(cbf), wstrip=np.ascontiguousarray(wstrip), negcmp=np.ascontiguousarray(negcmp),
                rope=rope, identf=np.ascontiguousarray(ident, dtype=np.float32), s5k=s5k, s5mm=s5mm, selc=np.ascontiguousarray(sel.reshape(128, 768)))


def _host_layout(inp):
    offs = np.cumsum([0] + list(IN_SIZES))
    col = {n: np.arange(offs[i], offs[i + 1]) for i, n in enumerate(IN_NAMES)}

    def swap(cols, dh):
        c = cols.reshape(-1, dh)
        h = dh // 2
        return np.concatenate([c[:, h:], c[:, :h]], axis=1).reshape(-1)

    order = np.concatenate([
        col['a_q'], swap(col['a_q'], 32), col['a_k'], swap(col['a_k'], 32), col['a_v'], col['a_z'],
        col['b_q'], swap(col['b_q'], 64), col['b_k'], swap(col['b_k'], 64), col['b_v'], col['b_z'],
        col['c_u'], col['c_z'],
        col['d_q'], swap(col['d_q'], 64),
        col['d_kc'], col['d_vc'], col['d_ks'], col['d_ks'], swap(col['d_ks'], 64), swap(col['d_ks'], 64),
        col['d_kw'], col['d_kw'], swap(col['d_kw'], 64), swap(col['d_kw'], 64), col['d_vs'], col['d_vw'],
        col['d_z'], col['d_g'],
        col['e_q'], col['e_z']])
    assert order.shape[0] == NWIN
    f = lambda a: np.ascontiguousarray(np.asarray(a, dtype=np.float32))
    d = {}
    d['win'] = f(inp['w_in'][:, :, order])
    d['wmerge'] = f(inp['w_merge'])
    d['bmerge'] = f(inp['b_merge'].reshape(DEPTH, 40, 128).transpose(0, 2, 1))
    d['wbranch'] = f(inp['w_branch'])
    d['wout'] = f(inp['w_out'])
    d['normg'] = f(inp['norm_g'])
    d['finalg'] = f(inp['final_g'].reshape(1, D))
    d['memg'] = f(inp['mem_norm_g'])
    d['wmemkv'] = f(inp['w_mem_kv'])
    d['dlam'] = f(inp['diff_lambda'].reshape(DEPTH, 128))
    d['sublng'] = f(inp['diff_subln_g'])
    lr, li, ldt = inp['s5_lambda_re'], inp['s5_lambda_im'], inp['s5_log_dt']
    l1 = np.zeros((DEPTH, 128, 24), np.float32)
    for j in range(8):
        for g2 in range(2):
            g = 2 * j + g2
            l1[:, g2 * 64:(g2 + 1) * 64, j] = lr[:, g, :]
            l1[:, g2 * 64:(g2 + 1) * 64, 8 + j] = li[:, g, :]
            l1[:, g2 * 64:(g2 + 1) * 64, 16 + j] = ldt[:, g][:, None]
    d['s5l1'] = l1
    bre, bim = inp['s5_b_re'], inp['s5_b_im']
    cre, cim = inp['s5_c_re'], inp['s5_c_im']
    bT = np.zeros((DEPTH, 128, 8, 2, 16), np.float32)
    cTt = np.zeros((DEPTH, 128, 8, 2, 16), np.float32)
    for j in range(8):
        for g2 in range(2):
            g = 2 * j + g2
            rows = slice(g2 * 64, (g2 + 1) * 64)
            bT[:, rows, j, 0, :] = bre[:, g]
            bT[:, rows, j, 1, :] = bim[:, g]
            cTt[:, rows, j, 0, :] = cre[:, g].transpose(0, 2, 1)
            cTt[:, rows, j, 1, :] = cim[:, g].transpose(0, 2, 1)
    d['s5bT'] = bT.reshape(DEPTH, 128, 256)
    d['s5cT'] = cTt.reshape(DEPTH, 128, 256)
    d['s5dflat'] = f(inp['s5_d'].reshape(DEPTH, 256))
    d['wglu'] = f(inp['w_glu'])
    d['bglu'] = f(inp['b_glu'].reshape(DEPTH, 4, 128).transpose(0, 2, 1))
    pe = inp['nsa_pe']
    d['nsape'] = f(pe.transpose(0, 1, 3, 2).reshape(DEPTH, 128, 32))
    w1 = inp['nsa_w1'].reshape(DEPTH, 2, 32, 64, 256)
    d['nsaw1'] = f(w1.transpose(0, 1, 3, 2, 4).reshape(DEPTH, 2, 64, 32 * 256))
    w2 = inp['nsa_w2'].reshape(DEPTH, 2, 2, 128, 64)
    w2k = w2[:, 0].transpose(0, 2, 1, 3)
    w2k = np.concatenate([w2k, w2k], axis=3).reshape(DEPTH, 128, 256)
    w2v = w2[:, 1].transpose(0, 2, 1, 3).reshape(DEPTH, 128, 128)
    d['nsaw2'] = f(np.concatenate([w2k, w2v], axis=2))
    return d


_CACHE = {}


def kernel(**inputs):
    inp = {k: np.asarray(v) for k, v in inputs.items()}
    if 'nc' not in _CACHE:
        _CACHE['nc'] = build_program()
        _CACHE['consts'] = _host_consts()
    nc = _CACHE['nc']
    shared = dict(_CACHE['consts'])
    shared.update(_host_layout(inp))
    in_maps = []
    for b in range(8):
        m = dict(shared)
        m['x'] = np.ascontiguousarray(inp['x'][b], dtype=np.float32)
        m['mem'] = np.ascontiguousarray(inp['mem'][b], dtype=np.float32)
        in_maps.append(m)
    res = run_bass_kernel_spmd(nc, in_maps, core_ids=list(range(8)))
    out = np.stack([np.asarray(r['out'], dtype=np.float32) for r in res.results], axis=0)
    return out
```

```python
import math
import numpy as np
import ml_dtypes
from contextlib import ExitStack
import concourse.bass as bass
import concourse.mybir as mybir
from concourse.bass_utils import run_bass_kernel_spmd

F32 = mybir.dt.float32
BF16 = mybir.dt.bfloat16
I32 = mybir.dt.int32
AF = mybir.ActivationFunctionType
ALU = mybir.AluOpType
AX = mybir.AxisListType

S_LEN = 2048
D = 1024
NT = 16
DEPTH = 2
NWIN = 5644
BIG = 32768.0
EPS = 1e-6
TWO_PI = 6.28318
IN_SIZES = (256, 256, 256, 256, 256, 256, 256, 256, 256, 256, 256, 64, 64, 64, 64, 64, 64, 12, 256, 256, 256)
IN_NAMES = ['a_q', 'a_k', 'a_v', 'a_z', 'b_q', 'b_k', 'b_v', 'b_z', 'c_u', 'c_z', 'd_q', 'd_kc', 'd_vc', 'd_ks',
            'd_vs', 'd_kw', 'd_vw', 'd_g', 'd_z', 'e_q', 'e_z']
OFF_A1, OFF_A2, OFF_B1, OFF_B2, OFF_C, OFF_D1, OFF_D2, OFF_D3, OFF_E = 0, 1024, 1536, 2560, 3072, 3584, 4096, 4864, 5132

ENGS = ('pe', 'act', 'dve', 'pool', 'sp')


class Sched:
    def __init__(self):
        self.q = {e: [] for e in ENGS}
        self.cnt = {e: 0 for e in ENGS}
        self.lastw = {}
        self.readers = {}
        self.dmacnt = {}
        self.seen = {e: {} for e in ENGS}
        self.group = {}

    def _need(self, eng, waits, tok):
        if tok is None:
            return
        k, v = tok
        if k == eng and eng == 'pe':
            return
        if self.seen[eng].get(k, 0) >= v:
            return
        if waits.get(k, 0) < v:
            waits[k] = v

    def op(self, eng, fn, R=(), W=(), dma=None):
        waits = {}
        for r in R:
            self._need(eng, waits, self.lastw.get(r))
        for w in W:
            self._need(eng, waits, self.lastw.get(w))
            for t in self.readers.get(w, ()):
                self._need(eng, waits, t)
        for k, v in waits.items():
            self.seen[eng][k] = v
        if dma is None:
            self.cnt[eng] += 1
            tok = (eng, self.cnt[eng])
        else:
            self.dmacnt[dma] = self.dmacnt.get(dma, 0) + 16
            tok = (dma, self.dmacnt[dma])
        for r in R:
            self.readers.setdefault(r, []).append(tok)
        for w in W:
            self.lastw[w] = tok
            self.readers[w] = []
        if dma is not None and (dma.startswith('c') and not dma.startswith('cb') or dma == 'x'):
            g = self.group.setdefault(dma, [])
            g.extend(W)
            for w in g:
                if self.lastw.get(w, (None,))[0] == dma:
                    self.lastw[w] = tok
        self.q[eng].append((waits, fn, tok))

    def barrier(self):
        self.group = {}
        for e in ENGS:
            waits = {}
            for e2 in ENGS:
                if e2 != e and e2 != 'sp' and self.cnt[e2] > 0:
                    self._need(e, waits, (e2, self.cnt[e2]))
            for k, v in self.dmacnt.items():
                self._need(e, waits, (k, v))
            for k, v in waits.items():
                self.seen[e][k] = v
            if waits:
                self.q[e].append((waits, None, None))

    def final_wait(self, eng='sp'):
        waits = {}
        for k, v in self.dmacnt.items():
            self._need(eng, waits, (k, v))
        for e2 in ENGS:
            if e2 != eng and e2 != 'sp' and self.cnt[e2] > 0:
                self._need(eng, waits, (e2, self.cnt[e2]))
        self.q[eng].append((waits, None, None))


class Rot:
    def __init__(self, items):
        self.items = list(items)
        self.i = 0

    def next(self):
        v = self.items[self.i % len(self.items)]
        self.i += 1
        return v


def build_program(debug=False, n_layers=DEPTH):
    nc = bass.Bass("TRN2", target_bir_lowering=False)
    S = Sched()

    def din(name, shape, dt=F32):
        return nc.dram_tensor(name, list(shape), dt, kind="ExternalInput").ap()

    x_d = din("x", [S_LEN, D])
    mem_d = din("mem", [256, D])
    win_d = din("win", [DEPTH, D, NWIN])
    wmerge_d = din("wmerge", [DEPTH, D, 5 * D])
    bmerge_d = din("bmerge", [DEPTH, 128, 40])
    wbranch_d = din("wbranch", [DEPTH, 5, 256, D])
    wout_d = din("wout", [DEPTH, D, D])
    normg_d = din("normg", [DEPTH, D])
    finalg_d = din("finalg", [1, D])
    memg_d = din("memg", [DEPTH, D])
    wmemkv_d = din("wmemkv", [DEPTH, D, 512])
    dlam_d = din("dlam", [DEPTH, 128])
    sublng_d = din("sublng", [DEPTH, 64])
    s5l1_d = din("s5l1", [DEPTH, 128, 24])
    s5bT_d = din("s5bT", [DEPTH, 128, 256])
    s5cT_d = din("s5cT", [DEPTH, 128, 256])
    s5dflat_d = din("s5dflat", [DEPTH, 256])
    s5k_d = din("s5k", [128, 161])
    s5mm_d = din("s5mm", [128, 512], BF16)
    wglu_d = din("wglu", [DEPTH, 256, 512])
    bglu_d = din("bglu", [DEPTH, 128, 4])
    nsape_d = din("nsape", [DEPTH, 128, 32])
    nsaw1_d = din("nsaw1", [DEPTH, 2, 64, 32 * 256])
    nsaw2_d = din("nsaw2", [DEPTH, 128, 384])
    cbf_d = din("cbf", [128, 384 + 2048 + 33 + 256], BF16)
    wstrip_d = din("wstrip", [128, 2048], BF16)
    negcmp_d = din("negcmp", [128, 2048], BF16)
    rope_d = din("rope", [4, 128, S_LEN])
    identf_d = din("identf", [128, 128])
    selc_d = din("selc", [128, 768])
    out_d = nc.dram_tensor("out", [S_LEN, D], F32, kind="ExternalOutput").ap()
    brt_d = nc.dram_tensor("brt", [10, 128, S_LEN], BF16,
                           kind=("ExternalOutput" if debug else "Internal")).ap()

    es = ExitStack()

    def sb(name, shape, dt):
        return es.enter_context(nc.sbuf_tensor(name, list(shape), dt))

    xs = sb("xs", [128, NT, D], F32)
    hT = sb("hT", [128, 8, S_LEN], BF16)
    wbuf = [sb("wbuf0", [128, 8192], BF16), sb("wbuf1", [128, 8192], BF16)]
    cbf = sb("cbf_s", [128, 384 + 2048 + 33 + 256], BF16)
    small = sb("small", [128, 256], F32)
    identf = sb("identf_s", [128, 128], F32)
    ARENA_W = 18680
    arena = sb("arena", [128, ARENA_W], F32)
    PS = [es.enter_context(nc.psum_tensor(f"ps{i}", [128, 512], F32)) for i in range(8)]

    ident = cbf[:, 0:128]
    negtri = cbf[:, 128:256]
    negtri2 = cbf[:, 256:384]
    expand = cbf[:, 384:384 + 2048]
    ovl = cbf[:, 384 + 2048:384 + 2048 + 33]
    perm32 = cbf[:, 2465:2465 + 128]
    perm64 = cbf[:, 2465 + 128:2465 + 256]

    ss = small[:, 0:16]
    rs = small[:, 16:32]
    lamt = small[:, 32:40]

    class Arena:
        def __init__(self):
            self.off = 0

        def reset(self):
            S.barrier()
            self.off = 0

        def mark(self):
            return self.off

        def release_to(self, off):
            S.barrier()
            self.off = off

        def get(self, name, free, dt):
            words = (free * (2 if dt == BF16 else 4) + 3) // 4
            assert self.off + words <= ARENA_W, (name, self.off, words)
            v = arena[:, self.off:self.off + words]
            self.off += words
            if dt != F32:
                v = v.bitcast(dt)
                if v.shape[1] != free:
                    v = v[:, 0:free]
            return v

    AR = Arena()

    def MM(out, lhsT, rhs, start, stop, R, W, sgc=False):
        if sgc:
            S.op('pe', lambda e: e.matmul(out, lhsT=lhsT, rhs=rhs, start=start, stop=stop, skip_group_check=True), R, W)
        else:
            S.op('pe', lambda e: e.matmul(out, lhsT=lhsT, rhs=rhs, start=start, stop=stop), R, W)

    def TR(out, in_, R, W):
        n = in_.shape[0]
        S.op('pe', lambda e: e.transpose(out, in_, ident[0:n, 0:n]), R, W)

    def TRF(out, in_, R, W):
        n = in_.shape[0]
        S.op('pe', lambda e: e.transpose(out, in_, identf[0:n, 0:n]), R, W)

    def ACT(out, in_, func, R, W, bias=0.0, scale=1.0, accum_out=None):
        if accum_out is None:
            S.op('act', lambda e: e.activation(out, in_, func, bias=bias, scale=scale), R, W)
        else:
            S.op('act', lambda e: e.activation(out, in_, func, bias=bias, scale=scale, accum_out=accum_out), R, W)

    def TT(eng, out, in0, in1, op, R, W):
        S.op(eng, lambda e: e.tensor_tensor(out, in0, in1, op), R, W)

    def TS(eng, out, in0, s1, s2, op0, op1, R, W):
        if op1 is None:
            S.op(eng, lambda e: e.tensor_scalar(out, in0, s1, None, op0), R, W)
        else:
            S.op(eng, lambda e: e.tensor_scalar(out, in0, s1, s2, op0, op1), R, W)

    def STT(out, in0, scalar, in1, op0, op1, R, W):
        S.op('dve', lambda e: e.scalar_tensor_tensor(out, in0, scalar, in1, op0, op1), R, W)

    def CP(eng, out, in_, R, W):
        if eng == 'act':
            S.op('act', lambda e: e.copy(out, in_), R, W)
        else:
            S.op(eng, lambda e: e.tensor_copy(out, in_), R, W)

    def RECIP(out, in_, R, W):
        S.op('dve', lambda e: e.reciprocal(out, in_), R, W)

    def MEMSET(eng, out, val, W):
        S.op(eng, lambda e: e.memset(out, val), (), W)

    def DMA(eng, out, in_, R, W, key):
        S.op(eng, lambda e: e.dma_start(out=out, in_=in_), R, W, dma=key)

    psA = Rot([0, 1])
    psB = Rot([2, 3])
    psC = Rot([4, 5])
    psD = Rot([6, 7])
    psS = Rot([0, 1, 4])
    psT = Rot([5])

    def P(i):
        return ('ps', i)

    wrot = Rot([0, 1])

    def load_w(ncols, src3):
        s = wrot.next()
        dst = wbuf[s][:, 0:8 * ncols].rearrange("p (a b) -> p a b", a=8)
        DMA('pool', dst, src3, (), [('w', s)], f'w{s}')
        return s, dst

    pending_w = {}

    def prefetch_w(tag, ncols, src3):
        if tag not in pending_w:
            pending_w[tag] = load_w(ncols, src3)

    def get_w(tag, ncols, src3):
        if tag in pending_w:
            return pending_w.pop(tag)
        return load_w(ncols, src3)

    def w_rows(dram2d, c0, ncols):
        return dram2d[:, c0:c0 + ncols].rearrange("(c p) n -> p c n", p=128)

    DMA('sp', cbf[:, :], cbf_d[:, :], (), ['cbf'], 'c0')
    DMA('sp', identf[:, :], identf_d[:, :], (), ['identf'], 'c0')
    for t4 in range(4):
        DMA('sp', xs[:, 4 * t4:4 * t4 + 4, :], x_d[512 * t4:512 * (t4 + 1), :].rearrange("(t p) d -> p t d", p=128),
            (), [('x', 4 * t4 + i) for i in range(4)], 'x')

    def rmsnorm_rows(src_tile_ap, ntiles, xres, gb_ap, gres, dstT, dstT_res, hb_bufs, junk):
        for t in range(ntiles):
            ACT(junk, src_tile_ap[:, t, :], AF.Square, [xres(t)], ['junk', ('ss', t)], accum_out=ss[:, t:t + 1])
        allss = [('ss', t) for t in range(ntiles)]
        TS('dve', rs[:, 0:ntiles], ss[:, 0:ntiles], 1.0 / D, EPS, ALU.mult, ALU.add, allss, ['rs'])
        ACT(rs[:, 0:ntiles], rs[:, 0:ntiles], AF.Sqrt, ['rs'], ['rs'])
        RECIP(rs[:, 0:ntiles], rs[:, 0:ntiles], ['rs'], ['rs'])
        for t in range(ntiles):
            hb = hb_bufs[t % 2]
            STT(hb, src_tile_ap[:, t, :], rs[:, t:t + 1], gb_ap, ALU.mult, ALU.mult, [xres(t), 'rs', gres], [('hb', t % 2)])
            b = psC.next()
            pst = PS[b][:, :].bitcast(BF16)
            for c in range(8):
                TR(pst[:, c * 128:(c + 1) * 128], hb[:, c * 128:(c + 1) * 128], [('hb', t % 2), 'cbf'], [P(b)])
            CP('act', dstT[:, :, t * 128:(t + 1) * 128], pst.rearrange("p (c k) -> p c k", c=8), [P(b)], [dstT_res])

    def proj_fm(ws, wres, c0, m, tc, src=None, srcres='hT', ntok=512, bank=None):
        src = hT if src is None else src
        b = psA.next() if bank is None else bank
        for c in range(8):
            MM(PS[b][0:m, 0:ntok], ws[:, c, c0:c0 + m], src[:, c, tc * ntok:(tc + 1) * ntok], c == 0, c == 7,
               [wres, srcres], [P(b)])
        return b

    def proj_tm(ws, wres, c0, n, t, src=None, srcres='hT'):
        src = hT if src is None else src
        b = psA.next()
        for c in range(8):
            MM(PS[b][:, 0:n], src[:, c, t * 128:(t + 1) * 128], ws[:, c, c0:c0 + n], c == 0, c == 7, [wres, srcres], [P(b)])
        return b

    QA_ROT = Rot([0, 1])

    def rope_proj(ws, wres, c_orig, c_sw, dst, dstres, ropeC, ropeS, t1, t2, m=128, perm=None, qa=None):
        for tc in range(4):
            b1 = proj_fm(ws, wres, c_orig, m, tc, bank=psA.next())
            b2 = psB.next()
            ai = QA_ROT.next() % len(qa)
            CP('act', qa[ai][0:m, :], PS[b1][0:m, :], [P(b1)], [('qa', ai)])
            MM(PS[b2][0:m, :], perm[0:m, 0:m], qa[ai][0:m, :], True, True, ['cbf', ('qa', ai)], [P(b2)])
            sl = slice(tc * 512, (tc + 1) * 512)
            TT('dve', t1[0:m, :], PS[b1][0:m, :], ropeC[0:m, sl], ALU.mult, [P(b1), 'ropeC', ('qa', ai)], ['t1'])
            TT('dve', t2[0:m, :], PS[b2][0:m, :], ropeS[0:m, sl], ALU.mult, [P(b2), 'ropeS'], ['t2'])
            if isinstance(dst, tuple):
                TT('pool', dst[0][0:64, sl], t1[0:64, :], t2[0:64, :], ALU.add, ['t1', 't2'], [dstres])
                TT('pool', dst[1][64:128, sl], t1[64:128, :], t2[64:128, :], ALU.add, ['t1', 't2'], [dstres])
            else:
                TT('pool', dst[0:m, sl], t1[0:m, :], t2[0:m, :], ALU.add, ['t1', 't2'], [dstres])

    E_ROT = Rot([0, 1, 2])
    OT_ROT = Rot([0, 1])
    RD_ROT = Rot([0, 1, 2, 3])

    def attention(nvh, qf, kf, vf, scale, pairs, outcb, Ebufs, OTs, vw=65, kparts=128, post=None, qcs=range(4), vwm=None,
                  erot=None, otrot=None, batch=1, srot=None, botbank=None, prep=None):
        vwm = vw if vwm is None else vwm
        srot = psS if srot is None else srot
        erot = E_ROT if erot is None else erot
        otrot = OT_ROT if otrot is None else otrot
        items = []
        for qc in qcs:
            for vh in range(nvh):
                plist = pairs(qc)
                grp = {'bot': None}
                for pi, pr in enumerate(plist):
                    items.append((qc, vh, pi, len(plist), pr, grp))
        st = {}

        def doS(i):
            qc, vh, pi, npl, (kt, c0, c1, adds, mul), grp = items[i]
            bs = srot.next()
            st[i] = bs
            if pi == 0 and prep is not None:
                prep(vh, qc)
            qr_ = qf(vh) if prep is None else qf(vh, qc)
            q_ap, qres = qr_[0], qr_[1]
            k_ap, kres = kf(vh)
            ncol = c1 - c0
            q0 = qc * 512 + c0 - (qr_[2] if len(qr_) > 2 else 0)
            MM(PS[bs][0:kparts, 0:ncol], k_ap[:, kt * 128:kt * 128 + kparts], q_ap[:, q0:q0 + ncol], True, len(adds) == 0,
               [qres, kres], [P(bs)])
            for ai, (lT, rh, co, ncl, ares) in enumerate(adds):
                MM(PS[bs][0:kparts, co:co + ncl], lT, rh, False, ai == len(adds) - 1, ares, [P(bs)])

        def doEV(i):
            qc, vh, pi, npl, (kt, c0, c1, adds, mul), grp = items[i]
            bs = st.pop(i)
            ncol = c1 - c0
            ei = erot.next()
            Eb = Ebufs[ei]
            ACT(Eb[0:kparts, 0:ncol], PS[bs][0:kparts, 0:ncol], AF.Exp, [P(bs)], [('E', ei)], scale=scale)
            if mul is not None:
                m_ap, mres = mul
                TT('dve', Eb[0:kparts, 0:ncol], Eb[0:kparts, 0:ncol], m_ap, ALU.mult, [('E', ei), mres], [('E', ei)])
            if grp['bot'] is None:
                grp['bot'] = psD.next() if botbank is None else botbank
            bot = grp['bot']
            v_ap, vres = vf(vh, kt)
            MM(PS[bot][0:vwm, c0:c1], v_ap, Eb[0:kparts, 0:ncol], pi == 0, pi == npl - 1, [('E', ei), vres], [P(bot)], sgc=True)
            if pi == npl - 1:
                bo = psB.next()
                oi = otrot.next()
                ots = OTs[oi]
                CP('dve', ots[0:vw, :], PS[bot][0:vw, :], [P(bot)], [('ots', oi)])
                for jl in range(4):
                    TRF(PS[bo][:, jl * vw:(jl + 1) * vw], ots[0:vw, jl * 128:(jl + 1) * 128], [('ots', oi), 'identf'], [P(bo)])
                outcb(vh, qc, bo)
                if vh == nvh - 1 and post is not None:
                    post(qc)

        n = len(items)
        LA = 2
        for i in range(min(LA, n)):
            doS(i)
        if batch == 1:
            for i in range(n):
                if i + LA < n:
                    doS(i + LA)
                doEV(i)
        else:
            for i0 in range(0, n, 2):
                for i in (i0 + 2, i0 + 3):
                    if i < n:
                        doS(i)
                for i in (i0, i0 + 1):
                    if i < n:
                        doEV(i)

    rdbuf = [small[:, 64 + 4 * i:68 + 4 * i] for i in range(4)]

    def norm_out(bo, vw, dst, dstres, coef=None, coefres=None, accumulate=False, tmp=None):
        O = PS[bo][:, 0:4 * vw].rearrange("p (j w) -> p j w", j=4)
        ri = RD_ROT.next()
        rd = rdbuf[ri]
        TS('dve', rd, O[:, :, 64], 1e-30, None, ALU.max, None, [P(bo)], [('rd', ri)])
        RECIP(rd, rd, [('rd', ri)], [('rd', ri)])
        if coef is not None:
            TT('dve', rd, rd, coef, ALU.mult, [('rd', ri), coefres], [('rd', ri)])
        rb = rd.unsqueeze(2).to_broadcast([128, 4, 64])
        if not accumulate:
            TT('dve', dst, O[:, :, 0:64], rb, ALU.mult, [P(bo), ('rd', ri)], [dstres])
        else:
            TT('dve', tmp, O[:, :, 0:64], rb, ALU.mult, [P(bo), ('rd', ri)], ['ntmp'])
            TT('pool', dst, dst, tmp, ALU.add, ['ntmp', dstres], [dstres])
        return rd, ('rd', ri)

    def causal_pairs(qc, extra=None):
        pl = []
        for kt in range(4 * qc + 4):
            c0 = max(0, kt - 4 * qc) * 128
            adds = []
            if kt >= 4 * qc:
                adds.append((ident, negtri, 0, 128, ['cbf']))
            if extra is not None:
                adds += extra(qc, kt, c0)
            pl.append((kt, c0, 512, adds, None))
        return pl

    def store_chunk(n, qc, brbk, k):
        for hf in range(2):
            DMA('sp', brt_d[2 * n + hf, :, qc * 512:(qc + 1) * 512], brbk[:, hf, :], [('brb', k)], [('brt', 2 * n + hf, qc)], f'brt{k}')

    def finish_branch(n, obf, qc, zsT, brb):
        b = psT.next()
        pst = PS[b][:, :].bitcast(BF16)
        for jl in range(4):
            for hf in range(2):
                i = jl * 2 + hf
                TR(pst[:, i * 128:(i + 1) * 128], obf[:, jl, hf * 128:(hf + 1) * 128], ['obf', 'cbf'], [P(b)])
        sl = slice(qc * 512, (qc + 1) * 512)
        k = qc % len(brb)
        TT('dve', brb[k].rearrange("p h (j k) -> p h j k", j=4),
           pst.rearrange("p (j h k) -> p h j k", j=4, h=2),
           zsT[:, :, sl].rearrange("p h (j k) -> p h j k", j=4), ALU.mult, [P(b), 'zsT'], [('brb', k)])
        store_chunk(n, qc, brb[k], k)

    def zs_proj(ws, wres, c0, zsT):
        for ch in range(2):
            for tc in range(4):
                b = proj_fm(ws, wres, c0 + ch * 128, 128, tc)
                ACT(zsT[:, ch, tc * 512:(tc + 1) * 512], PS[b][:, :], AF.Silu, [P(b)], ['zsT'])

    def v_proj(ws, wres, c0, nh, vaug, vres):
        for t in range(NT):
            b = proj_tm(ws, wres, c0, nh * 64, t)
            CP('act', vaug[:, t, :, 0:64], PS[b][:, 0:nh * 64].rearrange("p (h d) -> p h d", h=nh), [P(b)], [vres])

    def r3(ap, c):
        return ap.rearrange("p (c t) -> p c t", c=c)

    for l in range(n_layers):
        lam_init = 0.8 - 0.6 * math.exp(-0.3 * l)
        AR.reset()
        hb0 = AR.get("hb0", D, BF16)
        hb1 = AR.get("hb1", D, BF16)
        junk = AR.get("junk", D, BF16)
        gbc = AR.get("gbc", D, F32)
        DMA('sp', gbc, normg_d[l:l + 1, :].partition_broadcast(128), (), ['gbc'], 'c1')
        rmsnorm_rows(xs, NT, lambda t: ('x', t), gbc, 'gbc', hT, 'hT', [hb0, hb1], junk)

        AR.reset()
        Xc = AR.get("Xc", 16 * 256, BF16).rearrange("p (s c) -> p s c", s=16)
        Yc = AR.get("Yc", 16 * 256, F32).rearrange("p (s c) -> p s c", s=16)
        brb = [r3(AR.get("brb0", 2 * 512, BF16), 2)]
        zsc = r3(AR.get("zsc", 2 * 512, BF16), 2)
        dbc = AR.get("dbc", 256, F32)
        bgl = AR.get("bgl", 4, F32)
        wgl = AR.get("wgl", 2 * 512, BF16).rearrange("p (c n) -> p c n", c=2)
        mark5 = AR.mark()
        s5k = AR.get("s5k", 17 + 16 + 128, F32)
        kidx, krev, iotac = s5k[:, 0:17], s5k[:, 17:33], s5k[:, 33:161]
        maskM = AR.get("maskM", 512, BF16).rearrange("p (h c) -> p h c", h=2)
        l1 = AR.get("l1", 24, F32)
        bT = AR.get("bT", 256, F32).rearrange("p (j r k) -> p j r k", j=8, r=2)
        cT = AR.get("cT", 256, F32).rearrange("p (j r k) -> p j r k", j=8, r=2)
        p1 = AR.get("p1", 16 * 8, F32).rearrange("p (s n) -> p s n", s=16)
        pti = AR.get("pti", 136, I32)
        PW = AR.get("PW", 12 * 136, F32).rearrange("p (s n) -> p s n", s=12)
        bb = AR.get("bb", 2 * 128, F32).rearrange("p (r j k) -> p r j k", r=2, j=8)
        tmpa = AR.get("tmpa", 256, F32)
        tmpb = AR.get("tmpb", 256, F32)
        mats2 = [AR.get(f"mats{i}", 8 * 256, BF16).rearrange("p (m k) -> p m k", m=8) for i in range(2)]
        Mg2 = [AR.get(f"Mg{i}", 2 * 512, BF16).rearrange("p (g h c) -> p g h c", g=2, h=2) for i in range(2)]
        Rtr2 = [AR.get(f"Rtr{i}", 4 * 128, BF16).rearrange("p (a c) -> p a c", a=4) for i in range(2)]
        Ut2 = [AR.get(f"Ut{i}", 4 * 128, BF16).rearrange("p (a c) -> p a c", a=4) for i in range(2)]
        Xg = AR.get("Xg", 2 * 256, BF16).rearrange("p (g k) -> p g k", g=2)
        scb = [AR.get("sc0", 11 * 128, F32).rearrange("p (s n) -> p s n", s=11)]
        sci2 = [AR.get("sci0", 128, I32)]
        Xp = AR.get("Xp", 2 * 128, BF16).rearrange("p (r c) -> p r c", r=2)
        Ysb = PW.rearrange("p s n -> p (s n)")[:, 6 * 136:6 * 136 + 256]
        tmpc = PW.rearrange("p s n -> p (s n)")[:, 8 * 136:8 * 136 + 256]
        tmpd = PW.rearrange("p s n -> p (s n)")[:, 8 * 136 + 256:8 * 136 + 512]

        DMA('sp', s5k, s5k_d[:, :], (), ['s5k'], 'c1')
        DMA('sp', maskM, s5mm_d[:, :].rearrange("p (h c) -> p h c", h=2), (), ['maskM'], 'c1')
        DMA('sp', dbc, s5dflat_d[l:l + 1, :].partition_broadcast(128), (), ['dbc'], 'c1')
        DMA('sp', l1, s5l1_d[l, :, :], (), ['l1'], 'c1')
        DMA('sp', bT, s5bT_d[l].rearrange("p (j r k) -> p j r k", j=8, r=2), (), ['bT'], 'c1')
        DMA('sp', cT, s5cT_d[l].rearrange("p (j r k) -> p j r k", j=8, r=2), (), ['cT'], 'c1')
        DMA('sp', bgl, bglu_d[l, :, :], (), ['bgl'], 'c1')
        DMA('pool', wgl, wglu_d[l].rearrange("(c p) n -> p c n", p=128), (), ['wgl'], 'c3')
        s, ws = get_w(('C', l), 512, w_rows(win_d[l], OFF_C, 512))
        MEMSET('pool', Xp[:, :, 0:1], 0.0, ['Xp'])
        prefetch_w(('AB1', l, 0), 1024, w_rows(win_d[l], OFF_A1, 1024))

        hT16 = hT.rearrange("p k (c s) -> p k c s", s=16)
        for sg in range(8):
            b = psA.next()
            for s2_ in range(2):
                st_ = 2 * sg + s2_
                for c in range(8):
                    MM(PS[b][:, s2_ * 256:(s2_ + 1) * 256], hT16[:, c, :, st_], ws[:, c, 0:256], c == 0, c == 7, [('w', s), 'hT'], [P(b)])
            CP('act', Xc[:, 2 * sg:2 * sg + 2, :], PS[b][:, :].rearrange("p (s c) -> p s c", s=2), [P(b)], ['Xc'])

        def frac_sincos(turns, n, sin_o, cos_o, ta, tb, ti, res):
            CP('dve', ti, turns, [res], [res])
            TT('dve', ta, turns, ti, ALU.subtract, [res], [res])
            ACT(sin_o, ta, AF.Sin, [res], [res], scale=TWO_PI)
            TS('dve', tb, turns, 0.25, None, ALU.add, None, [res], [res])
            CP('dve', ti, tb, [res], [res])
            TT('dve', ta, tb, ti, ALU.subtract, [res], [res])
            ACT(cos_o, ta, AF.Sin, [res], [res], scale=TWO_PI)

        R1 = 'p1'
        T8 = lambda i: p1[:, i, :]
        lr1, li1, ldt1 = l1[:, 0:8], l1[:, 8:16], l1[:, 16:24]
        ACT(T8(0), ldt1, AF.Exp, ['l1'], [R1])
        TT('dve', T8(1), lr1, T8(0), ALU.mult, ['l1', R1], [R1])
        TT('dve', T8(2), li1, T8(0), ALU.mult, ['l1', R1], [R1])
        TS('dve', T8(3), T8(2), 1.0 / (2 * math.pi), None, ALU.mult, None, [R1], [R1])
        CP('dve', pti[:, 0:8], T8(3), [R1], [R1])
        TT('dve', T8(4), T8(3), pti[:, 0:8], ALU.subtract, [R1], [R1])
        frac_sincos(T8(4), 8, T8(5), T8(6), T8(7), T8(8), pti[:, 0:8], R1)
        ACT(T8(7), T8(1), AF.Exp, [R1], [R1])
        TT('dve', T8(8), T8(7), T8(6), ALU.mult, [R1], [R1])
        TT('dve', T8(9), T8(7), T8(5), ALU.mult, [R1], [R1])
        TS('dve', T8(8), T8(8), -1.0, None, ALU.add, None, [R1], [R1])
        TT('dve', T8(10), lr1, lr1, ALU.mult, ['l1'], [R1])
        TT('dve', T8(11), li1, li1, ALU.mult, ['l1'], [R1])
        TT('dve', T8(10), T8(10), T8(11), ALU.add, [R1], [R1])
        RECIP(T8(10), T8(10), [R1], [R1])
        TT('dve', T8(11), T8(8), lr1, ALU.mult, ['l1', R1], [R1])
        TT('dve', T8(12), T8(9), li1, ALU.mult, ['l1', R1], [R1])
        TT('dve', T8(11), T8(11), T8(12), ALU.add, [R1], [R1])
        TT('dve', T8(11), T8(11), T8(10), ALU.mult, [R1], [R1])
        TT('dve', T8(12), T8(9), lr1, ALU.mult, ['l1', R1], [R1])
        TT('dve', T8(13), T8(8), li1, ALU.mult, ['l1', R1], [R1])
        TT('dve', T8(12), T8(12), T8(13), ALU.subtract, [R1], [R1])
        TT('dve', T8(12), T8(12), T8(10), ALU.mult, [R1], [R1])
        zre_b = T8(11).unsqueeze(2).to_broadcast([128, 8, 16])
        zim_b = T8(12).unsqueeze(2).to_broadcast([128, 8, 16])
        b3 = lambda ap: ap.rearrange("p (j k) -> p j k", j=8)
        TT('dve', b3(tmpa[:, 0:128]), zre_b, bT[:, :, 0, :], ALU.mult, [R1, 'bT'], ['tmpa'])
        TT('dve', b3(tmpb[:, 0:128]), zim_b, bT[:, :, 1, :], ALU.mult, [R1, 'bT'], ['tmpb'])
        TT('dve', bb[:, 0, :, :], b3(tmpa[:, 0:128]), b3(tmpb[:, 0:128]), ALU.subtract, ['tmpa', 'tmpb'], ['bb'])
        TT('dve', b3(tmpa[:, 0:128]), zre_b, bT[:, :, 1, :], ALU.mult, [R1, 'bT'], ['tmpa'])
        TT('dve', b3(tmpb[:, 0:128]), zim_b, bT[:, :, 0, :], ALU.mult, [R1, 'bT'], ['tmpb'])
        TT('dve', bb[:, 1, :, :], b3(tmpa[:, 0:128]), b3(tmpb[:, 0:128]), ALU.add, ['tmpa', 'tmpb'], ['bb'])
        TS('dve', T8(13), T8(4), 16.0, None, ALU.mult, None, [R1], [R1])
        CP('dve', pti[:, 0:8], T8(13), [R1], [R1])
        TT('dve', T8(14), T8(13), pti[:, 0:8], ALU.subtract, [R1], [R1])
        ACT(T8(15), T8(1), AF.Exp, [R1], [R1], scale=16.0)
        RP = 'PW'
        W17 = lambda i: PW[:, i, :].rearrange("p (j k) -> p j k", j=8)
        W16 = lambda i: PW[:, i, 0:128].rearrange("p (j k) -> p j k", j=8)
        kk17 = kidx.unsqueeze(1).to_broadcast([128, 8, 17])
        kk16r = krev.unsqueeze(1).to_broadcast([128, 8, 16])
        lm17 = T8(1).unsqueeze(2).to_broadcast([128, 8, 17])
        lm16 = T8(1).unsqueeze(2).to_broadcast([128, 8, 16])
        ph17 = T8(4).unsqueeze(2).to_broadcast([128, 8, 17])
        ph16 = T8(4).unsqueeze(2).to_broadcast([128, 8, 16])
        TT('dve', W17(4), kk17, lm17, ALU.mult, ['s5k', R1], [RP])
        ACT(PW[:, 5, :], PW[:, 4, :], AF.Exp, [RP], [RP])
        ACT(PW[:, 6, :], PW[:, 4, :], AF.Exp, [RP], [RP], scale=-1.0)
        TT('dve', W17(7), kk17, ph17, ALU.mult, ['s5k', R1], [RP])
        frac_sincos(PW[:, 7, :], 136, PW[:, 8, :], PW[:, 9, :], PW[:, 10, :], PW[:, 11, :], pti, RP)
        TT('dve', PW[:, 0, :], PW[:, 5, :], PW[:, 9, :], ALU.mult, [RP], [RP])
        TT('dve', PW[:, 1, :], PW[:, 5, :], PW[:, 8, :], ALU.mult, [RP], [RP])
        TT('dve', PW[:, 2, :], PW[:, 6, :], PW[:, 9, :], ALU.mult, [RP], [RP])
        STT(PW[:, 3, :], PW[:, 8, :], -1.0, PW[:, 6, :], ALU.mult, ALU.mult, [RP], [RP])
        TT('dve', W16(10), kk16r, lm16, ALU.mult, ['s5k', R1], [RP])
        ACT(PW[:, 11, 0:128], PW[:, 10, 0:128], AF.Exp, [RP], [RP])
        TT('dve', W16(10), kk16r, ph16, ALU.mult, ['s5k', R1], [RP])
        frac_sincos(PW[:, 10, 0:128], 128, PW[:, 6, 0:128], PW[:, 7, 0:128], PW[:, 8, 0:128], PW[:, 9, 0:128], pti[:, 0:128], RP)
        TT('dve', PW[:, 4, 0:128], PW[:, 11, 0:128], PW[:, 7, 0:128], ALU.mult, [RP], [RP])
        TT('dve', PW[:, 5, 0:128], PW[:, 11, 0:128], PW[:, 6, 0:128], ALU.mult, [RP], [RP])

        def outer(eng, dst, tab, vec, W):
            TT(eng, dst.rearrange("p (a b) -> p a b", a=16), tab.unsqueeze(2).to_broadcast([128, 16, 16]),
               vec.unsqueeze(1).to_broadcast([128, 16, 16]), ALU.mult, [RP, 'bb', 'cT'], W)

        def s5_stage1(j):
            pj = j % 2
            mats, Mg, Rtr, Ut = mats2[pj], Mg2[pj], Rtr2[pj], Ut2[pj]
            Pre_j, Pim_j = W17(0)[:, j, :], W17(1)[:, j, :]
            PIre_j, PIim_j = W17(2)[:, j, 0:16], W17(3)[:, j, 0:16]
            PRre_j, PRim_j = W16(4)[:, j, :], W16(5)[:, j, :]
            bre_j, bim_j = bb[:, 0, j, :], bb[:, 1, j, :]
            cre_j, cim_j = cT[:, j, 0, :], cT[:, j, 1, :]
            specs = [(PIre_j, PIim_j, bre_j, bim_j, 0, 1, False),
                     (Pre_j[:, 0:16], Pim_j[:, 0:16], cre_j, cim_j, 2, 3, True),
                     (PRre_j, PRim_j, bre_j, bim_j, 4, 5, False),
                     (Pre_j[:, 1:17], Pim_j[:, 1:17], cre_j, cim_j, 6, 7, True)]
            for (tr_, ti_, vr_, vi_, o_re, o_im, neg) in specs:
                if not neg:
                    outer('dve', tmpa, tr_, vr_, ['tmpa'])
                    outer('dve', tmpb, ti_, vi_, ['tmpb'])
                    TT('dve', mats[:, o_re, :], tmpa, tmpb, ALU.subtract, ['tmpa', 'tmpb'], [('mats', o_re, pj)])
                    outer('pool', tmpc, tr_, vi_, ['tmpc'])
                    outer('pool', tmpd, ti_, vr_, ['tmpd'])
                    TT('pool', mats[:, o_im, :], tmpc, tmpd, ALU.add, ['tmpc', 'tmpd'], [('mats', o_im, pj)])
                else:
                    outer('pool', tmpc, tr_, vr_, ['tmpc'])
                    outer('pool', tmpd, ti_, vi_, ['tmpd'])
                    TT('pool', mats[:, o_re, :], tmpc, tmpd, ALU.subtract, ['tmpc', 'tmpd'], [('mats', o_re, pj)])
                    outer('dve', tmpa, tr_, vi_, ['tmpa'])
                    outer('dve', tmpb, ti_, vr_, ['tmpb'])
                    STT(mats[:, o_im, :], tmpa, -1.0, tmpb, ALU.mult, ALU.subtract, ['tmpa', 'tmpb'], [('mats', o_im, pj)])
            for g2 in range(2):
                rows = slice(g2 * 64, (g2 + 1) * 64)
                for sh in range(2):
                    b = psA.next()
                    MM(PS[b][:, 0:256], mats[rows, 0, sh * 128:(sh + 1) * 128], mats[rows, 2, :], True, False,
                       [('mats', 0, pj), ('mats', 2, pj)], [P(b)])
                    MM(PS[b][:, 0:256], mats[rows, 1, sh * 128:(sh + 1) * 128], mats[rows, 3, :], False, True,
                       [('mats', 1, pj), ('mats', 3, pj)], [P(b)])
                    TT('dve', Mg[:, g2, sh, :], PS[b][:, 0:256], maskM[:, sh, :], ALU.mult, [P(b), 'maskM'], [('Mg', g2, pj)])
            b = psC.next()
            pst = PS[b][:, :].bitcast(BF16)
            for ri in range(2):
                for sh in range(2):
                    a_ = ri * 2 + sh
                    TR(pst[:, a_ * 128:(a_ + 1) * 128], mats[:, 4 + ri, sh * 128:(sh + 1) * 128], [('mats', 4 + ri, pj), 'cbf'], [P(b)])
            CP('act', Rtr, pst[:, 0:512].rearrange("p (a c) -> p a c", a=4), [P(b)], [('Rtr', pj)])
            b = psC.next()
            pst = PS[b][:, :].bitcast(BF16)
            for g2 in range(2):
                ch0 = (2 * j + g2) * 16
                CP('pool', Xg[:, g2, :].rearrange("p (s k) -> p s k", s=16), Xc[:, :, ch0:ch0 + 16], ['Xc'], ['Xg'])
                for sh in range(2):
                    a_ = g2 * 2 + sh
                    TR(pst[:, a_ * 128:(a_ + 1) * 128], Xg[:, g2, sh * 128:(sh + 1) * 128], ['Xg', 'cbf'], [P(b)])
            CP('act', Ut, pst[:, 0:512].rearrange("p (a c) -> p a c", a=4), [P(b)], [('Ut', pj)])

        def s5_stage2(j):
            pj = j % 2
            mats, Mg, Rtr, Ut = mats2[pj], Mg2[pj], Rtr2[pj], Ut2[pj]
            bw = psA.next()
            for ri in range(2):
                for g2 in range(2):
                    for sh in range(2):
                        MM(PS[bw][g2 * 64:(g2 + 1) * 64, ri * 128:(ri + 1) * 128], Rtr[:, ri * 2 + sh, g2 * 64:(g2 + 1) * 64],
                           Ut[:, g2 * 2 + sh, :], sh == 0, sh == 1, [('Rtr', pj), ('Ut', pj)], [P(bw)])
            pj2 = 0
            SCb = scb[pj2]
            SC = lambda i: SCb[:, i, :]
            r = lambda i: ('sc', i, pj2)
            sci_, rsi = sci2[pj2], ('sci', pj2)
            TS('dve', SC(0), iotac, T8(14)[:, j:j + 1], None, ALU.mult, None, ['s5k', R1], [r(0)])
            CP('dve', sci_, SC(0), [r(0)], [rsi])
            TT('dve', SC(3), SC(0), sci_, ALU.subtract, [r(0), rsi], [r(3)])
            ACT(SC(1), SC(3), AF.Sin, [r(3)], [r(1)], scale=TWO_PI)
            TS('dve', SC(4), SC(0), 0.25, None, ALU.add, None, [r(0)], [r(4)])
            CP('dve', sci_, SC(4), [r(4)], [rsi])
            TT('dve', SC(5), SC(4), sci_, ALU.subtract, [r(4), rsi], [r(5)])
            ACT(SC(2), SC(5), AF.Sin, [r(5)], [r(2)], scale=TWO_PI)
            Wre, Wim = PS[bw][:, 0:128], PS[bw][:, 128:256]
            TT('dve', SC(3), Wre, SC(2), ALU.mult, [P(bw), r(2)], [r(3)])
            TT('dve', SC(4), Wim, SC(1), ALU.mult, [P(bw), r(1)], [r(4)])
            TT('dve', SC(5), Wim, SC(2), ALU.mult, [P(bw), r(2)], [r(5)])
            TT('dve', SC(6), Wre, SC(1), ALU.mult, [P(bw), r(1)], [r(6)])
            TT('pool', SC(7), SC(3), SC(4), ALU.add, [r(3), r(4)], [r(7)])
            TT('pool', SC(8), SC(5), SC(6), ALU.subtract, [r(5), r(6)], [r(8)])
            magA = T8(15)[:, j:j + 1].to_broadcast([128, 128])
            S.op('dve', lambda e, magA=magA, o=SC(9), i=SC(7): e.tensor_tensor_scan(o, magA, i, 0.0, ALU.mult, ALU.add), [r(7), R1], [r(9)])
            S.op('dve', lambda e, magA=magA, o=SC(10), i=SC(8): e.tensor_tensor_scan(o, magA, i, 0.0, ALU.mult, ALU.add), [r(8), R1], [r(10)])
            TT('dve', SC(3), SC(9), SC(2), ALU.mult, [r(9), r(2)], [r(3)])
            TT('pool', SC(4), SC(10), SC(1), ALU.mult, [r(10), r(1)], [r(4)])
            TT('dve', Xp[:, 0, 1:128], SC(3)[:, 0:127], SC(4)[:, 0:127], ALU.subtract, [r(3), r(4)], ['Xp'])
            TT('dve', SC(5), SC(9), SC(1), ALU.mult, [r(9), r(1)], [r(5)])
            TT('pool', SC(6), SC(10), SC(2), ALU.mult, [r(10), r(2)], [r(6)])
            TT('dve', Xp[:, 1, 1:128], SC(5)[:, 0:127], SC(6)[:, 0:127], ALU.add, [r(5), r(6)], ['Xp'])
            for g2 in range(2):
                rows = slice(g2 * 64, (g2 + 1) * 64)
                g = 2 * j + g2
                by = psB.next()
                for th in range(2):
                    osl = PS[by][:, th * 128:(th + 1) * 128]
                    csl = slice(th * 128, (th + 1) * 128)
                    MM(osl, Mg[:, g2, 0, csl], Ut[:, g2 * 2 + 0, :], True, False, [('Mg', g2, pj), ('Ut', pj)], [P(by)])
                    if th == 1:
                        MM(osl, Mg[:, g2, 1, csl], Ut[:, g2 * 2 + 1, :], False, False, [('Mg', g2, pj), ('Ut', pj)], [P(by)])
                    MM(osl, mats[rows, 6, csl], Xp[rows, 0, :], False, False, [('mats', 6, pj), 'Xp'], [P(by)])
                    MM(osl, mats[rows, 7, csl], Xp[rows, 1, :], False, True, [('mats', 7, pj), 'Xp'], [P(by)])
                CP('act', Ysb, PS[by][:, 0:256], [P(by)], ['Ysb'])
                bt = psD.next()
                for th in range(2):
                    TRF(PS[bt][:, th * 128:(th + 1) * 128], Ysb[:, th * 128:(th + 1) * 128], ['Ysb', 'identf'], [P(bt)])
                CP('dve', Yc[:, :, g * 16:(g + 1) * 16], PS[bt][:, 0:256].rearrange("p (s k) -> p s k", s=16), [P(bt)], [('Yc', g)])
        s5_stage1(0)
        for j in range(8):
            if j + 1 < 8:
                s5_stage1(j + 1)
            s5_stage2(j)
        AR.release_to(mark5)
        gyT = r3(AR.get("gyT", 2 * S_LEN, BF16), 2)
        gqs = [[AR.get(f"gq{p_}_{i}", 512, F32) for i in range(2)] for p_ in range(2)]
        gq = gqs[0]
        allY = [('Yc', g) for g in range(16)]
        for qd in range(8):
            ssl = slice(2 * qd, 2 * qd + 2)
            g0, g1_ = gqs[qd % 2]
            rg0, rg1 = ('gq', 0, qd % 2), ('gq', 1, qd % 2)
            v3 = lambda ap: ap.rearrange("p (s c) -> p s c", s=2)
            TT('dve', v3(g0), Xc[:, ssl, :], dbc.unsqueeze(1).to_broadcast([128, 2, 256]), ALU.mult, ['Xc', 'dbc'], [rg0])
            TT('pool', Yc[:, ssl, :], Yc[:, ssl, :], v3(g0), ALU.add, [rg0] + allY, [('Yq', qd)])
            yq = Yc[:, ssl, :]
            TT('dve', v3(g0), yq, yq, ALU.mult, [('Yq', qd)], [rg0])
            TS('dve', g0, g0, 0.044715, 1.0, ALU.mult, ALU.add, [rg0], [rg0])
            TT('pool', v3(g1_), v3(g0), yq, ALU.mult, [rg0, ('Yq', qd)], [rg1])
            ACT(g1_, g1_, AF.Sigmoid, [rg1], [rg1], scale=1.5957691216)
            TT('dve', Xc[:, ssl, :], v3(g1_), yq, ALU.mult, [rg1, ('Yq', qd)], ['Xc'])
        gy16 = gyT.rearrange("p h (c s) -> p h s c", s=16)
        for hf in range(2):
            for sg in range(2):
                b = psC.next()
                pst = PS[b][:, :].bitcast(BF16)
                for s8 in range(8):
                    TR(pst[:, s8 * 128:(s8 + 1) * 128], Xc[:, sg * 8 + s8, hf * 128:(hf + 1) * 128], ['Xc', 'cbf'], [P(b)])
                CP('act', gy16[:, hf, sg * 8:(sg + 1) * 8, :], pst.rearrange("p (s c) -> p s c", s=8), [P(b)], ['gyT'])
        for cc in range(4):
            csl = slice(cc * 512, (cc + 1) * 512)
            for ch in range(2):
                b = proj_fm(ws, ('w', s), 256 + ch * 128, 128, cc)
                ACT(zsc[:, ch, :], PS[b][:, :], AF.Silu, [P(b)], ['zsc'])
            for ch in range(2):
                bv = psA.next()
                bg = psB.next()
                ga, gb = gq[0][:, 0:512], gq[1][:, 0:512]
                for c in range(2):
                    MM(PS[bv][:, :], wgl[:, c, ch * 128:(ch + 1) * 128], gyT[:, c, csl], c == 0, c == 1, ['wgl', 'gyT'], [P(bv)])
                for c in range(2):
                    MM(PS[bg][:, :], wgl[:, c, 256 + ch * 128:256 + (ch + 1) * 128], gyT[:, c, csl], c == 0, c == 1, ['wgl', 'gyT'], [P(bg)])
                ACT(ga, PS[bv][:, :], AF.Identity, [P(bv), 'bgl'], [('gq', 0, 0)], bias=bgl[:, ch:ch + 1])
                ACT(gb, PS[bg][:, :], AF.Sigmoid, [P(bg), 'bgl'], [('gq', 1, 0)], bias=bgl[:, 2 + ch:3 + ch])
                TT('pool', ga, ga, gb, ALU.mult, [('gq', 0, 0), ('gq', 1, 0)], [('gq', 0, 0)])
                TT('dve', brb[0][:, ch, :], ga, zsc[:, ch, :], ALU.mult, [('gq', 0, 0), 'zsc'], [('brb', 0)])
            store_chunk(2, cc, brb[0], 0)
        if debug == 'C':
            break

        for br in range(2):
            AR.reset()
            nq = 3 if br == 0 else 2
            qT = r3(AR.get("qT", nq * S_LEN, BF16), nq)
            kT = r3(AR.get("kT", (3 if br == 0 else 4) * S_LEN, BF16), 3 if br == 0 else 4)
            VW = 128
            vaug = AR.get("vaug", NT * 4 * VW, BF16).rearrange("p (t h w) -> p t h w", t=NT, h=4)
            MEMSET('pool', kT[:, :, :], 0.0, ['kT'])
            MEMSET('pool', vaug[:, :, :, 65:128], 0.0, ['vaug'])
            zsT = r3(AR.get("zsT", 2 * S_LEN, BF16), 2)
            sgb = AR.get("sgb", 64, F32)
            dl = AR.get("dl", 128, F32)
            ssq = AR.get("ssq", 16, F32)
            mark = AR.mark()
            ropeC = AR.get("ropeC", S_LEN, F32)
            ropeS = AR.get("ropeS", S_LEN, F32)
            t1 = AR.get("t1", 512, F32)
            t2 = AR.get("t2", 512, F32)
            qa = [AR.get("qa0", 512, BF16)]
            permAB = perm32 if br == 0 else perm64
            DMA('sp', ropeC, rope_d[2 * br, :, :], (), ['ropeC'], 'c2')
            DMA('sp', ropeS, rope_d[2 * br + 1, :, :], (), ['ropeS'], 'c2')
            MEMSET('pool', vaug[:, :, :, 64:65], 1.0, ['vaug'])
            off1, off2 = (OFF_A1, OFF_A2) if br == 0 else (OFF_B1, OFF_B2)
            s, ws = get_w(('AB1', l, br), 1024, w_rows(win_d[l], off1, 1024))
            s2, ws2 = get_w(('AB2', l, br), 512, w_rows(win_d[l], off2, 512))
            if br == 0:
                for ti in range(3):
                    m = 96 if ti < 2 else 64
                    rope_proj(ws, ('w', s), ti * 96, 256 + ti * 96, qT[:, ti, :], 'qT', ropeC, ropeS, t1, t2, m=m, perm=permAB, qa=qa)
                    rope_proj(ws, ('w', s), 512 + ti * 96, 768 + ti * 96, kT[:, ti, :], 'kT', ropeC, ropeS, t1, t2, m=m, perm=permAB, qa=qa)
            else:
                for ch in range(2):
                    rope_proj(ws, ('w', s), ch * 128, 256 + ch * 128, qT[:, ch, :], 'qT', ropeC, ropeS, t1, t2, perm=permAB, qa=qa)
                    rope_proj(ws, ('w', s), 512 + ch * 128, 768 + ch * 128, (kT[:, 2 * ch, :], kT[:, 2 * ch + 1, :]), 'kT',
                              ropeC, ropeS, t1, t2, perm=permAB, qa=qa)
            v_proj(ws2, ('w', s2), 0, 4, vaug, 'vaug')
            zs_proj(ws2, ('w', s2), 256, zsT)
            AR.release_to(mark)
            if br == 0:
                prefetch_w(('AB1', l, 1), 1024, w_rows(win_d[l], OFF_B1, 1024))
                prefetch_w(('AB2', l, 1), 512, w_rows(win_d[l], OFF_B2, 512))
            else:
                prefetch_w(('D1', l), 512, w_rows(win_d[l], OFF_D1, 512))
                prefetch_w(('D2', l), 768, w_rows(win_d[l], OFF_D2, 768))
            brb = [r3(AR.get(f"brb{i}", 2 * 512, BF16), 2) for i in range(2 if br == 1 else 1)]
            Ebufs = [AR.get(f"E{i}", 512, BF16) for i in range(3)]
            OTs = [AR.get(f"ots{i}", 512, F32) for i in range(2 if br == 1 else 1)]
            obf = AR.get("obf", 4 * 256, BF16).rearrange("p (j c) -> p j c", j=4)
            if br == 0:
                tmpA = AR.get("tmpA", 4 * 8 * 64, F32).rearrange("p (j v d) -> p j v d", j=4, v=8)
                ocomb = AR.get("ocomb", 4 * 256, F32)
                osq = tmpA.rearrange("p j v d -> p (j v d)")[:, 0:1024]
                qpad = [AR.get(f"qpad{i}", 512, BF16) for i in range(3)]
                for i_ in range(3):
                    MEMSET('pool', qpad[i_], 0.0, [('qpad', i_)])
                DMA('sp', sgb, sublng_d[l:l + 1, :].partition_broadcast(128), (), ['sgb'], 'c1')
                DMA('sp', dl, dlam_d[l:l + 1, :].partition_broadcast(128), (), ['dl'], 'c1')
                TT('dve', dl[:, 0:32], dl[:, 0:32], dl[:, 32:64], ALU.mult, ['dl'], ['dl'])
                TT('dve', dl[:, 64:96], dl[:, 64:96], dl[:, 96:128], ALU.mult, ['dl'], ['dl'])
                S.op('dve', lambda e, dl=dl: e.tensor_reduce(lamt[:, 0:1], dl[:, 0:32], AX.X, ALU.add), ['dl'], ['lamt'])
                S.op('dve', lambda e, dl=dl: e.tensor_reduce(lamt[:, 1:2], dl[:, 64:96], AX.X, ALU.add), ['dl'], ['lamt'])
                ACT(lamt[:, 0:2], lamt[:, 0:2], AF.Exp, ['lamt'], ['lamt'])
                TT('dve', lamt[:, 2:3], lamt[:, 0:1], lamt[:, 1:2], ALU.subtract, ['lamt'], ['lamt'])
                TS('dve', lamt[:, 3:4], lamt[:, 2:3], lam_init, -1.0, ALU.add, ALU.mult, ['lamt'], ['lamt'])

                def prepA(vh, qc, qT=qT, qpad=qpad):
                    pos = vh % 3
                    CP('pool', qpad[pos][pos * 32:pos * 32 + 32, :], qT[pos * 32:pos * 32 + 32, vh // 3, qc * 512:(qc + 1) * 512],
                       ['qT'], [('qpad', pos)])

                def qfA(vh, qc, qpad=qpad):
                    return qpad[vh % 3], ('qpad', vh % 3), qc * 512

                def kfA(vh, kT=kT):
                    return kT[:, vh // 3, :], 'kT'

                def vfA(vh, kt, vaug=vaug):
                    return vaug[:, kt, vh // 2, :], 'vaug'

                def outA(vh, qc, bo, tmpA=tmpA):
                    norm_out(bo, 65, tmpA[:, :, vh, :], ('tmpA', vh))

                def postA(qc, tmpA=tmpA, ocomb=ocomb, osq=osq, obf=obf, zsT=zsT, brb=brb, ssq=ssq, sgb=sgb, lam_init=lam_init):
                    tv = tmpA.rearrange("p j (h c) d -> p j h c d", c=2)
                    oc = ocomb.rearrange("p (j h d) -> p j h d", j=4, h=4)
                    allt = [('tmpA', v) for v in range(8)]
                    for j in range(4):
                        STT(oc[:, j], tv[:, j, :, 1, :], lamt[:, 3:4], tv[:, j, :, 0, :], ALU.mult, ALU.add, allt + ['lamt'], ['ocomb'])
                    TT('pool', osq, ocomb, ocomb, ALU.mult, ['ocomb'], allt)
                    S.op('dve', lambda e: e.tensor_reduce(ssq, osq.rearrange("p (g d) -> p g d", d=64), AX.X, ALU.add), allt, ['ssq'])
                    TS('dve', ssq, ssq, 1.0 / 64, EPS, ALU.mult, ALU.add, ['ssq'], ['ssq'])
                    ACT(ssq, ssq, AF.Sqrt, ['ssq'], ['ssq'])
                    RECIP(ssq, ssq, ['ssq'], ['ssq'])
                    o3 = ocomb.rearrange("p (g d) -> p g d", d=64)
                    TT('dve', o3, o3, ssq.unsqueeze(2).to_broadcast([128, 16, 64]), ALU.mult, ['ocomb', 'ssq'], ['ocomb'])
                    STT(obf.rearrange("p j (h d) -> p (j h) d", d=64), o3, 1.0 - lam_init,
                        sgb.unsqueeze(1).to_broadcast([128, 16, 64]), ALU.mult, ALU.mult, ['ocomb', 'sgb'], ['obf'])
                    finish_branch(0, obf, qc, zsT, brb)

                attention(8, qfA, kfA, vfA, 32 ** -0.5, causal_pairs, outA, Ebufs, OTs, post=postA, vwm=128,
                          otrot=Rot([0]), prep=prepA)
            else:
                wstrip = AR.get("wstrip", 2048, BF16)
                tmpB = AR.get("tmpB", 4 * 256, F32).rearrange("p (j h d) -> p j h d", j=4, h=4)
                DMA('sp', wstrip, wstrip_d[:, :], (), ['wstrip'], 'c1')

                def qfB(h, qT=qT):
                    return qT[:, h // 2, :], 'qT'

                def kfB(h, kT=kT):
                    return kT[:, h, :], 'kT'

                def vfB(h, kt, vaug=vaug):
                    return vaug[:, kt, h, :], 'vaug'

                def pairsB(qc, wstrip=wstrip):
                    pl = []
                    for kt in range(4 * qc + 4):
                        c0 = max(0, kt - 4 * qc) * 128
                        x0 = qc * 512 + c0 - kt * 128
                        pl.append((kt, c0, 512, [], (wstrip[:, x0:x0 + 512 - c0], 'wstrip')))
                    return pl

                def outB(h, qc, bo, tmpB=tmpB):
                    norm_out(bo, 65, tmpB[:, :, h, :], 'tmpB')

                def postB(qc, tmpB=tmpB, obf=obf, zsT=zsT, brb=brb):
                    CP('act', obf, tmpB.rearrange("p j h d -> p j (h d)"), ['tmpB'], ['obf'])
                    finish_branch(1, obf, qc, zsT, brb)

                attention(4, qfB, kfB, vfB, 64 ** -0.5, pairsB, outB, Ebufs, OTs, post=postB, vwm=128)

        AR.reset()
        zsT = r3(AR.get("zsT", 2 * S_LEN, BF16), 2)
        brb = [r3(AR.get("brb0", 2 * 512, BF16), 2)]
        qT = r3(AR.get("qT", 2 * S_LEN, BF16), 2)
        qrT = r3(AR.get("qrT", 4 * S_LEN, BF16), 4)
        ksT = AR.get("ksT", S_LEN, BF16)
        kwT = AR.get("kwT", S_LEN, BF16)
        vsw = AR.get("vsw", NT * 2 * 128, BF16).rearrange("p (t h w) -> p t h w", t=NT, h=2)
        gts = AR.get("gts", NT * 12, F32).rearrange("p (t g) -> p t g", t=NT)
        kcmpT = AR.get("kcmpT", 128, BF16)
        vca = AR.get("vca", 97, BF16)
        mark = AR.mark()
        ropeC = AR.get("ropeC", S_LEN, F32)
        ropeS = AR.get("ropeS", S_LEN, F32)
        t1 = AR.get("t1", 512, F32)
        t2 = AR.get("t2", 512, F32)
        qa = [AR.get("qa0", 512, BF16)]
        DMA('sp', ropeC, rope_d[2, :, :], (), ['ropeC'], 'c2')
        DMA('sp', ropeS, rope_d[3, :, :], (), ['ropeS'], 'c2')
        MEMSET('pool', vsw[:, :, :, 65:128], 0.0, ['vsw'])
        MEMSET('pool', vsw[:, :, :, 64:65], 1.0, ['vsw'])
        MEMSET('pool', qrT[:, :, :], 0.0, ['qrT'])
        CP('dve', vca[:, 64:97], ovl, ['cbf'], ['vca'])
        s, ws = get_w(('D1', l), 512, w_rows(win_d[l], OFF_D1, 512))
        s2, ws2 = get_w(('D2', l), 768, w_rows(win_d[l], OFF_D2, 768))
        for ch in range(2):
            for tc in range(4):
                b = proj_fm(ws, ('w', s), ch * 128, 128, tc)
                CP('act', qT[:, ch, tc * 512:(tc + 1) * 512], PS[b][:, :], [P(b)], ['qT'])
            rope_proj(ws, ('w', s), ch * 128, 256 + ch * 128, (qrT[:, 2 * ch, :], qrT[:, 2 * ch + 1, :]), 'qrT', ropeC, ropeS, t1, t2, perm=perm64, qa=qa)
        rope_proj(ws2, ('w', s2), 128, 256, ksT, 'ksT', ropeC, ropeS, t1, t2, perm=perm64, qa=qa)
        rope_proj(ws2, ('w', s2), 384, 512, kwT, 'kwT', ropeC, ropeS, t1, t2, perm=perm64, qa=qa)
        for t in range(NT):
            b = proj_tm(ws2, ('w', s2), 640, 128, t)
            CP('act', vsw[:, t, :, 0:64], PS[b][:, 0:128].rearrange("p (h d) -> p h d", h=2), [P(b)], ['vsw'])
        AR.release_to(mark)
        kvA = AR.get("kvA", S_LEN + 32, BF16)
        kvB = AR.get("kvB", S_LEN + 32, BF16)
        peT = AR.get("peT", 32, F32)
        hidT = AR.get("hidT", 4 * 128, BF16).rearrange("p (a n) -> p a n", a=4)
        gh1 = AR.get("gh1", 128, F32)
        gh2 = AR.get("gh2", 128, F32)
        gh3 = AR.get("gh3", 128, F32)
        w2all = AR.get("w2", 384, BF16)
        w2k = w2all[:, 0:256].rearrange("p (a d) -> p a d", a=2)
        w2v = w2all[:, 256:384].rearrange("p (a d) -> p a d", a=2)
        DMA('sp', peT, nsape_d[l, :, :], (), ['peT'], 'c1')
        DMA('pool', w2all, nsaw2_d[l], (), ['w2'], 'c3')
        for tc in range(4):
            b = proj_fm(ws2, ('w', s2), 0, 128, tc)
            sl = slice(tc * 512, (tc + 1) * 512)
            TT('dve', kvA[:, sl].rearrange("p (g r) -> p g r", r=16), PS[b][:, :].rearrange("p (g r) -> p g r", r=16),
               peT[:, 0:16].unsqueeze(1).to_broadcast([128, 32, 16]), ALU.add, [P(b), 'peT'], ['kvA'])
            TT('dve', kvB[:, sl].rearrange("p (g r) -> p g r", r=16), PS[b][:, :].rearrange("p (g r) -> p g r", r=16),
               peT[:, 16:32].unsqueeze(1).to_broadcast([128, 32, 16]), ALU.add, [P(b), 'peT'], ['kvB'])
        s3, ws3 = load_w(268, w_rows(win_d[l], OFF_D3, 268))
        zs_proj(ws3, ('w', s3), 0, zsT)
        for t in range(NT):
            b = proj_tm(ws3, ('w', s3), 256, 12, t)
            ACT(gts[:, t, :], PS[b][:, 0:12], AF.Sigmoid, [P(b)], ['gts'])
        sw = []
        for kv in range(2):
            sl_ = wrot.next()
            rows = slice(kv * 64, kv * 64 + 64)
            DMA('pool', wbuf[sl_][rows, 0:8192], nsaw1_d[l, kv], (), [('w', sl_)], f'w{sl_}')
            sw.append(sl_)
        for kv in range(2):
            rows = slice(kv * 64, kv * 64 + 64)
            w1v = wbuf[sw[kv]][:, 0:8192].rearrange("p (j h) -> p j h", j=32)
            for hc in range(2):
                b = psA.next()
                for j in range(32):
                    srcT = kvA if j < 16 else kvB
                    srcv = srcT[rows, j:j + 2032].rearrange("p (n r) -> p n r", r=16)[:, :, 0]
                    MM(PS[b][:, 0:127], w1v[rows, j, hc * 128:(hc + 1) * 128], srcv, j == 0, j == 31,
                       [('w', sw[kv]), 'kvA', 'kvB'], [P(b)])
                CP('act', gh3[:, 0:127], PS[b][:, 0:127], [P(b)], ['gh3'])
                TT('dve', gh1[:, 0:127], gh3[:, 0:127], gh3[:, 0:127], ALU.mult, ['gh3'], ['gh1'])
                TS('dve', gh1[:, 0:127], gh1[:, 0:127], 0.044715, 1.0, ALU.mult, ALU.add, ['gh1'], ['gh1'])
                TT('dve', gh2[:, 0:127], gh1[:, 0:127], gh3[:, 0:127], ALU.mult, ['gh1', 'gh3'], ['gh2'])
                ACT(gh2[:, 0:127], gh2[:, 0:127], AF.Sigmoid, ['gh2'], ['gh2'], scale=1.5957691216)
                TT('dve', hidT[:, kv * 2 + hc, 0:127], gh2[:, 0:127], gh3[:, 0:127], ALU.mult, ['gh2', 'gh3'], ['hidT'])
        b = psA.next()
        for c in range(2):
            MM(PS[b][:, 0:127], w2k[:, c, :], hidT[:, c, 0:127], c == 0, c == 1, ['w2', 'hidT'], [P(b)])
        CP('act', kcmpT[:, 0:127], PS[b][:, 0:127], [P(b)], ['kcmpT'])
        b = psA.next()
        for c in range(2):
            MM(PS[b][0:127, 0:64], hidT[:, 2 + c, 0:127], w2v[:, c, :], c == 0, c == 1, ['w2', 'hidT'], [P(b)])
        CP('act', vca[0:127, 0:64], PS[b][0:127, 0:64], [P(b)], ['vca'])

        AR.release_to(mark)
        Ebufs = [AR.get(f"E{i}", 512, BF16) for i in range(2)]
        OTs = [AR.get("ots0", 512, F32)]
        erotD, otrotD = Rot([0, 1]), Rot([0])
        oacc = AR.get("oacc", 4 * 256, F32).rearrange("p (j h d) -> p j h d", j=4, h=4)
        ntmp = AR.get("ntmp", 4 * 64, F32).rearrange("p (j d) -> p j d", j=4)
        ntmp2 = AR.get("ntmp2", 4 * 32, F32).rearrange("p (j k) -> p j k", j=4)
        obf = AR.get("obf", 4 * 256, BF16).rearrange("p (j c) -> p j c", j=4)
        MnegT = AR.get("MnegT", 1024, BF16)
        negcmp = AR.get("negcmp", 2048, BF16)
        selc = AR.get("selc", 768, F32).rearrange("p (k t j) -> p k t j", k=3, t=8)
        imp = AR.get("imp", 4 * 32, F32).rearrange("p (j k) -> p j k", j=4)
        impm = AR.get("impm", 32, F32)
        impm2 = AR.get("impm2", 32, F32)
        mx8 = AR.get("mx8", 16, F32)
        selm = AR.get("selm", 32, F32)
        selb = AR.get("selb", 32, BF16)
        prefetch_w(('E1', l), 512, w_rows(win_d[l], OFF_E, 512))
        prefetch_w(('E2', l), 512, w_rows(wmemkv_d[l], 0, 512))
        MEMSET('pool', MnegT, 0.0, ['MnegT'])
        DMA('sp', negcmp, negcmp_d[:, :], (), ['negcmp'], 'c1')
        DMA('sp', selc, selc_d[:, :].rearrange("p (k t j) -> p k t j", k=3, t=8), (), ['selc'], 'c1')

        def qfD(h):
            pb = (h % 2) * 64
            return qT[pb:pb + 64, h // 2, :], 'qT'

        def qfDr(h):
            return qrT[:, h, :], 'qrT'

        def hrows(h):
            return slice((h % 2) * 64, (h % 2) * 64 + 64)

        for qc in range(4):
            def pairs_cmp(qc_):
                return [(0, 0, 512, [(ident[0:127, 0:127], negcmp[0:127, qc_ * 512:(qc_ + 1) * 512], 0, 512, ['cbf', 'negcmp'])], None)]

            def out_cmp(h, qc_, bo):
                rd, rdres = norm_out(bo, 97, oacc[:, :, h, :], 'oacc')
                O = PS[bo][:, 0:4 * 97].rearrange("p (j w) -> p j w", j=4)
                if qc_ >= 2:
                    rb = rd.unsqueeze(2).to_broadcast([128, 4, 32])
                    if h == 0:
                        TT('dve', imp, O[:, :, 65:97], rb, ALU.mult, [P(bo), rdres], ['imp'])
                    else:
                        TT('dve', ntmp2, O[:, :, 65:97], rb, ALU.mult, [P(bo), rdres], ['ntmp2'])
                        TT('dve', imp, imp, ntmp2, ALU.add, ['ntmp2', 'imp'], ['imp'])
                coef = gts[:, 4 * qc_:4 * qc_ + 4, 3 * h]
                TT('dve', oacc[:, :, h, :], oacc[:, :, h, :], coef.unsqueeze(2).to_broadcast([128, 4, 64]), ALU.mult,
                   ['oacc', 'gts'], ['oacc'])

            attention(4, qfD, lambda h: (kcmpT[hrows(h), :], 'kcmpT'), lambda h, kt: (vca[0:127, :], 'vca'), 64 ** -0.5,
                      pairs_cmp, out_cmp, Ebufs, OTs, vw=97, kparts=127, qcs=[qc], erot=erotD, otrot=otrotD)
            if qc >= 2:
                for jl in range(4):
                    qt8 = 4 * qc + jl - 8
                    cand, candm1, forced = selc[:, 0, qt8, :], selc[:, 1, qt8, :], selc[:, 2, qt8, :]
                    TT('dve', impm, imp[:, jl, :], cand, ALU.mult, ['imp', 'selc'], ['impm'])
                    TT('dve', impm, impm, candm1, ALU.add, ['impm', 'selc'], ['impm'])
                    S.op('dve', lambda e: e.max(mx8[:, 0:8], impm), ['impm'], ['mx8'])
                    S.op('dve', lambda e: e.match_replace(impm2, mx8[:, 0:8], impm, -1e9), ['mx8', 'impm'], ['impm2'])
                    S.op('dve', lambda e: e.max(mx8[:, 8:16], impm2), ['impm2'], ['mx8'])
                    TS('dve', selm, impm, mx8[:, 12:13], None, ALU.is_ge, None, ['impm', 'mx8'], ['selm'])
                    TT('dve', selm, selm, cand, ALU.mult, ['selm', 'selc'], ['selm'])
                    TT('dve', selm, selm, forced, ALU.max, ['selm', 'selc'], ['selm'])
                    TS('dve', selb, selm, -1.0, BIG, ALU.add, ALU.mult, ['selm'], ['selb'])
                    b = psT.next()
                    pst = PS[b][:, :].bitcast(BF16)
                    TR(pst[0:32, 0:128], selb, ['selb', 'cbf'], [P(b)])
                    CP('act', MnegT[0:32, (4 * qc + jl - 8) * 128:(4 * qc + jl - 7) * 128], pst[0:32, 0:128], [P(b)], ['MnegT'])

            def extra_slc(qc_, kt, c0):
                if qc_ < 2:
                    return []
                q0 = qc_ * 512 + c0 - 1024
                return [(expand[:, kt * 128:(kt + 1) * 128], MnegT[:, q0:q0 + 512 - c0], 0, 512 - c0, ['cbf', 'MnegT'])]

            def out_slc(h, qc_, bo):
                norm_out(bo, 65, oacc[:, :, h, :], 'oacc', coef=gts[:, 4 * qc_:4 * qc_ + 4, 3 * h + 1], coefres='gts',
                         accumulate=True, tmp=ntmp)

            attention(4, qfDr, lambda h: (ksT[:, :], 'ksT'), lambda h, kt: (vsw[:, kt, 0, :], 'vsw'), 64 ** -0.5,
                      lambda q_: causal_pairs(q_, extra_slc), out_slc, Ebufs, OTs, qcs=[qc], vwm=128, erot=erotD, otrot=otrotD)

            def pairs_win(qc_):
                pl = []
                for kt in range(max(0, 4 * qc_ - 4), 4 * qc_ + 4):
                    c0 = max(0, kt - 4 * qc_) * 128
                    c1 = min(4, kt + 5 - 4 * qc_) * 128
                    adds = []
                    if kt >= 4 * qc_:
                        adds.append((ident, negtri, 0, 128, ['cbf']))
                    if kt + 4 <= 4 * qc_ + 3:
                        jl = kt + 4 - 4 * qc_
                        adds.append((ident, negtri2, jl * 128 - c0, 128, ['cbf']))
                    pl.append((kt, c0, c1, adds, None))
                return pl

            def out_win(h, qc_, bo):
                norm_out(bo, 65, oacc[:, :, h, :], 'oacc', coef=gts[:, 4 * qc_:4 * qc_ + 4, 3 * h + 2], coefres='gts',
                         accumulate=True, tmp=ntmp)

            attention(4, qfDr, lambda h: (kwT[:, :], 'kwT'), lambda h, kt: (vsw[:, kt, 1, :], 'vsw'), 64 ** -0.5,
                      pairs_win, out_win, Ebufs, OTs, qcs=[qc], vwm=128, erot=erotD, otrot=otrotD)
            CP('act', obf, oacc.rearrange("p j h d -> p j (h d)"), ['oacc'], ['obf'])
            finish_branch(3, obf, qc, zsT, brb)

        AR.reset()
        zsT = r3(AR.get("zsT", 2 * S_LEN, BF16), 2)
        brb = [r3(AR.get(f"brb{i}", 2 * 512, BF16), 2) for i in range(2)]
        qT = r3(AR.get("qT", 2 * S_LEN, BF16), 2)
        mems = AR.get("mems", 2 * D, F32).rearrange("p (t d) -> p t d", t=2)
        mgb = AR.get("mgb", D, F32)
        hb0 = AR.get("hb0", D, BF16)
        hb1 = AR.get("hb1", D, BF16)
        junk = AR.get("junk", D, BF16)
        memhT = r3(AR.get("memhT", 8 * 256, BF16), 8)
        kmT = r3(AR.get("kmT", 4 * 256, BF16), 4)
        vm = AR.get("vm", 2 * 4 * 128, BF16).rearrange("p (t h w) -> p t h w", t=2, h=4)
        Ebufs = [AR.get(f"E{i}", 512, BF16) for i in range(3)]
        OTs = [AR.get(f"ots{i}", 512, F32) for i in range(2)]
        tmpE = AR.get("tmpE", 4 * 256, F32).rearrange("p (j h d) -> p j h d", j=4, h=4)
        obf = AR.get("obf", 4 * 256, BF16).rearrange("p (j c) -> p j c", j=4)
        DMA('sp', mems, mem_d.rearrange("(t p) d -> p t d", p=128), (), ['mems'], 'c1')
        DMA('sp', mgb, memg_d[l:l + 1, :].partition_broadcast(128), (), ['mgb'], 'c1')
        MEMSET('pool', vm[:, :, :, 65:128], 0.0, ['vm'])
        MEMSET('pool', vm[:, :, :, 64:65], 1.0, ['vm'])
        MEMSET('pool', kmT[:, :, :], 0.0, ['kmT'])
        rmsnorm_rows(mems, 2, lambda t: 'mems', mgb, 'mgb', memhT, 'memhT', [hb0, hb1], junk)
        s, ws = get_w(('E1', l), 512, w_rows(win_d[l], OFF_E, 512))
        s2, ws2 = get_w(('E2', l), 512, w_rows(wmemkv_d[l], 0, 512))
        for ch in range(2):
            for tc in range(4):
                b = proj_fm(ws, ('w', s), ch * 128, 128, tc)
                CP('act', qT[:, ch, tc * 512:(tc + 1) * 512], PS[b][:, :], [P(b)], ['qT'])
        zs_proj(ws, ('w', s), 256, zsT)
        for ch in range(2):
            b = proj_fm(ws2, ('w', s2), ch * 128, 128, 0, src=memhT, srcres='memhT', ntok=256)
            CP('act', kmT[0:64, 2 * ch, :], PS[b][0:64, 0:256], [P(b)], ['kmT'])
            CP('act', kmT[64:128, 2 * ch + 1, :], PS[b][64:128, 0:256], [P(b)], ['kmT'])
        for t in range(2):
            b = proj_tm(ws2, ('w', s2), 256, 256, t, src=memhT, srcres='memhT')
            CP('act', vm[:, t, :, 0:64], PS[b][:, 0:256].rearrange("p (h d) -> p h d", h=4), [P(b)], ['vm'])

        def qfE(h):
            return qT[:, h // 2, :], 'qT'

        def kfE(h):
            return kmT[:, h, :], 'kmT'

        def outE(h, qc, bo):
            norm_out(bo, 65, tmpE[:, :, h, :], 'tmpE')

        def postE(qc):
            CP('act', obf, tmpE.rearrange("p j h d -> p j (h d)"), ['tmpE'], ['obf'])
            finish_branch(4, obf, qc, zsT, brb)

        if not debug:
            prefetch_w(('M', l, 0, 0), 1024, w_rows(wmerge_d[l], 0, 1024))
        attention(4, qfE, kfE, lambda h, kt: (vm[:, kt, h, :], 'vm'), 64 ** -0.5,
                  lambda qc: [(0, 0, 512, [], None), (1, 0, 512, [], None)], outE, Ebufs, OTs, post=postE, vwm=128)

        if debug:
            break

        AR.reset()
        bm = AR.get("bm", 40, F32)
        DMA('sp', bm, bmerge_d[l, :, :], (), ['bm'], 'c1')
        wbr = [r3(AR.get(f"wbr{i}", 2 * D, BF16), 2) for i in range(2)]
        mixed = r3(AR.get("mixed", 8 * 1024, F32), 8)
        brc = [r3(AR.get(f"brc{i}", 2 * 1024, BF16), 2) for i in range(2)]
        gate = [AR.get(f"gate{i}", 512, BF16) for i in range(2)]
        prod = [AR.get(f"prod{i}", 512, F32) for i in range(2)]
        mbf = [r3(AR.get(f"mbf{i}", 8 * 128, BF16), 8) for i in range(2)]
        psMg = Rot([0, 1, 4, 6])
        psMy = Rot([2, 3, 5, 7])
        for tp in range(2):
            tsl = slice(tp * 1024, (tp + 1) * 1024)
            for n in range(5):
                s, ws = get_w(('M', l, tp, n), 1024, w_rows(wmerge_d[l], n * 1024, 1024))
                if n < 4:
                    prefetch_w(('M', l, tp, n + 1), 1024, w_rows(wmerge_d[l], (n + 1) * 1024, 1024))
                else:
                    prefetch_w(('O', l, tp), 1024, w_rows(wout_d[l], 0, 1024))
                wb = n % 2
                if n == 0 and tp == 0:
                    DMA('pool', wbr[0], wbranch_d[l, 0].rearrange("(c p) n -> p c n", p=128), (), [('wbr', 0)], 'wbr0')
                if n < 4:
                    nb = (n + 1) % 2
                    DMA('pool', wbr[nb], wbranch_d[l, n + 1].rearrange("(c p) n -> p c n", p=128), (), [('wbr', nb)], f'wbr{nb}')
                def load_brc(n_, tp_):
                    wb_ = n_ % 2
                    tsl_ = slice(tp_ * 1024, (tp_ + 1) * 1024)
                    DMA('sp', brc[wb_], brt_d[2 * n_:2 * n_ + 2, :, tsl_].rearrange("c p t -> p c t"),
                        [('brt', 2 * n_ + hf_, 2 * tp_ + q_) for hf_ in range(2) for q_ in range(2)], [('brc', wb_)], f'brc{wb_}')
                if n == 0 and tp == 0:
                    load_brc(0, 0)
                if n < 4:
                    load_brc(n + 1, tp)
                for dc in range(8):
                    for t2_ in range(2):
                        tsub = slice(tp * 1024 + t2_ * 512, tp * 1024 + (t2_ + 1) * 512)
                        bg = psMg.next()
                        for c in range(8):
                            MM(PS[bg][:, :], ws[:, c, dc * 128:(dc + 1) * 128], hT[:, c, tsub], c == 0, c == 7, [('w', s), 'hT'], [P(bg)])
                        by = psMy.next()
                        for c in range(2):
                            MM(PS[by][:, :], wbr[wb][:, c, dc * 128:(dc + 1) * 128], brc[wb][:, c, t2_ * 512:(t2_ + 1) * 512],
                               c == 0, c == 1, [('wbr', wb), ('brc', wb)], [P(by)])
                        gi = (dc * 2 + t2_) % 2
                        ACT(gate[gi], PS[bg][:, :], AF.Sigmoid, [P(bg), 'bm'], [('gate', gi)], bias=bm[:, n * 8 + dc:n * 8 + dc + 1])
                        msl = mixed[:, dc, t2_ * 512:(t2_ + 1) * 512]
                        if n == 0:
                            TT('dve', msl, PS[by][:, :], gate[gi], ALU.mult, [P(by), ('gate', gi)], [('mixed', dc, t2_)])
                        else:
                            TT('dve', prod[gi], PS[by][:, :], gate[gi], ALU.mult, [P(by), ('gate', gi)], [('prod', gi)])
                            TT('pool', msl, msl, prod[gi], ALU.add, [('prod', gi), ('mixed', dc, t2_)], [('mixed', dc, t2_)])
            s, ws = get_w(('O', l, tp), 1024, w_rows(wout_d[l], 0, 1024))
            if tp == 0:
                prefetch_w(('M', l, 1, 0), 1024, w_rows(wmerge_d[l], 0, 1024))
                DMA('pool', wbr[0], wbranch_d[l, 0].rearrange("(c p) n -> p c n", p=128), (), [('wbr', 0)], 'wbr0')
                load_brc(0, 1)
            elif l + 1 < n_layers:
                prefetch_w(('C', l + 1), 512, w_rows(win_d[l + 1], OFF_C, 512))
            for tt in range(8):
                t = tp * 8 + tt
                mi = tt % 2
                CP('act', mbf[mi], mixed[:, :, tt * 128:(tt + 1) * 128],
                   [('mixed', dc, tt // 4) for dc in range(8)], [('mbf', mi)])
                for half in range(2):
                    b = psC.next()
                    for c in range(8):
                        MM(PS[b][:, :], mbf[mi][:, c, :], ws[:, c, half * 512:(half + 1) * 512], c == 0, c == 7, [('mbf', mi), ('w', s)], [P(b)])
                    TT('dve', xs[:, t, half * 512:(half + 1) * 512], xs[:, t, half * 512:(half + 1) * 512], PS[b][:, :], ALU.add,
                       [P(b), ('x', t)], [('x', t)])

    if not debug:
        AR.reset()
        gbc = AR.get("gbc", D, F32)
        DMA('sp', gbc, finalg_d[0:1, :].partition_broadcast(128), (), ['gbc'], 'c1')
        junk = AR.get("junk", D, BF16)
        ob = [AR.get(f"ob{i}", D, F32) for i in range(2)]
        for t in range(NT):
            ACT(junk, xs[:, t, :], AF.Square, [('x', t)], ['junk', ('ss', t)], accum_out=ss[:, t:t + 1])
        allss = [('ss', t) for t in range(NT)]
        TS('dve', rs, ss, 1.0 / D, EPS, ALU.mult, ALU.add, allss, ['rs'])
        ACT(rs, rs, AF.Sqrt, ['rs'], ['rs'])
        RECIP(rs, rs, ['rs'], ['rs'])
        for t in range(NT):
            STT(ob[t % 2], xs[:, t, :], rs[:, t:t + 1], gbc, ALU.mult, ALU.mult, [('x', t), 'rs', 'gbc'], [('ob', t % 2)])
            DMA('sp', out_d[t * 128:(t + 1) * 128, :], ob[t % 2], [('ob', t % 2)], [('out', t)], f'o{t % 2}')
    S.final_wait('sp')

    sem_names = list(ENGS[:4]) + sorted(S.dmacnt.keys())
    sems = {k: es.enter_context(nc.semaphore(f"s_{k}")) for k in sem_names}
    block = es.enter_context(nc.Block())

    def mk(eng):
        def body(engine):
            for waits, fn, tok in S.q[eng]:
                for k, v in waits.items():
                    engine.wait_ge(sems[k], v)
                if fn is None:
                    continue
                ins = fn(engine)
                ins.then_inc(sems[tok[0]], 1 if tok[0] in ENGS else 16)
        return body

    block.tensor(mk('pe'))
    block.scalar(mk('act'))
    block.vector(mk('dve'))
    block.gpsimd(mk('pool'))
    block.sync(mk('sp'))
    es.close()
    return nc


def _host_consts():
    bf = ml_dtypes.bfloat16
    k = np.arange(128)[:, None]
    q = np.arange(128)[None, :]
    ident = (k == q).astype(np.float32)
    negtri = np.where(k > q, -BIG, 0.0).astype(np.float32)
    negtri2 = np.where(q >= k, -BIG, 0.0).astype(np.float32)
    expand = np.zeros((128, 16, 128), np.float32)
    for kt in range(16):
        for kk in range(128):
            expand[2 * kt + kk // 64, kt, kk] = 1.0
    n_cmp = 127
    c0 = np.arange(n_cmp)[:, None] * 16
    s0 = np.arange(32)[None, :] * 64
    overlap = np.clip(np.minimum(c0 + 32, s0 + 64) - np.maximum(c0, s0), 0, None) / 16
    ovl = np.zeros((128, 33), np.float32)
    ovl[:127, 0] = 1.0
    ovl[:127, 1:] = overlap
    perms = []
    for dh in (32, 64):
        pm = np.zeros((128, 128), np.float32)
        for m_ in range(128):
            blk, j = m_ // dh, m_ % dh
            pm[blk * dh + (j + dh // 2) % dh, m_] = 1.0
        perms.append(pm)
    cbf = np.concatenate([ident, negtri, negtri2, expand.reshape(128, 2048), ovl] + perms, axis=1).astype(bf)
    x = np.arange(2048)[None, :]
    dlt = x - k
    wst = ((dlt >= 0) & (dlt <= 128)).astype(np.float32) + ((dlt >= 0) & (dlt % 4 == 0) & (dlt <= 512)).astype(np.float32) \
        + ((dlt >= 0) & (dlt % 16 == 0) & (dlt <= 2048)).astype(np.float32)
    wstrip = wst.astype(bf)
    negcmp = np.where(16 * k + 31 <= x, 0.0, -BIG).astype(np.float32)
    negcmp[127, :] = -BIG
    negcmp = negcmp.astype(bf)
    t = np.arange(S_LEN, dtype=np.float32)
    rope = np.zeros((4, 128, S_LEN), np.float32)
    for ti, dh in enumerate((32, 64)):
        half = dh // 2
        inv = (np.float32(10000.0) ** (-np.arange(half, dtype=np.float32) / np.float32(half))).astype(np.float32)
        ang = (t[:, None] * inv[None, :]).astype(np.float32)
        cs, sn = np.cos(ang).astype(np.float32), np.sin(ang).astype(np.float32)
        for p in range(128):
            j = p % dh
            rope[2 * ti, p] = cs[:, j % half]
            rope[2 * ti + 1, p] = sn[:, j % half] * (-1.0 if j < half else 1.0)
    s5k = np.concatenate([np.arange(17), 15 - np.arange(16), np.arange(128)]).astype(np.float32)
    s5k = np.ascontiguousarray(np.broadcast_to(s5k[None, :], (128, 161)))
    s5mm = np.zeros((128, 2, 16, 16), np.float32)
    for s8 in range(8):
        for sh in range(2):
            s5mm[s8 * 16:(s8 + 1) * 16, sh, sh * 8 + s8:, :] = 1.0
    s5mm = np.ascontiguousarray(s5mm.reshape(128, 512).astype(bf))
    sel = np.zeros((128, 3, 8, 32), np.float32)
    for qt in range(8, 16):
        for qq in range(128):
            qblk = (qt * 128 + qq) // 64
            j = np.arange(32)
            cand = ((j >= 1) & (j <= qblk - 2)).astype(np.float32)
            forced = ((j == 0) | (j == qblk) | (j == qblk - 1)).astype(np.float32)
            sel[qq, 0, qt - 8] = cand
            sel[qq, 1, qt - 8] = cand - 1.0
            sel[qq, 2, qt - 8] = forced
    return dict(cbf=np.ascontiguousarray(cbf), wstrip=np.ascontiguousarray(wstrip), negcmp=np.ascontiguousarray(negcmp),
                rope=rope, identf=np.ascontiguousarray(ident, dtype=np.float32), s5k=s5k, s5mm=s5mm, selc=np.ascontiguousarray(sel.reshape(128, 768)))


def _host_layout(inp):
    offs = np.cumsum([0] + list(IN_SIZES))
    col = {n: np.arange(offs[i], offs[i + 1]) for i, n in enumerate(IN_NAMES)}

    def swap(cols, dh):
        c = cols.reshape(-1, dh)
        h = dh // 2
        return np.concatenate([c[:, h:], c[:, :h]], axis=1).reshape(-1)

    order = np.concatenate([
        col['a_q'], swap(col['a_q'], 32), col['a_k'], swap(col['a_k'], 32), col['a_v'], col['a_z'],
        col['b_q'], swap(col['b_q'], 64), col['b_k'], swap(col['b_k'], 64), col['b_v'], col['b_z'],
        col['c_u'], col['c_z'],
        col['d_q'], swap(col['d_q'], 64),
        col['d_kc'], col['d_vc'], col['d_ks'], col['d_ks'], swap(col['d_ks'], 64), swap(col['d_ks'], 64),
        col['d_kw'], col['d_kw'], swap(col['d_kw'], 64), swap(col['d_kw'], 64), col['d_vs'], col['d_vw'],
        col['d_z'], col['d_g'],
        col['e_q'], col['e_z']])
    assert order.shape[0] == NWIN
    f = lambda a: np.ascontiguousarray(np.asarray(a, dtype=np.float32))
    d = {}
    d['win'] = f(inp['w_in'][:, :, order])
    d['wmerge'] = f(inp['w_merge'])
    d['bmerge'] = f(inp['b_merge'].reshape(DEPTH, 40, 128).transpose(0, 2, 1))
    d['wbranch'] = f(inp['w_branch'])
    d['wout'] = f(inp['w_out'])
    d['normg'] = f(inp['norm_g'])
    d['finalg'] = f(inp['final_g'].reshape(1, D))
    d['memg'] = f(inp['mem_norm_g'])
    d['wmemkv'] = f(inp['w_mem_kv'])
    d['dlam'] = f(inp['diff_lambda'].reshape(DEPTH, 128))
    d['sublng'] = f(inp['diff_subln_g'])
    lr, li, ldt = inp['s5_lambda_re'], inp['s5_lambda_im'], inp['s5_log_dt']
    l1 = np.zeros((DEPTH, 128, 24), np.float32)
    for j in range(8):
        for g2 in range(2):
            g = 2 * j + g2
            l1[:, g2 * 64:(g2 + 1) * 64, j] = lr[:, g, :]
            l1[:, g2 * 64:(g2 + 1) * 64, 8 + j] = li[:, g, :]
            l1[:, g2 * 64:(g2 + 1) * 64, 16 + j] = ldt[:, g][:, None]
    d['s5l1'] = l1
    bre, bim = inp['s5_b_re'], inp['s5_b_im']
    cre, cim = inp['s5_c_re'], inp['s5_c_im']
    bT = np.zeros((DEPTH, 128, 8, 2, 16), np.float32)
    cTt = np.zeros((DEPTH, 128, 8, 2, 16), np.float32)
    for j in range(8):
        for g2 in range(2):
            g = 2 * j + g2
            rows = slice(g2 * 64, (g2 + 1) * 64)
            bT[:, rows, j, 0, :] = bre[:, g]
            bT[:, rows, j, 1, :] = bim[:, g]
            cTt[:, rows, j, 0, :] = cre[:, g].transpose(0, 2, 1)
            cTt[:, rows, j, 1, :] = cim[:, g].transpose(0, 2, 1)
    d['s5bT'] = bT.reshape(DEPTH, 128, 256)
    d['s5cT'] = cTt.reshape(DEPTH, 128, 256)
    d['s5dflat'] = f(inp['s5_d'].reshape(DEPTH, 256))
    d['wglu'] = f(inp['w_glu'])
    d['bglu'] = f(inp['b_glu'].reshape(DEPTH, 4, 128).transpose(0, 2, 1))
    pe = inp['nsa_pe']
    d['nsape'] = f(pe.transpose(0, 1, 3, 2).reshape(DEPTH, 128, 32))
    w1 = inp['nsa_w1'].reshape(DEPTH, 2, 32, 64, 256)
    d['nsaw1'] = f(w1.transpose(0, 1, 3, 2, 4).reshape(DEPTH, 2, 64, 32 * 256))
    w2 = inp['nsa_w2'].reshape(DEPTH, 2, 2, 128, 64)
    w2k = w2[:, 0].transpose(0, 2, 1, 3)
    w2k = np.concatenate([w2k, w2k], axis=3).reshape(DEPTH, 128, 256)
    w2v = w2[:, 1].transpose(0, 2, 1, 3).reshape(DEPTH, 128, 128)
    d['nsaw2'] = f(np.concatenate([w2k, w2v], axis=2))
    return d


_CACHE = {}


def kernel(**inputs):
    inp = {k: np.asarray(v) for k, v in inputs.items()}
    if 'nc' not in _CACHE:
        _CACHE['nc'] = build_program()
        _CACHE['consts'] = _host_consts()
    nc = _CACHE['nc']
    shared = dict(_CACHE['consts'])
    shared.update(_host_layout(inp))
    in_maps = []
    for b in range(8):
        m = dict(shared)
        m['x'] = np.ascontiguousarray(inp['x'][b], dtype=np.float32)
        m['mem'] = np.ascontiguousarray(inp['mem'][b], dtype=np.float32)
        in_maps.append(m)
    res = run_bass_kernel_spmd(nc, in_maps, core_ids=list(range(8)))
    out = np.stack([np.asarray(r['out'], dtype=np.float32) for r in res.results], axis=0)
    return out
```

```python
import math
import numpy as np
import ml_dtypes
from contextlib import ExitStack
import concourse.bass as bass
import concourse.mybir as mybir
from concourse.bass_utils import run_bass_kernel_spmd

F32 = mybir.dt.float32
BF16 = mybir.dt.bfloat16
I32 = mybir.dt.int32
AF = mybir.ActivationFunctionType
ALU = mybir.AluOpType
AX = mybir.AxisListType

S_LEN = 2048
D = 1024
NT = 16
DEPTH = 2
NWIN = 5644
BIG = 32768.0
EPS = 1e-6
TWO_PI = 6.28318
IN_SIZES = (256, 256, 256, 256, 256, 256, 256, 256, 256, 256, 256, 64, 64, 64, 64, 64, 64, 12, 256, 256, 256)
IN_NAMES = ['a_q', 'a_k', 'a_v', 'a_z', 'b_q', 'b_k', 'b_v', 'b_z', 'c_u', 'c_z', 'd_q', 'd_kc', 'd_vc', 'd_ks',
            'd_vs', 'd_kw', 'd_vw', 'd_g', 'd_z', 'e_q', 'e_z']
OFF_A1, OFF_A2, OFF_B1, OFF_B2, OFF_C, OFF_D1, OFF_D2, OFF_D3, OFF_E = 0, 1024, 1536, 2560, 3072, 3584, 4096, 4864, 5132

ENGS = ('pe', 'act', 'dve', 'pool', 'sp')


class Sched:
    def __init__(self):
        self.q = {e: [] for e in ENGS}
        self.cnt = {e: 0 for e in ENGS}
        self.lastw = {}
        self.readers = {}
        self.dmacnt = {}
        self.seen = {e: {} for e in ENGS}
        self.group = {}

    def _need(self, eng, waits, tok):
        if tok is None:
            return
        k, v = tok
        if k == eng and eng == 'pe':
            return
        if self.seen[eng].get(k, 0) >= v:
            return
        if waits.get(k, 0) < v:
            waits[k] = v

    def op(self, eng, fn, R=(), W=(), dma=None):
        waits = {}
        for r in R:
            self._need(eng, waits, self.lastw.get(r))
        for w in W:
            self._need(eng, waits, self.lastw.get(w))
            for t in self.readers.get(w, ()):
                self._need(eng, waits, t)
        for k, v in waits.items():
            self.seen[eng][k] = v
        if dma is None:
            self.cnt[eng] += 1
            tok = (eng, self.cnt[eng])
        else:
            self.dmacnt[dma] = self.dmacnt.get(dma, 0) + 16
            tok = (dma, self.dmacnt[dma])
        for r in R:
            self.readers.setdefault(r, []).append(tok)
        for w in W:
            self.lastw[w] = tok
            self.readers[w] = []
        if dma is not None and (dma.startswith('c') and not dma.startswith('cb') or dma == 'x'):
            g = self.group.setdefault(dma, [])
            g.extend(W)
            for w in g:
                if self.lastw.get(w, (None,))[0] == dma:
                    self.lastw[w] = tok
        self.q[eng].append((waits, fn, tok))

    def barrier(self):
        self.group = {}
        for e in ENGS:
            waits = {}
            for e2 in ENGS:
                if e2 != e and e2 != 'sp' and self.cnt[e2] > 0:
                    self._need(e, waits, (e2, self.cnt[e2]))
            for k, v in self.dmacnt.items():
                self._need(e, waits, (k, v))
            for k, v in waits.items():
                self.seen[e][k] = v
            if waits:
                self.q[e].append((waits, None, None))

    def final_wait(self, eng='sp'):
        waits = {}
        for k, v in self.dmacnt.items():
            self._need(eng, waits, (k, v))
        for e2 in ENGS:
            if e2 != eng and e2 != 'sp' and self.cnt[e2] > 0:
                self._need(eng, waits, (e2, self.cnt[e2]))
        self.q[eng].append((waits, None, None))


class Rot:
    def __init__(self, items):
        self.items = list(items)
        self.i = 0

    def next(self):
        v = self.items[self.i % len(self.items)]
        self.i += 1
        return v


def build_program(debug=False, n_layers=DEPTH):
    nc = bass.Bass("TRN2", target_bir_lowering=False)
    S = Sched()

    def din(name, shape, dt=F32):
        return nc.dram_tensor(name, list(shape), dt, kind="ExternalInput").ap()

    x_d = din("x", [S_LEN, D])
    mem_d = din("mem", [256, D])
    win_d = din("win", [DEPTH, D, NWIN])
    wmerge_d = din("wmerge", [DEPTH, D, 5 * D])
    bmerge_d = din("bmerge", [DEPTH, 128, 40])
    wbranch_d = din("wbranch", [DEPTH, 5, 256, D])
    wout_d = din("wout", [DEPTH, D, D])
    normg_d = din("normg", [DEPTH, D])
    finalg_d = din("finalg", [1, D])
    memg_d = din("memg", [DEPTH, D])
    wmemkv_d = din("wmemkv", [DEPTH, D, 512])
    dlam_d = din("dlam", [DEPTH, 128])
    sublng_d = din("sublng", [DEPTH, 64])
    s5l1_d = din("s5l1", [DEPTH, 128, 24])
    s5bT_d = din("s5bT", [DEPTH, 128, 256])
    s5cT_d = din("s5cT", [DEPTH, 128, 256])
    s5dflat_d = din("s5dflat", [DEPTH, 256])
    s5k_d = din("s5k", [128, 161])
    s5mm_d = din("s5mm", [128, 512], BF16)
    wglu_d = din("wglu", [DEPTH, 256, 512])
    bglu_d = din("bglu", [DEPTH, 128, 4])
    nsape_d = din("nsape", [DEPTH, 128, 32])
    nsaw1_d = din("nsaw1", [DEPTH, 2, 64, 32 * 256])
    nsaw2_d = din("nsaw2", [DEPTH, 128, 384])
    cbf_d = din("cbf", [128, 384 + 2048 + 33 + 256], BF16)
    wstrip_d = din("wstrip", [128, 2048], BF16)
    negcmp_d = din("negcmp", [128, 2048], BF16)
    rope_d = din("rope", [4, 128, S_LEN])
    identf_d = din("identf", [128, 128])
    selc_d = din("selc", [128, 768])
    out_d = nc.dram_tensor("out", [S_LEN, D], F32, kind="ExternalOutput").ap()
    brt_d = nc.dram_tensor("brt", [10, 128, S_LEN], BF16,
                           kind=("ExternalOutput" if debug else "Internal")).ap()

    es = ExitStack()

    def sb(name, shape, dt):
        return es.enter_context(nc.sbuf_tensor(name, list(shape), dt))

    xs = sb("xs", [128, NT, D], F32)
    hT = sb("hT", [128, 8, S_LEN], BF16)
    wbuf = [sb("wbuf0", [128, 8192], BF16), sb("wbuf1", [128, 8192], BF16)]
    cbf = sb("cbf_s", [128, 384 + 2048 + 33 + 256], BF16)
    small = sb("small", [128, 256], F32)
    identf = sb("identf_s", [128, 128], F32)
    ARENA_W = 18680
    arena = sb("arena", [128, ARENA_W], F32)
    PS = [es.enter_context(nc.psum_tensor(f"ps{i}", [128, 512], F32)) for i in range(8)]

    ident = cbf[:, 0:128]
    negtri = cbf[:, 128:256]
    negtri2 = cbf[:, 256:384]
    expand = cbf[:, 384:384 + 2048]
    ovl = cbf[:, 384 + 2048:384 + 2048 + 33]
    perm32 = cbf[:, 2465:2465 + 128]
    perm64 = cbf[:, 2465 + 128:2465 + 256]

    ss = small[:, 0:16]
    rs = small[:, 16:32]
    lamt = small[:, 32:40]

    class Arena:
        def __init__(self):
            self.off = 0

        def reset(self):
            S.barrier()
            self.off = 0

        def mark(self):
            return self.off

        def release_to(self, off):
            S.barrier()
            self.off = off

        def get(self, name, free, dt):
            words = (free * (2 if dt == BF16 else 4) + 3) // 4
            assert self.off + words <= ARENA_W, (name, self.off, words)
            v = arena[:, self.off:self.off + words]
            self.off += words
            if dt != F32:
                v = v.bitcast(dt)
                if v.shape[1] != free:
                    v = v[:, 0:free]
            return v

    AR = Arena()

    def MM(out, lhsT, rhs, start, stop, R, W, sgc=False):
        if sgc:
            S.op('pe', lambda e: e.matmul(out, lhsT=lhsT, rhs=rhs, start=start, stop=stop, skip_group_check=True), R, W)
        else:
            S.op('pe', lambda e: e.matmul(out, lhsT=lhsT, rhs=rhs, start=start, stop=stop), R, W)

    def TR(out, in_, R, W):
        n = in_.shape[0]
        S.op('pe', lambda e: e.transpose(out, in_, ident[0:n, 0:n]), R, W)

    def TRF(out, in_, R, W):
        n = in_.shape[0]
        S.op('pe', lambda e: e.transpose(out, in_, identf[0:n, 0:n]), R, W)

    def ACT(out, in_, func, R, W, bias=0.0, scale=1.0, accum_out=None):
        if accum_out is None:
            S.op('act', lambda e: e.activation(out, in_, func, bias=bias, scale=scale), R, W)
        else:
            S.op('act', lambda e: e.activation(out, in_, func, bias=bias, scale=scale, accum_out=accum_out), R, W)

    def TT(eng, out, in0, in1, op, R, W):
        S.op(eng, lambda e: e.tensor_tensor(out, in0, in1, op), R, W)

    def TS(eng, out, in0, s1, s2, op0, op1, R, W):
        if op1 is None:
            S.op(eng, lambda e: e.tensor_scalar(out, in0, s1, None, op0), R, W)
        else:
            S.op(eng, lambda e: e.tensor_scalar(out, in0, s1, s2, op0, op1), R, W)

    def STT(out, in0, scalar, in1, op0, op1, R, W):
        S.op('dve', lambda e: e.scalar_tensor_tensor(out, in0, scalar, in1, op0, op1), R, W)

    def CP(eng, out, in_, R, W):
        if eng == 'act':
            S.op('act', lambda e: e.copy(out, in_), R, W)
        else:
            S.op(eng, lambda e: e.tensor_copy(out, in_), R, W)

    def RECIP(out, in_, R, W):
        S.op('dve', lambda e: e.reciprocal(out, in_), R, W)

    def MEMSET(eng, out, val, W):
        S.op(eng, lambda e: e.memset(out, val), (), W)

    def DMA(eng, out, in_, R, W, key):
        S.op(eng, lambda e: e.dma_start(out=out, in_=in_), R, W, dma=key)

    psA = Rot([0, 1])
    psB = Rot([2, 3])
    psC = Rot([4, 5])
    psD = Rot([6, 7])
    psS = Rot([0, 1, 4])
    psT = Rot([5])

    def P(i):
        return ('ps', i)

    wrot = Rot([0, 1])

    def load_w(ncols, src3):
        s = wrot.next()
        dst = wbuf[s][:, 0:8 * ncols].rearrange("p (a b) -> p a b", a=8)
        DMA('pool', dst, src3, (), [('w', s)], f'w{s}')
        return s, dst

    pending_w = {}

    def prefetch_w(tag, ncols, src3):
        if tag not in pending_w:
            pending_w[tag] = load_w(ncols, src3)

    def get_w(tag, ncols, src3):
        if tag in pending_w:
            return pending_w.pop(tag)
        return load_w(ncols, src3)

    def w_rows(dram2d, c0, ncols):
        return dram2d[:, c0:c0 + ncols].rearrange("(c p) n -> p c n", p=128)

    DMA('sp', cbf[:, :], cbf_d[:, :], (), ['cbf'], 'c0')
    DMA('sp', identf[:, :], identf_d[:, :], (), ['identf'], 'c0')
    for t4 in range(4):
        DMA('sp', xs[:, 4 * t4:4 * t4 + 4, :], x_d[512 * t4:512 * (t4 + 1), :].rearrange("(t p) d -> p t d", p=128),
            (), [('x', 4 * t4 + i) for i in range(4)], 'x')

    def rmsnorm_rows(src_tile_ap, ntiles, xres, gb_ap, gres, dstT, dstT_res, hb_bufs, junk):
        for t in range(ntiles):
            ACT(junk, src_tile_ap[:, t, :], AF.Square, [xres(t)], ['junk', ('ss', t)], accum_out=ss[:, t:t + 1])
        allss = [('ss', t) for t in range(ntiles)]
        TS('dve', rs[:, 0:ntiles], ss[:, 0:ntiles], 1.0 / D, EPS, ALU.mult, ALU.add, allss, ['rs'])
        ACT(rs[:, 0:ntiles], rs[:, 0:ntiles], AF.Sqrt, ['rs'], ['rs'])
        RECIP(rs[:, 0:ntiles], rs[:, 0:ntiles], ['rs'], ['rs'])
        for t in range(ntiles):
            hb = hb_bufs[t % 2]
            STT(hb, src_tile_ap[:, t, :], rs[:, t:t + 1], gb_ap, ALU.mult, ALU.mult, [xres(t), 'rs', gres], [('hb', t % 2)])
            b = psC.next()
            pst = PS[b][:, :].bitcast(BF16)
            for c in range(8):
                TR(pst[:, c * 128:(c + 1) * 128], hb[:, c * 128:(c + 1) * 128], [('hb', t % 2), 'cbf'], [P(b)])
            CP('act', dstT[:, :, t * 128:(t + 1) * 128], pst.rearrange("p (c k) -> p c k", c=8), [P(b)], [dstT_res])

    def proj_fm(ws, wres, c0, m, tc, src=None, srcres='hT', ntok=512, bank=None):
        src = hT if src is None else src
        b = psA.next() if bank is None else bank
        for c in range(8):
            MM(PS[b][0:m, 0:ntok], ws[:, c, c0:c0 + m], src[:, c, tc * ntok:(tc + 1) * ntok], c == 0, c == 7,
               [wres, srcres], [P(b)])
        return b

    def proj_tm(ws, wres, c0, n, t, src=None, srcres='hT'):
        src = hT if src is None else src
        b = psA.next()
        for c in range(8):
            MM(PS[b][:, 0:n], src[:, c, t * 128:(t + 1) * 128], ws[:, c, c0:c0 + n], c == 0, c == 7, [wres, srcres], [P(b)])
        return b

    QA_ROT = Rot([0, 1])

    def rope_proj(ws, wres, c_orig, c_sw, dst, dstres, ropeC, ropeS, t1, t2, m=128, perm=None, qa=None):
        for tc in range(4):
            b1 = proj_fm(ws, wres, c_orig, m, tc, bank=psA.next())
            b2 = psB.next()
            ai = QA_ROT.next() % len(qa)
            CP('act', qa[ai][0:m, :], PS[b1][0:m, :], [P(b1)], [('qa', ai)])
            MM(PS[b2][0:m, :], perm[0:m, 0:m], qa[ai][0:m, :], True, True, ['cbf', ('qa', ai)], [P(b2)])
            sl = slice(tc * 512, (tc + 1) * 512)
            TT('dve', t1[0:m, :], PS[b1][0:m, :], ropeC[0:m, sl], ALU.mult, [P(b1), 'ropeC', ('qa', ai)], ['t1'])
            TT('dve', t2[0:m, :], PS[b2][0:m, :], ropeS[0:m, sl], ALU.mult, [P(b2), 'ropeS'], ['t2'])
            if isinstance(dst, tuple):
                TT('pool', dst[0][0:64, sl], t1[0:64, :], t2[0:64, :], ALU.add, ['t1', 't2'], [dstres])
                TT('pool', dst[1][64:128, sl], t1[64:128, :], t2[64:128, :], ALU.add, ['t1', 't2'], [dstres])
            else:
                TT('pool', dst[0:m, sl], t1[0:m, :], t2[0:m, :], ALU.add, ['t1', 't2'], [dstres])

    E_ROT = Rot([0, 1, 2])
    OT_ROT = Rot([0, 1])
    RD_ROT = Rot([0, 1, 2, 3])

    def attention(nvh, qf, kf, vf, scale, pairs, outcb, Ebufs, OTs, vw=65, kparts=128, post=None, qcs=range(4), vwm=None,
                  erot=None, otrot=None, batch=1, srot=None, botbank=None, prep=None):
        vwm = vw if vwm is None else vwm
        srot = psS if srot is None else srot
        erot = E_ROT if erot is None else erot
        otrot = OT_ROT if otrot is None else otrot
        items = []
        for qc in qcs:
            for vh in range(nvh):
                plist = pairs(qc)
                grp = {'bot': None}
                for pi, pr in enumerate(plist):
                    items.append((qc, vh, pi, len(plist), pr, grp))
        st = {}

        def doS(i):
            qc, vh, pi, npl, (kt, c0, c1, adds, mul), grp = items[i]
            bs = srot.next()
            st[i] = bs
            if pi == 0 and prep is not None:
                prep(vh, qc)
            qr_ = qf(vh) if prep is None else qf(vh, qc)
            q_ap, qres = qr_[0], qr_[1]
            k_ap, kres = kf(vh)
            ncol = c1 - c0
            q0 = qc * 512 + c0 - (qr_[2] if len(qr_) > 2 else 0)
            MM(PS[bs][0:kparts, 0:ncol], k_ap[:, kt * 128:kt * 128 + kparts], q_ap[:, q0:q0 + ncol], True, len(adds) == 0,
               [qres, kres], [P(bs)])
            for ai, (lT, rh, co, ncl, ares) in enumerate(adds):
                MM(PS[bs][0:kparts, co:co + ncl], lT, rh, False, ai == len(adds) - 1, ares, [P(bs)])

        def doEV(i):
            qc, vh, pi, npl, (kt, c0, c1, adds, mul), grp = items[i]
            bs = st.pop(i)
            ncol = c1 - c0
            ei = erot.next()
            Eb = Ebufs[ei]
            ACT(Eb[0:kparts, 0:ncol], PS[bs][0:kparts, 0:ncol], AF.Exp, [P(bs)], [('E', ei)], scale=scale)
            if mul is not None:
                m_ap, mres = mul
                TT('dve', Eb[0:kparts, 0:ncol], Eb[0:kparts, 0:ncol], m_ap, ALU.mult, [('E', ei), mres], [('E', ei)])
            if grp['bot'] is None:
                grp['bot'] = psD.next() if botbank is None else botbank
            bot = grp['bot']
            v_ap, vres = vf(vh, kt)
            MM(PS[bot][0:vwm, c0:c1], v_ap, Eb[0:kparts, 0:ncol], pi == 0, pi == npl - 1, [('E', ei), vres], [P(bot)], sgc=True)
            if pi == npl - 1:
                bo = psB.next()
                oi = otrot.next()
                ots = OTs[oi]
                CP('dve', ots[0:vw, :], PS[bot][0:vw, :], [P(bot)], [('ots', oi)])
                for jl in range(4):
                    TRF(PS[bo][:, jl * vw:(jl + 1) * vw], ots[0:vw, jl * 128:(jl + 1) * 128], [('ots', oi), 'identf'], [P(bo)])
                outcb(vh, qc, bo)
                if vh == nvh - 1 and post is not None:
                    post(qc)

        n = len(items)
        LA = 2
        for i in range(min(LA, n)):
            doS(i)
        if batch == 1:
            for i in range(n):
                if i + LA < n:
                    doS(i + LA)
                doEV(i)
        else:
            for i0 in range(0, n, 2):
                for i in (i0 + 2, i0 + 3):
                    if i < n:
                        doS(i)
                for i in (i0, i0 + 1):
                    if i < n:
                        doEV(i)

    rdbuf = [small[:, 64 + 4 * i:68 + 4 * i] for i in range(4)]

    def norm_out(bo, vw, dst, dstres, coef=None, coefres=None, accumulate=False, tmp=None):
        O = PS[bo][:, 0:4 * vw].rearrange("p (j w) -> p j w", j=4)
        ri = RD_ROT.next()
        rd = rdbuf[ri]
        TS('dve', rd, O[:, :, 64], 1e-30, None, ALU.max, None, [P(bo)], [('rd', ri)])
        RECIP(rd, rd, [('rd', ri)], [('rd', ri)])
        if coef is not None:
            TT('dve', rd, rd, coef, ALU.mult, [('rd', ri), coefres], [('rd', ri)])
        rb = rd.unsqueeze(2).to_broadcast([128, 4, 64])
        if not accumulate:
            TT('dve', dst, O[:, :, 0:64], rb, ALU.mult, [P(bo), ('rd', ri)], [dstres])
        else:
            TT('dve', tmp, O[:, :, 0:64], rb, ALU.mult, [P(bo), ('rd', ri)], ['ntmp'])
            TT('pool', dst, dst, tmp, ALU.add, ['ntmp', dstres], [dstres])
        return rd, ('rd', ri)

    def causal_pairs(qc, extra=None):
        pl = []
        for kt in range(4 * qc + 4):
            c0 = max(0, kt - 4 * qc) * 128
            adds = []
            if kt >= 4 * qc:
                adds.append((ident, negtri, 0, 128, ['cbf']))
            if extra is not None:
                adds += extra(qc, kt, c0)
            pl.append((kt, c0, 512, adds, None))
        return pl

    def store_chunk(n, qc, brbk, k):
        for hf in range(2):
            DMA('sp', brt_d[2 * n + hf, :, qc * 512:(qc + 1) * 512], brbk[:, hf, :], [('brb', k)], [('brt', 2 * n + hf, qc)], f'brt{k}')

    def finish_branch(n, obf, qc, zsT, brb):
        b = psT.next()
        pst = PS[b][:, :].bitcast(BF16)
        for jl in range(4):
            for hf in range(2):
                i = jl * 2 + hf
                TR(pst[:, i * 128:(i + 1) * 128], obf[:, jl, hf * 128:(hf + 1) * 128], ['obf', 'cbf'], [P(b)])
        sl = slice(qc * 512, (qc + 1) * 512)
        k = qc % len(brb)
        TT('dve', brb[k].rearrange("p h (j k) -> p h j k", j=4),
           pst.rearrange("p (j h k) -> p h j k", j=4, h=2),
           zsT[:, :, sl].rearrange("p h (j k) -> p h j k", j=4), ALU.mult, [P(b), 'zsT'], [('brb', k)])
        store_chunk(n, qc, brb[k], k)

    def zs_proj(ws, wres, c0, zsT):
        for ch in range(2):
            for tc in range(4):
                b = proj_fm(ws, wres, c0 + ch * 128, 128, tc)
                ACT(zsT[:, ch, tc * 512:(tc + 1) * 512], PS[b][:, :], AF.Silu, [P(b)], ['zsT'])

    def v_proj(ws, wres, c0, nh, vaug, vres):
        for t in range(NT):
            b = proj_tm(ws, wres, c0, nh * 64, t)
            CP('act', vaug[:, t, :, 0:64], PS[b][:, 0:nh * 64].rearrange("p (h d) -> p h d", h=nh), [P(b)], [vres])

    def r3(ap, c):
        return ap.rearrange("p (c t) -> p c t", c=c)

    for l in range(n_layers):
        lam_init = 0.8 - 0.6 * math.exp(-0.3 * l)
        AR.reset()
        hb0 = AR.get("hb0", D, BF16)
        hb1 = AR.get("hb1", D, BF16)
        junk = AR.get("junk", D, BF16)
        gbc = AR.get("gbc", D, F32)
        DMA('sp', gbc, normg_d[l:l + 1, :].partition_broadcast(128), (), ['gbc'], 'c1')
        rmsnorm_rows(xs, NT, lambda t: ('x', t), gbc, 'gbc', hT, 'hT', [hb0, hb1], junk)

        AR.reset()
        Xc = AR.get("Xc", 16 * 256, BF16).rearrange("p (s c) -> p s c", s=16)
        Yc = AR.get("Yc", 16 * 256, F32).rearrange("p (s c) -> p s c", s=16)
        brb = [r3(AR.get("brb0", 2 * 512, BF16), 2)]
        zsc = r3(AR.get("zsc", 2 * 512, BF16), 2)
        dbc = AR.get("dbc", 256, F32)
        bgl = AR.get("bgl", 4, F32)
        wgl = AR.get("wgl", 2 * 512, BF16).rearrange("p (c n) -> p c n", c=2)
        mark5 = AR.mark()
        s5k = AR.get("s5k", 17 + 16 + 128, F32)
        kidx, krev, iotac = s5k[:, 0:17], s5k[:, 17:33], s5k[:, 33:161]
        maskM = AR.get("maskM", 512, BF16).rearrange("p (h c) -> p h c", h=2)
        l1 = AR.get("l1", 24, F32)
        bT = AR.get("bT", 256, F32).rearrange("p (j r k) -> p j r k", j=8, r=2)
        cT = AR.get("cT", 256, F32).rearrange("p (j r k) -> p j r k", j=8, r=2)
        p1 = AR.get("p1", 16 * 8, F32).rearrange("p (s n) -> p s n", s=16)
        pti = AR.get("pti", 136, I32)
        PW = AR.get("PW", 12 * 136, F32).rearrange("p (s n) -> p s n", s=12)
        bb = AR.get("bb", 2 * 128, F32).rearrange("p (r j k) -> p r j k", r=2, j=8)
        tmpa = AR.get("tmpa", 256, F32)
        tmpb = AR.get("tmpb", 256, F32)
        mats2 = [AR.get(f"mats{i}", 8 * 256, BF16).rearrange("p (m k) -> p m k", m=8) for i in range(2)]
        Mg2 = [AR.get(f"Mg{i}", 2 * 512, BF16).rearrange("p (g h c) -> p g h c", g=2, h=2) for i in range(2)]
        Rtr2 = [AR.get(f"Rtr{i}", 4 * 128, BF16).rearrange("p (a c) -> p a c", a=4) for i in range(2)]
        Ut2 = [AR.get(f"Ut{i}", 4 * 128, BF16).rearrange("p (a c) -> p a c", a=4) for i in range(2)]
        Xg = AR.get("Xg", 2 * 256, BF16).rearrange("p (g k) -> p g k", g=2)
        scb = [AR.get("sc0", 11 * 128, F32).rearrange("p (s n) -> p s n", s=11)]
        sci2 = [AR.get("sci0", 128, I32)]
        Xp = AR.get("Xp", 2 * 128, BF16).rearrange("p (r c) -> p r c", r=2)
        Ysb = PW.rearrange("p s n -> p (s n)")[:, 6 * 136:6 * 136 + 256]
        tmpc = PW.rearrange("p s n -> p (s n)")[:, 8 * 136:8 * 136 + 256]
        tmpd = PW.rearrange("p s n -> p (s n)")[:, 8 * 136 + 256:8 * 136 + 512]

        DMA('sp', s5k, s5k_d[:, :], (), ['s5k'], 'c1')
        DMA('sp', maskM, s5mm_d[:, :].rearrange("p (h c) -> p h c", h=2), (), ['maskM'], 'c1')
        DMA('sp', dbc, s5dflat_d[l:l + 1, :].partition_broadcast(128), (), ['dbc'], 'c1')
        DMA('sp', l1, s5l1_d[l, :, :], (), ['l1'], 'c1')
        DMA('sp', bT, s5bT_d[l].rearrange("p (j r k) -> p j r k", j=8, r=2), (), ['bT'], 'c1')
        DMA('sp', cT, s5cT_d[l].rearrange("p (j r k) -> p j r k", j=8, r=2), (), ['cT'], 'c1')
        DMA('sp', bgl, bglu_d[l, :, :], (), ['bgl'], 'c1')
        DMA('pool', wgl, wglu_d[l].rearrange("(c p) n -> p c n", p=128), (), ['wgl'], 'c3')
        s, ws = get_w(('C', l), 512, w_rows(win_d[l], OFF_C, 512))
        MEMSET('pool', Xp[:, :, 0:1], 0.0, ['Xp'])
        prefetch_w(('AB1', l, 0), 1024, w_rows(win_d[l], OFF_A1, 1024))

        hT16 = hT.rearrange("p k (c s) -> p k c s", s=16)
        for sg in range(8):
            b = psA.next()
            for s2_ in range(2):
                st_ = 2 * sg + s2_
                for c in range(8):
                    MM(PS[b][:, s2_ * 256:(s2_ + 1) * 256], hT16[:, c, :, st_], ws[:, c, 0:256], c == 0, c == 7, [('w', s), 'hT'], [P(b)])
            CP('act', Xc[:, 2 * sg:2 * sg + 2, :], PS[b][:, :].rearrange("p (s c) -> p s c", s=2), [P(b)], ['Xc'])

        def frac_sincos(turns, n, sin_o, cos_o, ta, tb, ti, res):
            CP('dve', ti, turns, [res], [res])
            TT('dve', ta, turns, ti, ALU.subtract, [res], [res])
            ACT(sin_o, ta, AF.Sin, [res], [res], scale=TWO_PI)
            TS('dve', tb, turns, 0.25, None, ALU.add, None, [res], [res])
            CP('dve', ti, tb, [res], [res])
            TT('dve', ta, tb, ti, ALU.subtract, [res], [res])
            ACT(cos_o, ta, AF.Sin, [res], [res], scale=TWO_PI)

        R1 = 'p1'
        T8 = lambda i: p1[:, i, :]
        lr1, li1, ldt1 = l1[:, 0:8], l1[:, 8:16], l1[:, 16:24]
        ACT(T8(0), ldt1, AF.Exp, ['l1'], [R1])
        TT('dve', T8(1), lr1, T8(0), ALU.mult, ['l1', R1], [R1])
        TT('dve', T8(2), li1, T8(0), ALU.mult, ['l1', R1], [R1])
        TS('dve', T8(3), T8(2), 1.0 / (2 * math.pi), None, ALU.mult, None, [R1], [R1])
        CP('dve', pti[:, 0:8], T8(3), [R1], [R1])
        TT('dve', T8(4), T8(3), pti[:, 0:8], ALU.subtract, [R1], [R1])
        frac_sincos(T8(4), 8, T8(5), T8(6), T8(7), T8(8), pti[:, 0:8], R1)
        ACT(T8(7), T8(1), AF.Exp, [R1], [R1])
        TT('dve', T8(8), T8(7), T8(6), ALU.mult, [R1], [R1])
        TT('dve', T8(9), T8(7), T8(5), ALU.mult, [R1], [R1])
        TS('dve', T8(8), T8(8), -1.0, None, ALU.add, None, [R1], [R1])
        TT('dve', T8(10), lr1, lr1, ALU.mult, ['l1'], [R1])
        TT('dve', T8(11), li1, li1, ALU.mult, ['l1'], [R1])
        TT('dve', T8(10), T8(10), T8(11), ALU.add, [R1], [R1])
        RECIP(T8(10), T8(10), [R1], [R1])
        TT('dve', T8(11), T8(8), lr1, ALU.mult, ['l1', R1], [R1])
        TT('dve', T8(12), T8(9), li1, ALU.mult, ['l1', R1], [R1])
        TT('dve', T8(11), T8(11), T8(12), ALU.add, [R1], [R1])
        TT('dve', T8(11), T8(11), T8(10), ALU.mult, [R1], [R1])
        TT('dve', T8(12), T8(9), lr1, ALU.mult, ['l1', R1], [R1])
        TT('dve', T8(13), T8(8), li1, ALU.mult, ['l1', R1], [R1])
        TT('dve', T8(12), T8(12), T8(13), ALU.subtract, [R1], [R1])
        TT('dve', T8(12), T8(12), T8(10), ALU.mult, [R1], [R1])
        zre_b = T8(11).unsqueeze(2).to_broadcast([128, 8, 16])
        zim_b = T8(12).unsqueeze(2).to_broadcast([128, 8, 16])
        b3 = lambda ap: ap.rearrange("p (j k) -> p j k", j=8)
        TT('dve', b3(tmpa[:, 0:128]), zre_b, bT[:, :, 0, :], ALU.mult, [R1, 'bT'], ['tmpa'])
        TT('dve', b3(tmpb[:, 0:128]), zim_b, bT[:, :, 1, :], ALU.mult, [R1, 'bT'], ['tmpb'])
        TT('dve', bb[:, 0, :, :], b3(tmpa[:, 0:128]), b3(tmpb[:, 0:128]), ALU.subtract, ['tmpa', 'tmpb'], ['bb'])
        TT('dve', b3(tmpa[:, 0:128]), zre_b, bT[:, :, 1, :], ALU.mult, [R1, 'bT'], ['tmpa'])
        TT('dve', b3(tmpb[:, 0:128]), zim_b, bT[:, :, 0, :], ALU.mult, [R1, 'bT'], ['tmpb'])
        TT('dve', bb[:, 1, :, :], b3(tmpa[:, 0:128]), b3(tmpb[:, 0:128]), ALU.add, ['tmpa', 'tmpb'], ['bb'])
        TS('dve', T8(13), T8(4), 16.0, None, ALU.mult, None, [R1], [R1])
        CP('dve', pti[:, 0:8], T8(13), [R1], [R1])
        TT('dve', T8(14), T8(13), pti[:, 0:8], ALU.subtract, [R1], [R1])
        ACT(T8(15), T8(1), AF.Exp, [R1], [R1], scale=16.0)
        RP = 'PW'
        W17 = lambda i: PW[:, i, :].rearrange("p (j k) -> p j k", j=8)
        W16 = lambda i: PW[:, i, 0:128].rearrange("p (j k) -> p j k", j=8)
        kk17 = kidx.unsqueeze(1).to_broadcast([128, 8, 17])
        kk16r = krev.unsqueeze(1).to_broadcast([128, 8, 16])
        lm17 = T8(1).unsqueeze(2).to_broadcast([128, 8, 17])
        lm16 = T8(1).unsqueeze(2).to_broadcast([128, 8, 16])
        ph17 = T8(4).unsqueeze(2).to_broadcast([128, 8, 17])
        ph16 = T8(4).unsqueeze(2).to_broadcast([128, 8, 16])
        TT('dve', W17(4), kk17, lm17, ALU.mult, ['s5k', R1], [RP])
        ACT(PW[:, 5, :], PW[:, 4, :], AF.Exp, [RP], [RP])
        ACT(PW[:, 6, :], PW[:, 4, :], AF.Exp, [RP], [RP], scale=-1.0)
        TT('dve', W17(7), kk17, ph17, ALU.mult, ['s5k', R1], [RP])
        frac_sincos(PW[:, 7, :], 136, PW[:, 8, :], PW[:, 9, :], PW[:, 10, :], PW[:, 11, :], pti, RP)
        TT('dve', PW[:, 0, :], PW[:, 5, :], PW[:, 9, :], ALU.mult, [RP], [RP])
        TT('dve', PW[:, 1, :], PW[:, 5, :], PW[:, 8, :], ALU.mult, [RP], [RP])
        TT('dve', PW[:, 2, :], PW[:, 6, :], PW[:, 9, :], ALU.mult, [RP], [RP])
        STT(PW[:, 3, :], PW[:, 8, :], -1.0, PW[:, 6, :], ALU.mult, ALU.mult, [RP], [RP])
        TT('dve', W16(10), kk16r, lm16, ALU.mult, ['s5k', R1], [RP])
        ACT(PW[:, 11, 0:128], PW[:, 10, 0:128], AF.Exp, [RP], [RP])
        TT('dve', W16(10), kk16r, ph16, ALU.mult, ['s5k', R1], [RP])
        frac_sincos(PW[:, 10, 0:128], 128, PW[:, 6, 0:128], PW[:, 7, 0:128], PW[:, 8, 0:128], PW[:, 9, 0:128], pti[:, 0:128], RP)
        TT('dve', PW[:, 4, 0:128], PW[:, 11, 0:128], PW[:, 7, 0:128], ALU.mult, [RP], [RP])
        TT('dve', PW[:, 5, 0:128], PW[:, 11, 0:128], PW[:, 6, 0:128], ALU.mult, [RP], [RP])

        def outer(eng, dst, tab, vec, W):
            TT(eng, dst.rearrange("p (a b) -> p a b", a=16), tab.unsqueeze(2).to_broadcast([128, 16, 16]),
               vec.unsqueeze(1).to_broadcast([128, 16, 16]), ALU.mult, [RP, 'bb', 'cT'], W)

        def s5_stage1(j):
            pj = j % 2
            mats, Mg, Rtr, Ut = mats2[pj], Mg2[pj], Rtr2[pj], Ut2[pj]
            Pre_j, Pim_j = W17(0)[:, j, :], W17(1)[:, j, :]
            PIre_j, PIim_j = W17(2)[:, j, 0:16], W17(3)[:, j, 0:16]
            PRre_j, PRim_j = W16(4)[:, j, :], W16(5)[:, j, :]
            bre_j, bim_j = bb[:, 0, j, :], bb[:, 1, j, :]
            cre_j, cim_j = cT[:, j, 0, :], cT[:, j, 1, :]
            specs = [(PIre_j, PIim_j, bre_j, bim_j, 0, 1, False),
                     (Pre_j[:, 0:16], Pim_j[:, 0:16], cre_j, cim_j, 2, 3, True),
                     (PRre_j, PRim_j, bre_j, bim_j, 4, 5, False),
                     (Pre_j[:, 1:17], Pim_j[:, 1:17], cre_j, cim_j, 6, 7, True)]
            for (tr_, ti_, vr_, vi_, o_re, o_im, neg) in specs:
                if not neg:
                    outer('dve', tmpa, tr_, vr_, ['tmpa'])
                    outer('dve', tmpb, ti_, vi_, ['tmpb'])
                    TT('dve', mats[:, o_re, :], tmpa, tmpb, ALU.subtract, ['tmpa', 'tmpb'], [('mats', o_re, pj)])
                    outer('pool', tmpc, tr_, vi_, ['tmpc'])
                    outer('pool', tmpd, ti_, vr_, ['tmpd'])
                    TT('pool', mats[:, o_im, :], tmpc, tmpd, ALU.add, ['tmpc', 'tmpd'], [('mats', o_im, pj)])
                else:
                    outer('pool', tmpc, tr_, vr_, ['tmpc'])
                    outer('pool', tmpd, ti_, vi_, ['tmpd'])
                    TT('pool', mats[:, o_re, :], tmpc, tmpd, ALU.subtract, ['tmpc', 'tmpd'], [('mats', o_re, pj)])
                    outer('dve', tmpa, tr_, vi_, ['tmpa'])
                    outer('dve', tmpb, ti_, vr_, ['tmpb'])
                    STT(mats[:, o_im, :], tmpa, -1.0, tmpb, ALU.mult, ALU.subtract, ['tmpa', 'tmpb'], [('mats', o_im, pj)])
            for g2 in range(2):
                rows = slice(g2 * 64, (g2 + 1) * 64)
                for sh in range(2):
                    b = psA.next()
                    MM(PS[b][:, 0:256], mats[rows, 0, sh * 128:(sh + 1) * 128], mats[rows, 2, :], True, False,
                       [('mats', 0, pj), ('mats', 2, pj)], [P(b)])
                    MM(PS[b][:, 0:256], mats[rows, 1, sh * 128:(sh + 1) * 128], mats[rows, 3, :], False, True,
                       [('mats', 1, pj), ('mats', 3, pj)], [P(b)])
                    TT('dve', Mg[:, g2, sh, :], PS[b][:, 0:256], maskM[:, sh, :], ALU.mult, [P(b), 'maskM'], [('Mg', g2, pj)])
            b = psC.next()
            pst = PS[b][:, :].bitcast(BF16)
            for ri in range(2):
                for sh in range(2):
                    a_ = ri * 2 + sh
                    TR(pst[:, a_ * 128:(a_ + 1) * 128], mats[:, 4 + ri, sh * 128:(sh + 1) * 128], [('mats', 4 + ri, pj), 'cbf'], [P(b)])
            CP('act', Rtr, pst[:, 0:512].rearrange("p (a c) -> p a c", a=4), [P(b)], [('Rtr', pj)])
            b = psC.next()
            pst = PS[b][:, :].bitcast(BF16)
            for g2 in range(2):
                ch0 = (2 * j + g2) * 16
                CP('act', Xg[:, g2, :].rearrange("p (s k) -> p s k", s=16), Xc[:, :, ch0:ch0 + 16], ['Xc'], [('Xg', g2)])
                for sh in range(2):
                    a_ = g2 * 2 + sh
                    TR(pst[:, a_ * 128:(a_ + 1) * 128], Xg[:, g2, sh * 128:(sh + 1) * 128], [('Xg', g2), 'cbf'], [P(b)])
            CP('act', Ut, pst[:, 0:512].rearrange("p (a c) -> p a c", a=4), [P(b)], [('Ut', pj)])

        def s5_stage2(j):
            pj = j % 2
            mats, Mg, Rtr, Ut = mats2[pj], Mg2[pj], Rtr2[pj], Ut2[pj]
            bw = psA.next()
            for ri in range(2):
                for g2 in range(2):
                    for sh in range(2):
                        MM(PS[bw][g2 * 64:(g2 + 1) * 64, ri * 128:(ri + 1) * 128], Rtr[:, ri * 2 + sh, g2 * 64:(g2 + 1) * 64],
                           Ut[:, g2 * 2 + sh, :], sh == 0, sh == 1, [('Rtr', pj), ('Ut', pj)], [P(bw)])
            pj2 = 0
            SCb = scb[pj2]
            SC = lambda i: SCb[:, i, :]
            r = lambda i: ('sc', i, pj2)
            sci_, rsi = sci2[pj2], ('sci', pj2)
            TS('dve', SC(0), iotac, T8(14)[:, j:j + 1], None, ALU.mult, None, ['s5k', R1], [r(0)])
            CP('dve', sci_, SC(0), [r(0)], [rsi])
            TT('dve', SC(3), SC(0), sci_, ALU.subtract, [r(0), rsi], [r(3)])
            ACT(SC(1), SC(3), AF.Sin, [r(3)], [r(1)], scale=TWO_PI)
            TS('dve', SC(4), SC(0), 0.25, None, ALU.add, None, [r(0)], [r(4)])
            CP('dve', sci_, SC(4), [r(4)], [rsi])
            TT('dve', SC(5), SC(4), sci_, ALU.subtract, [r(4), rsi], [r(5)])
            ACT(SC(2), SC(5), AF.Sin, [r(5)], [r(2)], scale=TWO_PI)
            Wre, Wim = PS[bw][:, 0:128], PS[bw][:, 128:256]
            TT('dve', SC(3), Wre, SC(2), ALU.mult, [P(bw), r(2)], [r(3)])
            TT('dve', SC(4), Wim, SC(1), ALU.mult, [P(bw), r(1)], [r(4)])
            TT('dve', SC(5), Wim, SC(2), ALU.mult, [P(bw), r(2)], [r(5)])
            TT('dve', SC(6), Wre, SC(1), ALU.mult, [P(bw), r(1)], [r(6)])
            TT('pool', SC(7), SC(3), SC(4), ALU.add, [r(3), r(4)], [r(7)])
            TT('pool', SC(8), SC(5), SC(6), ALU.subtract, [r(5), r(6)], [r(8)])
            magA = T8(15)[:, j:j + 1].to_broadcast([128, 128])
            S.op('dve', lambda e, magA=magA, o=SC(9), i=SC(7): e.tensor_tensor_scan(o, magA, i, 0.0, ALU.mult, ALU.add), [r(7), R1], [r(9)])
            S.op('dve', lambda e, magA=magA, o=SC(10), i=SC(8): e.tensor_tensor_scan(o, magA, i, 0.0, ALU.mult, ALU.add), [r(8), R1], [r(10)])
            TT('dve', SC(3), SC(9), SC(2), ALU.mult, [r(9), r(2)], [r(3)])
            TT('pool', SC(4), SC(10), SC(1), ALU.mult, [r(10), r(1)], [r(4)])
            TT('dve', Xp[:, 0, 1:128], SC(3)[:, 0:127], SC(4)[:, 0:127], ALU.subtract, [r(3), r(4)], ['Xp'])
            TT('dve', SC(5), SC(9), SC(1), ALU.mult, [r(9), r(1)], [r(5)])
            TT('pool', SC(6), SC(10), SC(2), ALU.mult, [r(10), r(2)], [r(6)])
            TT('dve', Xp[:, 1, 1:128], SC(5)[:, 0:127], SC(6)[:, 0:127], ALU.add, [r(5), r(6)], ['Xp'])
            for g2 in range(2):
                rows = slice(g2 * 64, (g2 + 1) * 64)
                g = 2 * j + g2
                by = psB.next()
                for th in range(2):
                    osl = PS[by][:, th * 128:(th + 1) * 128]
                    csl = slice(th * 128, (th + 1) * 128)
                    MM(osl, Mg[:, g2, 0, csl], Ut[:, g2 * 2 + 0, :], True, False, [('Mg', g2, pj), ('Ut', pj)], [P(by)])
                    if th == 1:
                        MM(osl, Mg[:, g2, 1, csl], Ut[:, g2 * 2 + 1, :], False, False, [('Mg', g2, pj), ('Ut', pj)], [P(by)])
                    MM(osl, mats[rows, 6, csl], Xp[rows, 0, :], False, False, [('mats', 6, pj), 'Xp'], [P(by)])
                    MM(osl, mats[rows, 7, csl], Xp[rows, 1, :], False, True, [('mats', 7, pj), 'Xp'], [P(by)])
                CP('act', Ysb, PS[by][:, 0:256], [P(by)], ['Ysb'])
                bt = psD.next()
                for th in range(2):
                    TRF(PS[bt][:, th * 128:(th + 1) * 128], Ysb[:, th * 128:(th + 1) * 128], ['Ysb', 'identf'], [P(bt)])
                CP('dve', Yc[:, :, g * 16:(g + 1) * 16], PS[bt][:, 0:256].rearrange("p (s k) -> p s k", s=16), [P(bt)], [('Yc', g)])
        s5_stage1(0)
        for j in range(8):
            if j + 1 < 8:
                s5_stage1(j + 1)
            s5_stage2(j)
        AR.release_to(mark5)
        gyT = r3(AR.get("gyT", 2 * S_LEN, BF16), 2)
        gqs = [[AR.get(f"gq{p_}_{i}", 512, F32) for i in range(2)] for p_ in range(2)]
        gq = gqs[0]
        allY = [('Yc', g) for g in range(16)]
        for qd in range(8):
            ssl = slice(2 * qd, 2 * qd + 2)
            g0, g1_ = gqs[qd % 2]
            rg0, rg1 = ('gq', 0, qd % 2), ('gq', 1, qd % 2)
            v3 = lambda ap: ap.rearrange("p (s c) -> p s c", s=2)
            TT('dve', v3(g0), Xc[:, ssl, :], dbc.unsqueeze(1).to_broadcast([128, 2, 256]), ALU.mult, ['Xc', 'dbc'], [rg0])
            TT('pool', Yc[:, ssl, :], Yc[:, ssl, :], v3(g0), ALU.add, [rg0] + allY, [('Yq', qd)])
            yq = Yc[:, ssl, :]
            TT('dve', v3(g0), yq, yq, ALU.mult, [('Yq', qd)], [rg0])
            TS('dve', g0, g0, 0.044715, 1.0, ALU.mult, ALU.add, [rg0], [rg0])
            TT('pool', v3(g1_), v3(g0), yq, ALU.mult, [rg0, ('Yq', qd)], [rg1])
            ACT(g1_, g1_, AF.Sigmoid, [rg1], [rg1], scale=1.5957691216)
            TT('dve', Xc[:, ssl, :], v3(g1_), yq, ALU.mult, [rg1, ('Yq', qd)], ['Xc'])
        gy16 = gyT.rearrange("p h (c s) -> p h s c", s=16)
        for hf in range(2):
            for sg in range(2):
                b = psC.next()
                pst = PS[b][:, :].bitcast(BF16)
                for s8 in range(8):
                    TR(pst[:, s8 * 128:(s8 + 1) * 128], Xc[:, sg * 8 + s8, hf * 128:(hf + 1) * 128], ['Xc', 'cbf'], [P(b)])
                CP('act', gy16[:, hf, sg * 8:(sg + 1) * 8, :], pst.rearrange("p (s c) -> p s c", s=8), [P(b)], ['gyT'])
        for cc in range(4):
            csl = slice(cc * 512, (cc + 1) * 512)
            for ch in range(2):
                b = proj_fm(ws, ('w', s), 256 + ch * 128, 128, cc)
                ACT(zsc[:, ch, :], PS[b][:, :], AF.Silu, [P(b)], ['zsc'])
            for ch in range(2):
                bv = psA.next()
                bg = psB.next()
                ga, gb = gq[0][:, 0:512], gq[1][:, 0:512]
                for c in range(2):
                    MM(PS[bv][:, :], wgl[:, c, ch * 128:(ch + 1) * 128], gyT[:, c, csl], c == 0, c == 1, ['wgl', 'gyT'], [P(bv)])
                for c in range(2):
                    MM(PS[bg][:, :], wgl[:, c, 256 + ch * 128:256 + (ch + 1) * 128], gyT[:, c, csl], c == 0, c == 1, ['wgl', 'gyT'], [P(bg)])
                ACT(ga, PS[bv][:, :], AF.Identity, [P(bv), 'bgl'], [('gq', 0, 0)], bias=bgl[:, ch:ch + 1])
                ACT(gb, PS[bg][:, :], AF.Sigmoid, [P(bg), 'bgl'], [('gq', 1, 0)], bias=bgl[:, 2 + ch:3 + ch])
                TT('pool', ga, ga, gb, ALU.mult, [('gq', 0, 0), ('gq', 1, 0)], [('gq', 0, 0)])
                TT('dve', brb[0][:, ch, :], ga, zsc[:, ch, :], ALU.mult, [('gq', 0, 0), 'zsc'], [('brb', 0)])
            store_chunk(2, cc, brb[0], 0)
        if debug == 'C':
            break

        for br in range(2):
            AR.reset()
            nq = 3 if br == 0 else 2
            qT = r3(AR.get("qT", nq * S_LEN, BF16), nq)
            kT = r3(AR.get("kT", (3 if br == 0 else 4) * S_LEN, BF16), 3 if br == 0 else 4)
            VW = 128
            vaug = AR.get("vaug", NT * 4 * VW, BF16).rearrange("p (t h w) -> p t h w", t=NT, h=4)
            MEMSET('pool', kT[:, :, :], 0.0, ['kT'])
            MEMSET('pool', vaug[:, :, :, 65:128], 0.0, ['vaug'])
            zsT = r3(AR.get("zsT", 2 * S_LEN, BF16), 2)
            sgb = AR.get("sgb", 64, F32)
            dl = AR.get("dl", 128, F32)
            ssq = AR.get("ssq", 16, F32)
            mark = AR.mark()
            ropeC = AR.get("ropeC", S_LEN, F32)
            ropeS = AR.get("ropeS", S_LEN, F32)
            t1 = AR.get("t1", 512, F32)
            t2 = AR.get("t2", 512, F32)
            qa = [AR.get("qa0", 512, BF16)]
            permAB = perm32 if br == 0 else perm64
            DMA('sp', ropeC, rope_d[2 * br, :, :], (), ['ropeC'], 'c2')
            DMA('sp', ropeS, rope_d[2 * br + 1, :, :], (), ['ropeS'], 'c2')
            MEMSET('pool', vaug[:, :, :, 64:65], 1.0, ['vaug'])
            off1, off2 = (OFF_A1, OFF_A2) if br == 0 else (OFF_B1, OFF_B2)
            s, ws = get_w(('AB1', l, br), 1024, w_rows(win_d[l], off1, 1024))
            s2, ws2 = get_w(('AB2', l, br), 512, w_rows(win_d[l], off2, 512))
            if br == 0:
                for ti in range(3):
                    m = 96 if ti < 2 else 64
                    rope_proj(ws, ('w', s), ti * 96, 256 + ti * 96, qT[:, ti, :], 'qT', ropeC, ropeS, t1, t2, m=m, perm=permAB, qa=qa)
                    rope_proj(ws, ('w', s), 512 + ti * 96, 768 + ti * 96, kT[:, ti, :], 'kT', ropeC, ropeS, t1, t2, m=m, perm=permAB, qa=qa)
            else:
                for ch in range(2):
                    rope_proj(ws, ('w', s), ch * 128, 256 + ch * 128, qT[:, ch, :], 'qT', ropeC, ropeS, t1, t2, perm=permAB, qa=qa)
                    rope_proj(ws, ('w', s), 512 + ch * 128, 768 + ch * 128, (kT[:, 2 * ch, :], kT[:, 2 * ch + 1, :]), 'kT',
                              ropeC, ropeS, t1, t2, perm=permAB, qa=qa)
            v_proj(ws2, ('w', s2), 0, 4, vaug, 'vaug')
            zs_proj(ws2, ('w', s2), 256, zsT)
            AR.release_to(mark)
            if br == 0:
                prefetch_w(('AB1', l, 1), 1024, w_rows(win_d[l], OFF_B1, 1024))
                prefetch_w(('AB2', l, 1), 512, w_rows(win_d[l], OFF_B2, 512))
            else:
                prefetch_w(('D1', l), 512, w_rows(win_d[l], OFF_D1, 512))
                prefetch_w(('D2', l), 768, w_rows(win_d[l], OFF_D2, 768))
            brb = [r3(AR.get(f"brb{i}", 2 * 512, BF16), 2) for i in range(2 if br == 1 else 1)]
            Ebufs = [AR.get(f"E{i}", 512, BF16) for i in range(3)]
            OTs = [AR.get(f"ots{i}", 512, F32) for i in range(2 if br == 1 else 1)]
            obf = AR.get("obf", 4 * 256, BF16).rearrange("p (j c) -> p j c", j=4)
            if br == 0:
                tmpA = AR.get("tmpA", 4 * 8 * 64, F32).rearrange("p (j v d) -> p j v d", j=4, v=8)
                ocomb = AR.get("ocomb", 4 * 256, F32)
                osq = tmpA.rearrange("p j v d -> p (j v d)")[:, 0:1024]
                qpad = [AR.get(f"qpad{i}", 512, BF16) for i in range(3)]
                for i_ in range(3):
                    MEMSET('pool', qpad[i_], 0.0, [('qpad', i_)])
                DMA('sp', sgb, sublng_d[l:l + 1, :].partition_broadcast(128), (), ['sgb'], 'c1')
                DMA('sp', dl, dlam_d[l:l + 1, :].partition_broadcast(128), (), ['dl'], 'c1')
                TT('dve', dl[:, 0:32], dl[:, 0:32], dl[:, 32:64], ALU.mult, ['dl'], ['dl'])
                TT('dve', dl[:, 64:96], dl[:, 64:96], dl[:, 96:128], ALU.mult, ['dl'], ['dl'])
                S.op('dve', lambda e, dl=dl: e.tensor_reduce(lamt[:, 0:1], dl[:, 0:32], AX.X, ALU.add), ['dl'], ['lamt'])
                S.op('dve', lambda e, dl=dl: e.tensor_reduce(lamt[:, 1:2], dl[:, 64:96], AX.X, ALU.add), ['dl'], ['lamt'])
                ACT(lamt[:, 0:2], lamt[:, 0:2], AF.Exp, ['lamt'], ['lamt'])
                TT('dve', lamt[:, 2:3], lamt[:, 0:1], lamt[:, 1:2], ALU.subtract, ['lamt'], ['lamt'])
                TS('dve', lamt[:, 3:4], lamt[:, 2:3], lam_init, -1.0, ALU.add, ALU.mult, ['lamt'], ['lamt'])

                def prepA(vh, qc, qT=qT, qpad=qpad):
                    pos = vh % 3
                    CP('pool', qpad[pos][pos * 32:pos * 32 + 32, :], qT[pos * 32:pos * 32 + 32, vh // 3, qc * 512:(qc + 1) * 512],
                       ['qT'], [('qpad', pos)])

                def qfA(vh, qc, qpad=qpad):
                    return qpad[vh % 3], ('qpad', vh % 3), qc * 512

                def kfA(vh, kT=kT):
                    return kT[:, vh // 3, :], 'kT'

                def vfA(vh, kt, vaug=vaug):
                    return vaug[:, kt, vh // 2, :], 'vaug'

                def outA(vh, qc, bo, tmpA=tmpA):
                    norm_out(bo, 65, tmpA[:, :, vh, :], ('tmpA', vh))

                def postA(qc, tmpA=tmpA, ocomb=ocomb, osq=osq, obf=obf, zsT=zsT, brb=brb, ssq=ssq, sgb=sgb, lam_init=lam_init):
                    tv = tmpA.rearrange("p j (h c) d -> p j h c d", c=2)
                    oc = ocomb.rearrange("p (j h d) -> p j h d", j=4, h=4)
                    allt = [('tmpA', v) for v in range(8)]
                    for j in range(4):
                        STT(oc[:, j], tv[:, j, :, 1, :], lamt[:, 3:4], tv[:, j, :, 0, :], ALU.mult, ALU.add, allt + ['lamt'], ['ocomb'])
                    TT('pool', osq, ocomb, ocomb, ALU.mult, ['ocomb'], allt)
                    S.op('dve', lambda e: e.tensor_reduce(ssq, osq.rearrange("p (g d) -> p g d", d=64), AX.X, ALU.add), allt, ['ssq'])
                    TS('dve', ssq, ssq, 1.0 / 64, EPS, ALU.mult, ALU.add, ['ssq'], ['ssq'])
                    ACT(ssq, ssq, AF.Sqrt, ['ssq'], ['ssq'])
                    RECIP(ssq, ssq, ['ssq'], ['ssq'])
                    o3 = ocomb.rearrange("p (g d) -> p g d", d=64)
                    TT('dve', o3, o3, ssq.unsqueeze(2).to_broadcast([128, 16, 64]), ALU.mult, ['ocomb', 'ssq'], ['ocomb'])
                    STT(obf.rearrange("p j (h d) -> p (j h) d", d=64), o3, 1.0 - lam_init,
                        sgb.unsqueeze(1).to_broadcast([128, 16, 64]), ALU.mult, ALU.mult, ['ocomb', 'sgb'], ['obf'])
                    finish_branch(0, obf, qc, zsT, brb)

                attention(8, qfA, kfA, vfA, 32 ** -0.5, causal_pairs, outA, Ebufs, OTs, post=postA, vwm=128,
                          otrot=Rot([0]), prep=prepA)
            else:
                wstrip = AR.get("wstrip", 2048, BF16)
                tmpB = AR.get("tmpB", 4 * 256, F32).rearrange("p (j h d) -> p j h d", j=4, h=4)
                DMA('sp', wstrip, wstrip_d[:, :], (), ['wstrip'], 'c1')

                def qfB(h, qT=qT):
                    return qT[:, h // 2, :], 'qT'

                def kfB(h, kT=kT):
                    return kT[:, h, :], 'kT'

                def vfB(h, kt, vaug=vaug):
                    return vaug[:, kt, h, :], 'vaug'

                def pairsB(qc, wstrip=wstrip):
                    pl = []
                    for kt in range(4 * qc + 4):
                        c0 = max(0, kt - 4 * qc) * 128
                        x0 = qc * 512 + c0 - kt * 128
                        pl.append((kt, c0, 512, [], (wstrip[:, x0:x0 + 512 - c0], 'wstrip')))
                    return pl

                def outB(h, qc, bo, tmpB=tmpB):
                    norm_out(bo, 65, tmpB[:, :, h, :], 'tmpB')

                def postB(qc, tmpB=tmpB, obf=obf, zsT=zsT, brb=brb):
                    CP('act', obf, tmpB.rearrange("p j h d -> p j (h d)"), ['tmpB'], ['obf'])
                    finish_branch(1, obf, qc, zsT, brb)

                attention(4, qfB, kfB, vfB, 64 ** -0.5, pairsB, outB, Ebufs, OTs, post=postB, vwm=128)

        AR.reset()
        zsT = r3(AR.get("zsT", 2 * S_LEN, BF16), 2)
        brb = [r3(AR.get("brb0", 2 * 512, BF16), 2)]
        qT = r3(AR.get("qT", 2 * S_LEN, BF16), 2)
        qrT = r3(AR.get("qrT", 4 * S_LEN, BF16), 4)
        ksT = AR.get("ksT", S_LEN, BF16)
        kwT = AR.get("kwT", S_LEN, BF16)
        vsw = AR.get("vsw", NT * 2 * 128, BF16).rearrange("p (t h w) -> p t h w", t=NT, h=2)
        gts = AR.get("gts", NT * 12, F32).rearrange("p (t g) -> p t g", t=NT)
        kcmpT = AR.get("kcmpT", 128, BF16)
        vca = AR.get("vca", 97, BF16)
        mark = AR.mark()
        ropeC = AR.get("ropeC", S_LEN, F32)
        ropeS = AR.get("ropeS", S_LEN, F32)
        t1 = AR.get("t1", 512, F32)
        t2 = AR.get("t2", 512, F32)
        qa = [AR.get("qa0", 512, BF16)]
        DMA('sp', ropeC, rope_d[2, :, :], (), ['ropeC'], 'c2')
        DMA('sp', ropeS, rope_d[3, :, :], (), ['ropeS'], 'c2')
        MEMSET('pool', vsw[:, :, :, 65:128], 0.0, ['vsw'])
        MEMSET('pool', vsw[:, :, :, 64:65], 1.0, ['vsw'])
        MEMSET('pool', qrT[:, :, :], 0.0, ['qrT'])
        CP('dve', vca[:, 64:97], ovl, ['cbf'], ['vca'])
        s, ws = get_w(('D1', l), 512, w_rows(win_d[l], OFF_D1, 512))
        s2, ws2 = get_w(('D2', l), 768, w_rows(win_d[l], OFF_D2, 768))
        for ch in range(2):
            for tc in range(4):
                b = proj_fm(ws, ('w', s), ch * 128, 128, tc)
                CP('act', qT[:, ch, tc * 512:(tc + 1) * 512], PS[b][:, :], [P(b)], ['qT'])
            rope_proj(ws, ('w', s), ch * 128, 256 + ch * 128, (qrT[:, 2 * ch, :], qrT[:, 2 * ch + 1, :]), 'qrT', ropeC, ropeS, t1, t2, perm=perm64, qa=qa)
        rope_proj(ws2, ('w', s2), 128, 256, ksT, 'ksT', ropeC, ropeS, t1, t2, perm=perm64, qa=qa)
        rope_proj(ws2, ('w', s2), 384, 512, kwT, 'kwT', ropeC, ropeS, t1, t2, perm=perm64, qa=qa)
        for t in range(NT):
            b = proj_tm(ws2, ('w', s2), 640, 128, t)
            CP('act', vsw[:, t, :, 0:64], PS[b][:, 0:128].rearrange("p (h d) -> p h d", h=2), [P(b)], ['vsw'])
        AR.release_to(mark)
        kvA = AR.get("kvA", S_LEN + 32, BF16)
        kvB = AR.get("kvB", S_LEN + 32, BF16)
        peT = AR.get("peT", 32, F32)
        hidT = AR.get("hidT", 4 * 128, BF16).rearrange("p (a n) -> p a n", a=4)
        gh1 = AR.get("gh1", 128, F32)
        gh2 = AR.get("gh2", 128, F32)
        gh3 = AR.get("gh3", 128, F32)
        w2all = AR.get("w2", 384, BF16)
        w2k = w2all[:, 0:256].rearrange("p (a d) -> p a d", a=2)
        w2v = w2all[:, 256:384].rearrange("p (a d) -> p a d", a=2)
        DMA('sp', peT, nsape_d[l, :, :], (), ['peT'], 'c1')
        DMA('pool', w2all, nsaw2_d[l], (), ['w2'], 'c3')
        for tc in range(4):
            b = proj_fm(ws2, ('w', s2), 0, 128, tc)
            sl = slice(tc * 512, (tc + 1) * 512)
            TT('dve', kvA[:, sl].rearrange("p (g r) -> p g r", r=16), PS[b][:, :].rearrange("p (g r) -> p g r", r=16),
               peT[:, 0:16].unsqueeze(1).to_broadcast([128, 32, 16]), ALU.add, [P(b), 'peT'], ['kvA'])
            TT('dve', kvB[:, sl].rearrange("p (g r) -> p g r", r=16), PS[b][:, :].rearrange("p (g r) -> p g r", r=16),
               peT[:, 16:32].unsqueeze(1).to_broadcast([128, 32, 16]), ALU.add, [P(b), 'peT'], ['kvB'])
        s3, ws3 = load_w(268, w_rows(win_d[l], OFF_D3, 268))
        zs_proj(ws3, ('w', s3), 0, zsT)
        for t in range(NT):
            b = proj_tm(ws3, ('w', s3), 256, 12, t)
            ACT(gts[:, t, :], PS[b][:, 0:12], AF.Sigmoid, [P(b)], ['gts'])
        sw = []
        for kv in range(2):
            sl_ = wrot.next()
            rows = slice(kv * 64, kv * 64 + 64)
            DMA('pool', wbuf[sl_][rows, 0:8192], nsaw1_d[l, kv], (), [('w', sl_)], f'w{sl_}')
            sw.append(sl_)
        for kv in range(2):
            rows = slice(kv * 64, kv * 64 + 64)
            w1v = wbuf[sw[kv]][:, 0:8192].rearrange("p (j h) -> p j h", j=32)
            for hc in range(2):
                b = psA.next()
                for j in range(32):
                    srcT = kvA if j < 16 else kvB
                    srcv = srcT[rows, j:j + 2032].rearrange("p (n r) -> p n r", r=16)[:, :, 0]
                    MM(PS[b][:, 0:127], w1v[rows, j, hc * 128:(hc + 1) * 128], srcv, j == 0, j == 31,
                       [('w', sw[kv]), 'kvA', 'kvB'], [P(b)])
                CP('act', gh3[:, 0:127], PS[b][:, 0:127], [P(b)], ['gh3'])
                TT('dve', gh1[:, 0:127], gh3[:, 0:127], gh3[:, 0:127], ALU.mult, ['gh3'], ['gh1'])
                TS('dve', gh1[:, 0:127], gh1[:, 0:127], 0.044715, 1.0, ALU.mult, ALU.add, ['gh1'], ['gh1'])
                TT('dve', gh2[:, 0:127], gh1[:, 0:127], gh3[:, 0:127], ALU.mult, ['gh1', 'gh3'], ['gh2'])
                ACT(gh2[:, 0:127], gh2[:, 0:127], AF.Sigmoid, ['gh2'], ['gh2'], scale=1.5957691216)
                TT('dve', hidT[:, kv * 2 + hc, 0:127], gh2[:, 0:127], gh3[:, 0:127], ALU.mult, ['gh2', 'gh3'], ['hidT'])
        b = psA.next()
        for c in range(2):
            MM(PS[b][:, 0:127], w2k[:, c, :], hidT[:, c, 0:127], c == 0, c == 1, ['w2', 'hidT'], [P(b)])
        CP('act', kcmpT[:, 0:127], PS[b][:, 0:127], [P(b)], ['kcmpT'])
        b = psA.next()
        for c in range(2):
            MM(PS[b][0:127, 0:64], hidT[:, 2 + c, 0:127], w2v[:, c, :], c == 0, c == 1, ['w2', 'hidT'], [P(b)])
        CP('act', vca[0:127, 0:64], PS[b][0:127, 0:64], [P(b)], ['vca'])

        AR.release_to(mark)
        Ebufs = [AR.get(f"E{i}", 512, BF16) for i in range(2)]
        OTs = [AR.get("ots0", 512, F32)]
        erotD, otrotD = Rot([0, 1]), Rot([0])
        oacc = AR.get("oacc", 4 * 256, F32).rearrange("p (j h d) -> p j h d", j=4, h=4)
        ntmp = AR.get("ntmp", 4 * 64, F32).rearrange("p (j d) -> p j d", j=4)
        ntmp2 = AR.get("ntmp2", 4 * 32, F32).rearrange("p (j k) -> p j k", j=4)
        obf = AR.get("obf", 4 * 256, BF16).rearrange("p (j c) -> p j c", j=4)
        MnegT = AR.get("MnegT", 1024, BF16)
        negcmp = AR.get("negcmp", 2048, BF16)
        selc = AR.get("selc", 768, F32).rearrange("p (k t j) -> p k t j", k=3, t=8)
        imp = AR.get("imp", 4 * 32, F32).rearrange("p (j k) -> p j k", j=4)
        impm = AR.get("impm", 32, F32)
        impm2 = AR.get("impm2", 32, F32)
        mx8 = AR.get("mx8", 16, F32)
        selm = AR.get("selm", 32, F32)
        selb = AR.get("selb", 32, BF16)
        prefetch_w(('E1', l), 512, w_rows(win_d[l], OFF_E, 512))
        prefetch_w(('E2', l), 512, w_rows(wmemkv_d[l], 0, 512))
        MEMSET('pool', MnegT, 0.0, ['MnegT'])
        DMA('sp', negcmp, negcmp_d[:, :], (), ['negcmp'], 'c1')
        DMA('sp', selc, selc_d[:, :].rearrange("p (k t j) -> p k t j", k=3, t=8), (), ['selc'], 'c1')

        def qfD(h):
            pb = (h % 2) * 64
            return qT[pb:pb + 64, h // 2, :], 'qT'

        def qfDr(h):
            return qrT[:, h, :], 'qrT'

        def hrows(h):
            return slice((h % 2) * 64, (h % 2) * 64 + 64)

        for qc in range(4):
            def pairs_cmp(qc_):
                return [(0, 0, 512, [(ident[0:127, 0:127], negcmp[0:127, qc_ * 512:(qc_ + 1) * 512], 0, 512, ['cbf', 'negcmp'])], None)]

            def out_cmp(h, qc_, bo):
                rd, rdres = norm_out(bo, 97, oacc[:, :, h, :], 'oacc')
                O = PS[bo][:, 0:4 * 97].rearrange("p (j w) -> p j w", j=4)
                if qc_ >= 2:
                    rb = rd.unsqueeze(2).to_broadcast([128, 4, 32])
                    if h == 0:
                        TT('dve', imp, O[:, :, 65:97], rb, ALU.mult, [P(bo), rdres], ['imp'])
                    else:
                        TT('dve', ntmp2, O[:, :, 65:97], rb, ALU.mult, [P(bo), rdres], ['ntmp2'])
                        TT('dve', imp, imp, ntmp2, ALU.add, ['ntmp2', 'imp'], ['imp'])
                coef = gts[:, 4 * qc_:4 * qc_ + 4, 3 * h]
                TT('dve', oacc[:, :, h, :], oacc[:, :, h, :], coef.unsqueeze(2).to_broadcast([128, 4, 64]), ALU.mult,
                   ['oacc', 'gts'], ['oacc'])

            attention(4, qfD, lambda h: (kcmpT[hrows(h), :], 'kcmpT'), lambda h, kt: (vca[0:127, :], 'vca'), 64 ** -0.5,
                      pairs_cmp, out_cmp, Ebufs, OTs, vw=97, kparts=127, qcs=[qc], erot=erotD, otrot=otrotD)
            if qc >= 2:
                for jl in range(4):
                    qt8 = 4 * qc + jl - 8
                    cand, candm1, forced = selc[:, 0, qt8, :], selc[:, 1, qt8, :], selc[:, 2, qt8, :]
                    TT('dve', impm, imp[:, jl, :], cand, ALU.mult, ['imp', 'selc'], ['impm'])
                    TT('dve', impm, impm, candm1, ALU.add, ['impm', 'selc'], ['impm'])
                    S.op('dve', lambda e: e.max(mx8[:, 0:8], impm), ['impm'], ['mx8'])
                    S.op('dve', lambda e: e.match_replace(impm2, mx8[:, 0:8], impm, -1e9), ['mx8', 'impm'], ['impm2'])
                    S.op('dve', lambda e: e.max(mx8[:, 8:16], impm2), ['impm2'], ['mx8'])
                    TS('dve', selm, impm, mx8[:, 12:13], None, ALU.is_ge, None, ['impm', 'mx8'], ['selm'])
                    TT('dve', selm, selm, cand, ALU.mult, ['selm', 'selc'], ['selm'])
                    TT('dve', selm, selm, forced, ALU.max, ['selm', 'selc'], ['selm'])
                    TS('dve', selb, selm, -1.0, BIG, ALU.add, ALU.mult, ['selm'], ['selb'])
                    b = psT.next()
                    pst = PS[b][:, :].bitcast(BF16)
                    TR(pst[0:32, 0:128], selb, ['selb', 'cbf'], [P(b)])
                    CP('act', MnegT[0:32, (4 * qc + jl - 8) * 128:(4 * qc + jl - 7) * 128], pst[0:32, 0:128], [P(b)], ['MnegT'])

            def extra_slc(qc_, kt, c0):
                if qc_ < 2:
                    return []
                q0 = qc_ * 512 + c0 - 1024
                return [(expand[:, kt * 128:(kt + 1) * 128], MnegT[:, q0:q0 + 512 - c0], 0, 512 - c0, ['cbf', 'MnegT'])]

            def out_slc(h, qc_, bo):
                norm_out(bo, 65, oacc[:, :, h, :], 'oacc', coef=gts[:, 4 * qc_:4 * qc_ + 4, 3 * h + 1], coefres='gts',
                         accumulate=True, tmp=ntmp)

            attention(4, qfDr, lambda h: (ksT[:, :], 'ksT'), lambda h, kt: (vsw[:, kt, 0, :], 'vsw'), 64 ** -0.5,
                      lambda q_: causal_pairs(q_, extra_slc), out_slc, Ebufs, OTs, qcs=[qc], vwm=128, erot=erotD, otrot=otrotD)

            def pairs_win(qc_):
                pl = []
                for kt in range(max(0, 4 * qc_ - 4), 4 * qc_ + 4):
                    c0 = max(0, kt - 4 * qc_) * 128
                    c1 = min(4, kt + 5 - 4 * qc_) * 128
                    adds = []
                    if kt >= 4 * qc_:
                        adds.append((ident, negtri, 0, 128, ['cbf']))
                    if kt + 4 <= 4 * qc_ + 3:
                        jl = kt + 4 - 4 * qc_
                        adds.append((ident, negtri2, jl * 128 - c0, 128, ['cbf']))
                    pl.append((kt, c0, c1, adds, None))
                return pl

            def out_win(h, qc_, bo):
                norm_out(bo, 65, oacc[:, :, h, :], 'oacc', coef=gts[:, 4 * qc_:4 * qc_ + 4, 3 * h + 2], coefres='gts',
                         accumulate=True, tmp=ntmp)

            attention(4, qfDr, lambda h: (kwT[:, :], 'kwT'), lambda h, kt: (vsw[:, kt, 1, :], 'vsw'), 64 ** -0.5,
                      pairs_win, out_win, Ebufs, OTs, qcs=[qc], vwm=128, erot=erotD, otrot=otrotD)
            CP('act', obf, oacc.rearrange("p j h d -> p j (h d)"), ['oacc'], ['obf'])
            finish_branch(3, obf, qc, zsT, brb)

        AR.reset()
        zsT = r3(AR.get("zsT", 2 * S_LEN, BF16), 2)
        brb = [r3(AR.get(f"brb{i}", 2 * 512, BF16), 2) for i in range(2)]
        qT = r3(AR.get("qT", 2 * S_LEN, BF16), 2)
        mems = AR.get("mems", 2 * D, F32).rearrange("p (t d) -> p t d", t=2)
        mgb = AR.get("mgb", D, F32)
        hb0 = AR.get("hb0", D, BF16)
        hb1 = AR.get("hb1", D, BF16)
        junk = AR.get("junk", D, BF16)
        memhT = r3(AR.get("memhT", 8 * 256, BF16), 8)
        kmT = r3(AR.get("kmT", 4 * 256, BF16), 4)
        vm = AR.get("vm", 2 * 4 * 128, BF16).rearrange("p (t h w) -> p t h w", t=2, h=4)
        Ebufs = [AR.get(f"E{i}", 512, BF16) for i in range(3)]
        OTs = [AR.get(f"ots{i}", 512, F32) for i in range(2)]
        tmpE = AR.get("tmpE", 4 * 256, F32).rearrange("p (j h d) -> p j h d", j=4, h=4)
        obf = AR.get("obf", 4 * 256, BF16).rearrange("p (j c) -> p j c", j=4)
        DMA('sp', mems, mem_d.rearrange("(t p) d -> p t d", p=128), (), ['mems'], 'c1')
        DMA('sp', mgb, memg_d[l:l + 1, :].partition_broadcast(128), (), ['mgb'], 'c1')
        MEMSET('pool', vm[:, :, :, 65:128], 0.0, ['vm'])
        MEMSET('pool', vm[:, :, :, 64:65], 1.0, ['vm'])
        MEMSET('pool', kmT[:, :, :], 0.0, ['kmT'])
        rmsnorm_rows(mems, 2, lambda t: 'mems', mgb, 'mgb', memhT, 'memhT', [hb0, hb1], junk)
        s, ws = get_w(('E1', l), 512, w_rows(win_d[l], OFF_E, 512))
        s2, ws2 = get_w(('E2', l), 512, w_rows(wmemkv_d[l], 0, 512))
        for ch in range(2):
            for tc in range(4):
                b = proj_fm(ws, ('w', s), ch * 128, 128, tc)
                CP('act', qT[:, ch, tc * 512:(tc + 1) * 512], PS[b][:, :], [P(b)], ['qT'])
        zs_proj(ws, ('w', s), 256, zsT)
        for ch in range(2):
            b = proj_fm(ws2, ('w', s2), ch * 128, 128, 0, src=memhT, srcres='memhT', ntok=256)
            CP('act', kmT[0:64, 2 * ch, :], PS[b][0:64, 0:256], [P(b)], ['kmT'])
            CP('act', kmT[64:128, 2 * ch + 1, :], PS[b][64:128, 0:256], [P(b)], ['kmT'])
        for t in range(2):
            b = proj_tm(ws2, ('w', s2), 256, 256, t, src=memhT, srcres='memhT')
            CP('act', vm[:, t, :, 0:64], PS[b][:, 0:256].rearrange("p (h d) -> p h d", h=4), [P(b)], ['vm'])

        def qfE(h):
            return qT[:, h // 2, :], 'qT'

        def kfE(h):
            return kmT[:, h, :], 'kmT'

        def outE(h, qc, bo):
            norm_out(bo, 65, tmpE[:, :, h, :], 'tmpE')

        def postE(qc):
            CP('act', obf, tmpE.rearrange("p j h d -> p j (h d)"), ['tmpE'], ['obf'])
            finish_branch(4, obf, qc, zsT, brb)

        if not debug:
            prefetch_w(('M', l, 0, 0), 1024, w_rows(wmerge_d[l], 0, 1024))
        attention(4, qfE, kfE, lambda h, kt: (vm[:, kt, h, :], 'vm'), 64 ** -0.5,
                  lambda qc: [(0, 0, 512, [], None), (1, 0, 512, [], None)], outE, Ebufs, OTs, post=postE, vwm=128)

        if debug:
            break

        AR.reset()
        bm = AR.get("bm", 40, F32)
        DMA('sp', bm, bmerge_d[l, :, :], (), ['bm'], 'c1')
        wbr = [r3(AR.get(f"wbr{i}", 2 * D, BF16), 2) for i in range(2)]
        mixed = r3(AR.get("mixed", 8 * 1024, F32), 8)
        brc = [r3(AR.get(f"brc{i}", 2 * 1024, BF16), 2) for i in range(2)]
        gate = [AR.get(f"gate{i}", 512, BF16) for i in range(2)]
        prod = [AR.get(f"prod{i}", 512, F32) for i in range(2)]
        mbf = [r3(AR.get(f"mbf{i}", 8 * 128, BF16), 8) for i in range(2)]
        psMg = Rot([0, 1, 4, 6])
        psMy = Rot([2, 3, 5, 7])
        for tp in range(2):
            tsl = slice(tp * 1024, (tp + 1) * 1024)
            for n in range(5):
                s, ws = get_w(('M', l, tp, n), 1024, w_rows(wmerge_d[l], n * 1024, 1024))
                if n < 4:
                    prefetch_w(('M', l, tp, n + 1), 1024, w_rows(wmerge_d[l], (n + 1) * 1024, 1024))
                else:
                    prefetch_w(('O', l, tp), 1024, w_rows(wout_d[l], 0, 1024))
                wb = n % 2
                if n == 0 and tp == 0:
                    DMA('pool', wbr[0], wbranch_d[l, 0].rearrange("(c p) n -> p c n", p=128), (), [('wbr', 0)], 'wbr0')
                if n < 4:
                    nb = (n + 1) % 2
                    DMA('pool', wbr[nb], wbranch_d[l, n + 1].rearrange("(c p) n -> p c n", p=128), (), [('wbr', nb)], f'wbr{nb}')
                def load_brc(n_, tp_):
                    wb_ = n_ % 2
                    tsl_ = slice(tp_ * 1024, (tp_ + 1) * 1024)
                    DMA('sp', brc[wb_], brt_d[2 * n_:2 * n_ + 2, :, tsl_].rearrange("c p t -> p c t"),
                        [('brt', 2 * n_ + hf_, 2 * tp_ + q_) for hf_ in range(2) for q_ in range(2)], [('brc', wb_)], f'brc{wb_}')
                if n == 0 and tp == 0:
                    load_brc(0, 0)
                if n < 4:
                    load_brc(n + 1, tp)
                for dc in range(8):
                    for t2_ in range(2):
                        tsub = slice(tp * 1024 + t2_ * 512, tp * 1024 + (t2_ + 1) * 512)
                        bg = psMg.next()
                        for c in range(8):
                            MM(PS[bg][:, :], ws[:, c, dc * 128:(dc + 1) * 128], hT[:, c, tsub], c == 0, c == 7, [('w', s), 'hT'], [P(bg)])
                        by = psMy.next()
                        for c in range(2):
                            MM(PS[by][:, :], wbr[wb][:, c, dc * 128:(dc + 1) * 128], brc[wb][:, c, t2_ * 512:(t2_ + 1) * 512],
                               c == 0, c == 1, [('wbr', wb), ('brc', wb)], [P(by)])
                        gi = (dc * 2 + t2_) % 2
                        ACT(gate[gi], PS[bg][:, :], AF.Sigmoid, [P(bg), 'bm'], [('gate', gi)], bias=bm[:, n * 8 + dc:n * 8 + dc + 1])
                        msl = mixed[:, dc, t2_ * 512:(t2_ + 1) * 512]
                        if n == 0:
                            TT('dve', msl, PS[by][:, :], gate[gi], ALU.mult, [P(by), ('gate', gi)], [('mixed', dc, t2_)])
                        else:
                            TT('dve', prod[gi], PS[by][:, :], gate[gi], ALU.mult, [P(by), ('gate', gi)], [('prod', gi)])
                            TT('pool', msl, msl, prod[gi], ALU.add, [('prod', gi), ('mixed', dc, t2_)], [('mixed', dc, t2_)])
            s, ws = get_w(('O', l, tp), 1024, w_rows(wout_d[l], 0, 1024))
            if tp == 0:
                prefetch_w(('M', l, 1, 0), 1024, w_rows(wmerge_d[l], 0, 1024))
                DMA('pool', wbr[0], wbranch_d[l, 0].rearrange("(c p) n -> p c n", p=128), (), [('wbr', 0)], 'wbr0')
                load_brc(0, 1)
            elif l + 1 < n_layers:
                prefetch_w(('C', l + 1), 512, w_rows(win_d[l + 1], OFF_C, 512))
            for tt in range(8):
                t = tp * 8 + tt
                mi = tt % 2
                CP('act', mbf[mi], mixed[:, :, tt * 128:(tt + 1) * 128],
                   [('mixed', dc, tt // 4) for dc in range(8)], [('mbf', mi)])
                for half in range(2):
                    b = psC.next()
                    for c in range(8):
                        MM(PS[b][:, :], mbf[mi][:, c, :], ws[:, c, half * 512:(half + 1) * 512], c == 0, c == 7, [('mbf', mi), ('w', s)], [P(b)])
                    TT('dve', xs[:, t, half * 512:(half + 1) * 512], xs[:, t, half * 512:(half + 1) * 512], PS[b][:, :], ALU.add,
                       [P(b), ('x', t)], [('x', t)])

    if not debug:
        AR.reset()
        gbc = AR.get("gbc", D, F32)
        DMA('sp', gbc, finalg_d[0:1, :].partition_broadcast(128), (), ['gbc'], 'c1')
        junk = AR.get("junk", D, BF16)
        ob = [AR.get(f"ob{i}", D, F32) for i in range(2)]
        for t in range(NT):
            ACT(junk, xs[:, t, :], AF.Square, [('x', t)], ['junk', ('ss', t)], accum_out=ss[:, t:t + 1])
        allss = [('ss', t) for t in range(NT)]
        TS('dve', rs, ss, 1.0 / D, EPS, ALU.mult, ALU.add, allss, ['rs'])
        ACT(rs, rs, AF.Sqrt, ['rs'], ['rs'])
        RECIP(rs, rs, ['rs'], ['rs'])
        for t in range(NT):
            STT(ob[t % 2], xs[:, t, :], rs[:, t:t + 1], gbc, ALU.mult, ALU.mult, [('x', t), 'rs', 'gbc'], [('ob', t % 2)])
            DMA('sp', out_d[t * 128:(t + 1) * 128, :], ob[t % 2], [('ob', t % 2)], [('out', t)], f'o{t % 2}')
    S.final_wait('sp')

    sem_names = list(ENGS[:4]) + sorted(S.dmacnt.keys())
    sems = {k: es.enter_context(nc.semaphore(f"s_{k}")) for k in sem_names}
    block = es.enter_context(nc.Block())

    def mk(eng):
        def body(engine):
            for waits, fn, tok in S.q[eng]:
                for k, v in waits.items():
                    engine.wait_ge(sems[k], v)
                if fn is None:
                    continue
                ins = fn(engine)
                ins.then_inc(sems[tok[0]], 1 if tok[0] in ENGS else 16)
        return body

    block.tensor(mk('pe'))
    block.scalar(mk('act'))
    block.vector(mk('dve'))
    block.gpsimd(mk('pool'))
    block.sync(mk('sp'))
    es.close()
    return nc


def _host_consts():
    bf = ml_dtypes.bfloat16
    k = np.arange(128)[:, None]
    q = np.arange(128)[None, :]
    ident = (k == q).astype(np.float32)
    negtri = np.where(k > q, -BIG, 0.0).astype(np.float32)
    negtri2 = np.where(q >= k, -BIG, 0.0).astype(np.float32)
    expand = np.zeros((128, 16, 128), np.float32)
    for kt in range(16):
        for kk in range(128):
            expand[2 * kt + kk // 64, kt, kk] = 1.0
    n_cmp = 127
    c0 = np.arange(n_cmp)[:, None] * 16
    s0 = np.arange(32)[None, :] * 64
    overlap = np.clip(np.minimum(c0 + 32, s0 + 64) - np.maximum(c0, s0), 0, None) / 16
    ovl = np.zeros((128, 33), np.float32)
    ovl[:127, 0] = 1.0
    ovl[:127, 1:] = overlap
    perms = []
    for dh in (32, 64):
        pm = np.zeros((128, 128), np.float32)
        for m_ in range(128):
            blk, j = m_ // dh, m_ % dh
            pm[blk * dh + (j + dh // 2) % dh, m_] = 1.0
        perms.append(pm)
    cbf = np.concatenate([ident, negtri, negtri2, expand.reshape(128, 2048), ovl] + perms, axis=1).astype(bf)
    x = np.arange(2048)[None, :]
    dlt = x - k
    wst = ((dlt >= 0) & (dlt <= 128)).astype(np.float32) + ((dlt >= 0) & (dlt % 4 == 0) & (dlt <= 512)).astype(np.float32) \
        + ((dlt >= 0) & (dlt % 16 == 0) & (dlt <= 2048)).astype(np.float32)
    wstrip = wst.astype(bf)
    negcmp = np.where(16 * k + 31 <= x, 0.0, -BIG).astype(np.float32)
    negcmp[127, :] = -BIG
    negcmp = negcmp.astype(bf)
    t = np.arange(S_LEN, dtype=np.float32)
    rope = np.zeros((4, 128, S_LEN), np.float32)
    for ti, dh in enumerate((32, 64)):
        half = dh // 2
        inv = (np.float32(10000.0) ** (-np.arange(half, dtype=np.float32) / np.float32(half))).astype(np.float32)
        ang = (t[:, None] * inv[None, :]).astype(np.float32)
        cs, sn = np.cos(ang).astype(np.float32), np.sin(ang).astype(np.float32)
        for p in range(128):
            j = p % dh
            rope[2 * ti, p] = cs[:, j % half]
            rope[2 * ti + 1, p] = sn[:, j % half] * (-1.0 if j < half else 1.0)
    s5k = np.concatenate([np.arange(17), 15 - np.arange(16), np.arange(128)]).astype(np.float32)
    s5k = np.ascontiguousarray(np.broadcast_to(s5k[None, :], (128, 161)))
    s5mm = np.zeros((128, 2, 16, 16), np.float32)
    for s8 in range(8):
        for sh in range(2):
            s5mm[s8 * 16:(s8 + 1) * 16, sh, sh * 8 + s8:, :] = 1.0
    s5mm = np.ascontiguousarray(s5mm.reshape(128, 512).astype(bf))
    sel = np.zeros((128, 3, 8, 32), np.float32)
    for qt in range(8, 16):
        for qq in range(128):
            qblk = (qt * 128 + qq) // 64
            j = np.arange(32)
            cand = ((j >= 1) & (j <= qblk - 2)).astype(np.float32)
            forced = ((j == 0) | (j == qblk) | (j == qblk - 1)).astype(np.float32)
            sel[qq, 0, qt - 8] = cand
            sel[qq, 1, qt - 8] = cand - 1.0
            sel[qq, 2, qt - 8] = forced
    return dict(cbf=np.ascontiguousarray(cbf), wstrip=np.ascontiguousarray(wstrip), negcmp=np.ascontiguousarray(negcmp),
                rope=rope, identf=np.ascontiguousarray(ident, dtype=np.float32), s5k=s5k, s5mm=s5mm, selc=np.ascontiguousarray(sel.reshape(128, 768)))


def _host_layout(inp):
    offs = np.cumsum([0] + list(IN_SIZES))
    col = {n: np.arange(offs[i], offs[i + 1]) for i, n in enumerate(IN_NAMES)}

    def swap(cols, dh):
        c = cols.reshape(-1, dh)
        h = dh // 2
        return np.concatenate([c[:, h:], c[:, :h]], axis=1).reshape(-1)

    order = np.concatenate([
        col['a_q'], swap(col['a_q'], 32), col['a_k'], swap(col['a_k'], 32), col['a_v'], col['a_z'],
        col['b_q'], swap(col['b_q'], 64), col['b_k'], swap(col['b_k'], 64), col['b_v'], col['b_z'],
        col['c_u'], col['c_z'],
        col['d_q'], swap(col['d_q'], 64),
        col['d_kc'], col['d_vc'], col['d_ks'], col['d_ks'], swap(col['d_ks'], 64), swap(col['d_ks'], 64),
        col['d_kw'], col['d_kw'], swap(col['d_kw'], 64), swap(col['d_kw'], 64), col['d_vs'], col['d_vw'],
        col['d_z'], col['d_g'],
        col['e_q'], col['e_z']])
    assert order.shape[0] == NWIN
    f = lambda a: np.ascontiguousarray(np.asarray(a, dtype=np.float32))
    d = {}
    d['win'] = f(inp['w_in'][:, :, order])
    d['wmerge'] = f(inp['w_merge'])
    d['bmerge'] = f(inp['b_merge'].reshape(DEPTH, 40, 128).transpose(0, 2, 1))
    d['wbranch'] = f(inp['w_branch'])
    d['wout'] = f(inp['w_out'])
    d['normg'] = f(inp['norm_g'])
    d['finalg'] = f(inp['final_g'].reshape(1, D))
    d['memg'] = f(inp['mem_norm_g'])
    d['wmemkv'] = f(inp['w_mem_kv'])
    d['dlam'] = f(inp['diff_lambda'].reshape(DEPTH, 128))
    d['sublng'] = f(inp['diff_subln_g'])
    lr, li, ldt = inp['s5_lambda_re'], inp['s5_lambda_im'], inp['s5_log_dt']
    l1 = np.zeros((DEPTH, 128, 24), np.float32)
    for j in range(8):
        for g2 in range(2):
            g = 2 * j + g2
            l1[:, g2 * 64:(g2 + 1) * 64, j] = lr[:, g, :]
            l1[:, g2 * 64:(g2 + 1) * 64, 8 + j] = li[:, g, :]
            l1[:, g2 * 64:(g2 + 1) * 64, 16 + j] = ldt[:, g][:, None]
    d['s5l1'] = l1
    bre, bim = inp['s5_b_re'], inp['s5_b_im']
    cre, cim = inp['s5_c_re'], inp['s5_c_im']
    bT = np.zeros((DEPTH, 128, 8, 2, 16), np.float32)
    cTt = np.zeros((DEPTH, 128, 8, 2, 16), np.float32)
    for j in range(8):
        for g2 in range(2):
            g = 2 * j + g2
            rows = slice(g2 * 64, (g2 + 1) * 64)
            bT[:, rows, j, 0, :] = bre[:, g]
            bT[:, rows, j, 1, :] = bim[:, g]
            cTt[:, rows, j, 0, :] = cre[:, g].transpose(0, 2, 1)
            cTt[:, rows, j, 1, :] = cim[:, g].transpose(0, 2, 1)
    d['s5bT'] = bT.reshape(DEPTH, 128, 256)
    d['s5cT'] = cTt.reshape(DEPTH, 128, 256)
    d['s5dflat'] = f(inp['s5_d'].reshape(DEPTH, 256))
    d['wglu'] = f(inp['w_glu'])
    d['bglu'] = f(inp['b_glu'].reshape(DEPTH, 4, 128).transpose(0, 2, 1))
    pe = inp['nsa_pe']
    d['nsape'] = f(pe.transpose(0, 1, 3, 2).reshape(DEPTH, 128, 32))
    w1 = inp['nsa_w1'].reshape(DEPTH, 2, 32, 64, 256)
    d['nsaw1'] = f(w1.transpose(0, 1, 3, 2, 4).reshape(DEPTH, 2, 64, 32 * 256))
    w2 = inp['nsa_w2'].reshape(DEPTH, 2, 2, 128, 64)
    w2k = w2[:, 0].transpose(0, 2, 1, 3)
    w2k = np.concatenate([w2k, w2k], axis=3).reshape(DEPTH, 128, 256)
    w2v = w2[:, 1].transpose(0, 2, 1, 3).reshape(DEPTH, 128, 128)
    d['nsaw2'] = f(np.concatenate([w2k, w2v], axis=2))
    return d


_CACHE = {}


def kernel(**inputs):
    inp = {k: np.asarray(v) for k, v in inputs.items()}
    if 'nc' not in _CACHE:
        _CACHE['nc'] = build_program()
        _CACHE['consts'] = _host_consts()
    nc = _CACHE['nc']
    shared = dict(_CACHE['consts'])
    shared.update(_host_layout(inp))
    in_maps = []
    for b in range(8):
        m = dict(shared)
        m['x'] = np.ascontiguousarray(inp['x'][b], dtype=np.float32)
        m['mem'] = np.ascontiguousarray(inp['mem'][b], dtype=np.float32)
        in_maps.append(m)
    res = run_bass_kernel_spmd(nc, in_maps, core_ids=list(range(8)))
    out = np.stack([np.asarray(r['out'], dtype=np.float32) for r in res.results], axis=0)
    return out
```
